# Optimizing a Trainium2 kernel written in Bass

```python
import math
import jax
import jax.numpy as jnp
from jax import lax
import numpy as np

D_MODEL = 2048
BATCH = 4
SEQ = 4096
DEPTH = 2
DEC_BATCH = 16
DEC_SEQ = 16
PAST_LEN = 4096

CHUNK = 64
QUERY_BLOCK = 128
N_MIXERS = 2
N_ATTN_LAYERS = (DEPTH + 1) // 2
N_REC_LAYERS = DEPTH // 2

ATTN_HEADS = 8
ATTN_HEAD_DIM = D_MODEL // ATTN_HEADS // 2
ATTN_V_DIM = 2 * ATTN_HEAD_DIM
ATTN_SCALE = ATTN_HEAD_DIM ** -0.5

REC_EXPAND = 128
REC_HEADS = D_MODEL // REC_EXPAND
REC_DK = REC_EXPAND
REC_DV = D_MODEL // REC_HEADS
REC_WIDTH = REC_HEADS * REC_DK

D_FF = 5632
CONV_W = 3
EPS = 1e-6

kernel_name = "diffattn_hgrn2_convffn_streaming_step"


def rms_norm(x, g):
    x32 = x.astype(jnp.float32)
    y = x32 * lax.rsqrt(jnp.mean(x32 * x32, axis=-1, keepdims=True) + EPS)
    return (y * g.astype(jnp.float32)).astype(x.dtype)


def alibi_slopes():
    h = jnp.arange(1, ATTN_HEADS + 1, dtype=jnp.float32)
    return jnp.exp2(-8.0 * h / ATTN_HEADS)


def diff_attention_core(q, k, v, q_pos, k_pos, lam, slopes):
    s = jnp.einsum('bqhcd,bkhcd->bhcqk', q, k) * ATTN_SCALE
    dist = jnp.abs(q_pos[:, None] - k_pos[None, :]).astype(jnp.float32)
    allowed = (k_pos[None, :] // CHUNK) <= (q_pos[:, None] // CHUNK)
    s = jnp.where(allowed, s - slopes[:, None, None, None] * dist, -jnp.inf)
    p = jax.nn.softmax(s, axis=-1)
    a = p[:, :, 0] - lam * p[:, :, 1]
    return jnp.einsum('bhqk,bkhe->bqhe', a, v)


def diff_attention(h, k_past, v_past, w_qkv, lq1, lk1, lq2, lk2, subln, w_o, layer_idx):
    B, T, _ = h.shape
    q, k, v = jnp.split(h @ w_qkv, 3, axis=-1)
    q = q.reshape(B, T, ATTN_HEADS, 2, ATTN_HEAD_DIM)
    k = k.reshape(B, T, ATTN_HEADS, 2, ATTN_HEAD_DIM)
    v = v.reshape(B, T, ATTN_HEADS, ATTN_V_DIM)
    lam_init = 0.8 - 0.6 * math.exp(-0.3 * layer_idx)
    lam = (jnp.exp(jnp.sum(lq1.astype(jnp.float32) * lk1.astype(jnp.float32)))
           - jnp.exp(jnp.sum(lq2.astype(jnp.float32) * lk2.astype(jnp.float32))) + lam_init)
    if k_past is None:
        past = 0
        k_all, v_all = k, v
    else:
        past = k_past.shape[1]
        k_all = jnp.concatenate([k_past.astype(k.dtype), k], axis=1)
        v_all = jnp.concatenate([v_past.astype(v.dtype), v], axis=1)
    k_all = k_all.astype(jnp.float32)
    v_all = v_all.astype(jnp.float32)
    q32 = q.astype(jnp.float32)
    q_pos = past + jnp.arange(T, dtype=jnp.int32)
    k_pos = jnp.arange(past + T, dtype=jnp.int32)
    slopes = alibi_slopes()
    if T <= QUERY_BLOCK:
        o = diff_attention_core(q32, k_all, v_all, q_pos, k_pos, lam, slopes)
    else:
        nb = T // QUERY_BLOCK
        q_blocks = jnp.moveaxis(q32.reshape(B, nb, QUERY_BLOCK, ATTN_HEADS, 2, ATTN_HEAD_DIM), 1, 0)
        pos_blocks = q_pos.reshape(nb, QUERY_BLOCK)
        o = lax.map(lambda a: diff_attention_core(a[0], k_all, v_all, a[1], k_pos, lam, slopes),
                    (q_blocks, pos_blocks))
        o = jnp.moveaxis(o, 0, 1).reshape(B, T, ATTN_HEADS, ATTN_V_DIM)
    o = rms_norm(o, subln) * (1.0 - lam_init)
    out = o.reshape(B, T, ATTN_HEADS * ATTN_V_DIM).astype(h.dtype) @ w_o
    return out, k, v


def gated_linear_recurrence(q, k, v, log_f, s0, chunk):
    B, T, H, DK = q.shape
    DV = v.shape[-1]
    n = T // chunk

    def blocks(a):
        return jnp.moveaxis(a.reshape(B, n, chunk, *a.shape[2:]), 1, 0)

    causal = jnp.tril(jnp.ones((chunk, chunk), dtype=bool))[None, :, :, None, None]

    def step(state, inp):
        qc, kc, vc, gc = inp
        b = jnp.cumsum(gc, axis=1)
        diff = b[:, :, None] - b[:, None, :]
        decay = jnp.exp(jnp.where(causal, diff, -jnp.inf))
        scores = jnp.einsum('bthd,bshd,btshd->bhts', qc, kc, decay)
        o = (jnp.einsum('bhts,bshv->bthv', scores, vc)
             + jnp.einsum('bthd,bhdv->bthv', qc * jnp.exp(b), state))
        b_last = b[:, -1]
        new_state = (jnp.exp(b_last)[..., None] * state
                     + jnp.einsum('bshd,bshv->bhdv', kc * jnp.exp(b_last[:, None] - b), vc))
        return new_state, o

    s_fin, o = lax.scan(step, s0, (blocks(q), blocks(k), blocks(v), blocks(log_f)))
    o = jnp.moveaxis(o, 0, 1).reshape(B, T, H, DV)
    return o, s_fin


def hgrn2(h, s0, w_qfig, lb, out_norm, w_o):
    B, T, _ = h.shape
    q, fz, i, g = jnp.split(h @ w_qfig, 4, axis=-1)
    log_f = jnp.logaddexp(jnp.log(lb), jnp.log1p(-lb) + jax.nn.log_sigmoid(fz.astype(jnp.float32)))
    k = -jnp.expm1(log_f)
    q = jax.nn.silu(q.astype(jnp.float32))
    shp_k = (B, T, REC_HEADS, REC_DK)
    shp_v = (B, T, REC_HEADS, REC_DV)
    chunk = CHUNK if T % CHUNK == 0 else T
    o, s_fin = gated_linear_recurrence(q.reshape(shp_k), k.reshape(shp_k),
                                       i.astype(jnp.float32).reshape(shp_v),
                                       log_f.reshape(shp_k), s0.astype(jnp.float32), chunk)
    o = rms_norm(o, out_norm) * jax.nn.silu(g.astype(jnp.float32)).reshape(shp_v)
    out = o.reshape(B, T, REC_HEADS * REC_DV).astype(h.dtype) @ w_o
    return out, s_fin


def conv_ffn(h, prev, w_up, w_conv, b_conv, w_down):
    T = h.shape[1]
    gate, val = jnp.split(h @ w_up, 2, axis=-1)
    hp = jnp.concatenate([prev.astype(gate.dtype), gate], axis=1)
    c = b_conv
    for j in range(CONV_W):
        c = c + hp[:, j:j + T] * w_conv[j]
    out = (jax.nn.silu(c) * val) @ w_down
    return out, hp[:, -(CONV_W - 1):]


def setup_inputs(seed: int = 0) -> dict:
    key = jax.random.key(seed)
    ks = jax.random.split(key, 24)

    def nrm(k, shape, scale):
        return jax.random.normal(k, shape, jnp.float32) * scale

    qkv_out = 3 * ATTN_HEADS * 2 * ATTN_HEAD_DIM
    attn_in = ATTN_HEADS * ATTN_V_DIM
    return {
        "x_prompt": nrm(ks[0], (BATCH, SEQ, D_MODEL), 1.0),
        "x_sample": nrm(ks[1], (DEC_BATCH, DEC_SEQ, D_MODEL), 1.0),
        "cache_k": nrm(ks[2], (N_ATTN_LAYERS, DEC_BATCH, PAST_LEN, ATTN_HEADS, 2, ATTN_HEAD_DIM), 1.0),
        "cache_v": nrm(ks[3], (N_ATTN_LAYERS, DEC_BATCH, PAST_LEN, ATTN_HEADS, ATTN_V_DIM), 1.0),
        "state_hgrn": nrm(ks[4], (N_REC_LAYERS, DEC_BATCH, REC_HEADS, REC_DK, REC_DV), 0.3),
        "state_conv": nrm(ks[5], (DEPTH, DEC_BATCH, CONV_W - 1, D_FF), 1.0),
        "mixer_norm": 1.0 + nrm(ks[6], (DEPTH, D_MODEL), 0.02),
        "ffn_norm": 1.0 + nrm(ks[7], (DEPTH, D_MODEL), 0.02),
        "attn_w_qkv": nrm(ks[8], (N_ATTN_LAYERS, D_MODEL, qkv_out), D_MODEL ** -0.5),
        "attn_lambda_q1": nrm(ks[9], (N_ATTN_LAYERS, ATTN_HEAD_DIM), 0.1),
        "attn_lambda_k1": nrm(ks[10], (N_ATTN_LAYERS, ATTN_HEAD_DIM), 0.1),
        "attn_lambda_q2": nrm(ks[11], (N_ATTN_LAYERS, ATTN_HEAD_DIM), 0.1),
        "attn_lambda_k2": nrm(ks[12], (N_ATTN_LAYERS, ATTN_HEAD_DIM), 0.1),
        "attn_subln": 1.0 + nrm(ks[13], (N_ATTN_LAYERS, ATTN_V_DIM), 0.02),
        "attn_w_o": nrm(ks[14], (N_ATTN_LAYERS, attn_in, D_MODEL), attn_in ** -0.5),
        "rec_w_qfig": nrm(ks[15], (N_REC_LAYERS, D_MODEL, 4 * REC_WIDTH), D_MODEL ** -0.5),
        "rec_lower_bounds": nrm(ks[16], (DEPTH, REC_WIDTH), 0.5),
        "rec_out_norm": 1.0 + nrm(ks[17], (N_REC_LAYERS, REC_DV), 0.02),
        "rec_w_o": nrm(ks[18], (N_REC_LAYERS, REC_HEADS * REC_DV, D_MODEL), (REC_HEADS * REC_DV) ** -0.5),
        "ffn_w_up": nrm(ks[19], (DEPTH, D_MODEL, 2 * D_FF), D_MODEL ** -0.5),
        "ffn_conv_w": nrm(ks[20], (DEPTH, CONV_W, D_FF), CONV_W ** -0.5),
        "ffn_conv_b": nrm(ks[21], (DEPTH, D_FF), 0.02),
        "ffn_w_down": nrm(ks[22], (DEPTH, D_FF, D_MODEL), D_FF ** -0.5),
        "final_norm": 1.0 + nrm(ks[23], (D_MODEL,), 0.02),
    }


def reference(x_prompt, x_sample, cache_k, cache_v, state_hgrn, state_conv,
              mixer_norm, ffn_norm, attn_w_qkv, attn_lambda_q1, attn_lambda_k1,
              attn_lambda_q2, attn_lambda_k2, attn_subln, attn_w_o,
              rec_w_qfig, rec_lower_bounds, rec_out_norm, rec_w_o,
              ffn_w_up, ffn_conv_w, ffn_conv_b, ffn_w_down, final_norm):
    xp, xs = x_prompt, x_sample
    lbs = jax.nn.softmax(rec_lower_bounds.astype(jnp.float32), axis=0)
    lb_all = jnp.cumsum(lbs, axis=0) - lbs[0]

    kp_l, vp_l, ks_l, vs_l = [], [], [], []
    sp_l, ss_l, cp_l, cs_l = [], [], [], []
    for i in range(DEPTH):
        j = i // N_MIXERS
        hp = rms_norm(xp, mixer_norm[i])
        hs = rms_norm(xs, mixer_norm[i])
        if i % N_MIXERS == 0:
            w = (attn_w_qkv[j], attn_lambda_q1[j], attn_lambda_k1[j], attn_lambda_q2[j],
                 attn_lambda_k2[j], attn_subln[j], attn_w_o[j], i)
            out_p, k_p, v_p = diff_attention(hp, None, None, *w)
            out_s, k_s, v_s = diff_attention(hs, cache_k[j], cache_v[j], *w)
            kp_l.append(k_p); vp_l.append(v_p); ks_l.append(k_s); vs_l.append(v_s)
        else:
            w = (rec_w_qfig[j], lb_all[i], rec_out_norm[j], rec_w_o[j])
            s0_p = jnp.zeros((xp.shape[0], REC_HEADS, REC_DK, REC_DV), jnp.float32)
            out_p, st_p = hgrn2(hp, s0_p, *w)
            out_s, st_s = hgrn2(hs, state_hgrn[j], *w)
            sp_l.append(st_p); ss_l.append(st_s)
        xp = xp + out_p
        xs = xs + out_s
        wf = (ffn_w_up[i], ffn_conv_w[i], ffn_conv_b[i], ffn_w_down[i])
        prev_p = jnp.zeros((xp.shape[0], CONV_W - 1, D_FF), xp.dtype)
        f_p, c_p = conv_ffn(rms_norm(xp, ffn_norm[i]), prev_p, *wf)
        f_s, c_s = conv_ffn(rms_norm(xs, ffn_norm[i]), state_conv[i], *wf)
        cp_l.append(c_p); cs_l.append(c_s)
        xp = xp + f_p
        xs = xs + f_s

    y_prompt = rms_norm(xp, final_norm)
    y_sample = rms_norm(xs, final_norm)
    return (y_prompt, y_sample,
            jnp.stack(kp_l), jnp.stack(vp_l), jnp.stack(ks_l), jnp.stack(vs_l),
            jnp.stack(sp_l), jnp.stack(ss_l), jnp.stack(cp_l), jnp.stack(cs_l))
```

```python
import numpy as np
from contextlib import ExitStack
import concourse.bass as bass
import concourse.mybir as mybir
from concourse.bass_utils import run_bass_kernel_spmd

F32 = mybir.dt.float32
BF16 = mybir.dt.bfloat16
AF = mybir.ActivationFunctionType
ALU = mybir.AluOpType
AX = mybir.AxisListType

D = 2048
NH = 8
T = 4096
TT = 512
NBLK = 4
KC = 16
DFF = 5632
FC = 44
RH = 16
EPS = 1e-6
SCALE = 128 ** -0.5
LAM_INIT0 = 0.8 - 0.6 * float(np.exp(-0.3 * 0))
NCORES = 8
SB = 2
ST = 16
PAST = 4096
DO_SAMPLE = True
VW = 264


class Buf:
    __slots__ = ("name", "w", "r", "excl")

    def __init__(self, name, excl=False):
        self.name = name
        self.w = {}
        self.r = {}
        self.excl = excl


class DSem:
    def __init__(self, h):
        self.h = h
        self.total = 0


class Eng:
    def __init__(self, e, sem, name):
        self.e = e
        self.sem = sem
        self.cnt = 0
        self.waited = {}
        self.name = name


class Prog:
    def __init__(self, nc, es):
        self.nc = nc
        self.es = es
        self.nsem = 0

    def sem(self, name):
        self.nsem += 1
        return self.es.enter_context(self.nc.semaphore(name))

    def dsem(self, name):
        return DSem(self.sem(name))

    def wait(self, eng, evs, skip=None):
        for ev in evs:
            if ev[0] == "e":
                src, c = ev[1], ev[2]
                if src is eng and eng.name == "pe":
                    continue
                key = id(src)
                if eng.waited.get(key, 0) >= c:
                    continue
                eng.e.wait_ge(src.sem, c)
                eng.waited[key] = c
            else:
                ds = ev[1]
                if ds is skip:
                    continue
                c = ds.total
                key = id(ds)
                if eng.waited.get(key, 0) >= c:
                    continue
                eng.e.wait_ge(ds.h, c)
                eng.waited[key] = c

    def _deps(self, reads, writes):
        evs = []
        for b in reads:
            evs.extend(b.w.values())
            if b.excl:
                evs.extend(b.r.values())
        for b in writes:
            evs.extend(b.w.values())
            evs.extend(b.r.values())
        return evs

    def op(self, eng, fn, reads=(), writes=(), sig=True):
        self.wait(eng, self._deps(reads, writes))
        inst = fn()
        if sig:
            eng.cnt += 1
            inst.then_inc(eng.sem, 1)
            ev = ("e", eng, eng.cnt)
        else:
            ev = ("e", eng, eng.cnt + 1)
        k = id(eng)
        for b in reads:
            b.r[k] = ev
        for b in writes:
            b.w = {k: ev}
            b.r = {}
        return inst

    def dma(self, q, out, in_, ds, reads=(), writes=(), **kw):
        self.wait(q, self._deps(reads, writes), skip=ds)
        inst = q.e.dma_start(out=out, in_=in_, **kw)
        ds.total += 16
        inst.then_inc(ds.h, 16)
        ev = ("d", ds)
        k = id(ds)
        for b in reads:
            b.r[k] = ev
        for b in writes:
            b.w = {k: ev}
            b.r = {}
        return inst


def host_consts():
    c = {}
    c["ident"] = np.eye(128, dtype=np.float32)
    p = np.arange(128)[:, None]
    xx = np.arange(1024)[None, :] - 384
    allowed = (xx // 64) >= (p // 64)
    tabs = np.where(allowed, np.abs(xx - p), 1.0e6).astype(np.float32)
    c["tabs"] = tabs
    c["cpos"] = np.tile((np.arange(32, dtype=np.float32) * 128.0)[None, :], (128, 1)).astype(np.float32)
    s = np.arange(64)
    tri = (s[:, None] <= s[None, :]).astype(np.float32)
    c["tri"] = np.concatenate([tri, tri], axis=0)
    s2 = np.arange(128)
    c["triu"] = (s2[:, None] <= s2[None, :]).astype(np.float32)
    i = np.arange(ST)[None, :]
    c["stab_c"] = (i - p + 128).astype(np.float32)
    c["stab_n"] = np.where(p < ST, np.abs(i - p), 0).astype(np.float32)
    return c


def build_program(NT=8, do_l1=True, do_sample=True, dbg=False, stage=9):
    nc = bass.Bass("TRN2", target_bir_lowering=False)
    es = ExitStack()
    P = Prog(nc, es)
    E = es.enter_context

    def din(name, shape, dt=F32):
        return nc.dram_tensor(name, list(shape), dt, kind="ExternalInput").ap()

    def dout(name, shape, dt=F32):
        return nc.dram_tensor(name, list(shape), dt, kind="ExternalOutput").ap()

    def dscr(name, shape, dt):
        return nc.dram_tensor(name, list(shape), dt).ap()

    xp = din("xp", [T, D])
    w_qkv = din("w_qkv", [D, 3 * D])
    w_o = din("w_o", [D, D])
    w_qfig = din("w_qfig", [D, 4 * D])
    w_o2 = din("w_o2", [D, D])
    w_up = din("w_up", [2, D, 2 * DFF])
    w_dn = din("w_dn", [2, DFF, D])
    gcols_d = din("gcols", [128, 4 * KC])
    fnorm_d = din("fnorm", [D])
    subln_d = din("subln", [256])
    lamv_d = din("lamv", [4 * 128])
    rlb_d = din("rlb", [128, 2 * RH])
    onorm_d = din("onorm", [128])
    convp_d = din("convp", [128, 2 * FC * 4])
    ident_d = din("ident", [128, 128])
    tabs_d = din("tabs", [128, 1024])
    cpos_d = din("cpos", [128, 32])
    tri_d = din("tri", [128, 64])
    triu_d = din("triu", [128, 128])
    if do_sample:
        xs_d = din("xs", [SB * ST, D])
        ck_d = din("ck", [SB, PAST, D])
        cv_d = din("cv", [SB, PAST, D])
        shg_in = din("shg_in", [SB, RH, 128, 128])
        scv_in = din("scv_in", [2, SB, 128, FC * 2])
        stabc_d = din("stab_c", [128, ST])
        stabn_d = din("stab_n", [128, ST])

    y_o = dout("y", [T, D])
    k_o = dout("kout", [T, D])
    v_o = dout("vout", [T, D])
    shg_o = dout("shg", [RH, 128, 128])
    scv_o = dout("scv", [2, 128, FC * 2])
    if do_sample:
        ys_o = dout("ys", [SB * ST, D])
        ks_o = dout("ksout", [SB * ST, D])
        vs_o = dout("vsout", [SB * ST, D])
        shgs_o = dout("shgs", [SB, RH, 128, 128])
        scvs_o = dout("scvs", [2, SB, 128, FC * 2])

    wqk_b = dscr("wqk_b", [NH, 128, KC, 512], BF16)
    wv_b = dscr("wv_b", [NH, 128, KC, 256], BF16)
    wo_b = dscr("wo_b", [4, 128, KC, 512], BF16)
    wqf_b = dscr("wqf_b", [RH, 128, KC, 512], BF16)
    wo2_b = dscr("wo2_b", [4, 128, KC, 512], BF16)
    wup_b = dscr("wup_b", [2, 22, 128, KC, 512], BF16)
    wdn_b = dscr("wdn_b", [2, 4, 4, 128, 11, 512], BF16)
    ktS = dscr("ktS", [2 * NH, 128, T], BF16)
    ktSs = dscr("ktSs", [SB, 2 * NH, 128, PAST], BF16)
    vS = dscr("vS", [NH, T, VW], BF16)

    PE = Eng(nc.tensor, P.sem("pe"), "pe")
    ACT = Eng(nc.scalar, P.sem("act"), "act")
    DVE = Eng(nc.vector, P.sem("dve"), "dve")
    POOL = Eng(nc.gpsimd, P.sem("pool"), "pool")
    SPQ = Eng(nc.sync, None, "sp")
    V = nc.vector
    A = nc.scalar
    TE = nc.tensor

    def sb(name, shape, dt):
        return E(nc.sbuf_tensor("s_" + name, list(shape), dt))

    xres = sb("xres", [128, NBLK, D], F32)
    hT = sb("hT", [128, KC, TT], BF16)
    oT = sb("oT", [128, KC, TT], BF16)
    hb = sb("hb", [128, D], BF16)
    NWS = 2
    wsl = sb("wsl", [128, NWS, KC * 512], BF16)
    KTW = 4224
    ARENA = 4 * KTW + 33 * VW
    arena = sb("arena", [128, ARENA], BF16)
    NSL = 10
    SLW = 528
    slots = sb("slots", [128, NSL, SLW], F32)
    Sst = sb("Sst", [128, RH, 128], F32)
    tabs = sb("tabs", [128, 1024], F32)
    biasH = sb("biasH", [128, 1024], F32)
    cpos = sb("cpos", [128, 32], F32)
    cH = sb("cH", [128, 32], F32)
    identf = sb("identf", [128, 128], F32)
    ident = sb("ident", [128, 128], BF16)
    gcols = sb("gcols", [128, 4 * KC], F32)
    sublnb = sb("sublnb", [128, 256], F32)
    lamv = sb("lamv", [128, 4 * 128], F32)
    lamt = sb("lamt", [128, 8], F32)
    rlb = sb("rlb", [128, 2 * RH], F32)
    lbt = sb("lbt", [128, 4 * RH], F32)
    onormb = sb("onormb", [128, 128], F32)
    convp = sb("convp", [128, 2 * FC * 4], F32)
    convst = sb("convst", [128, 2, FC, 2], F32)
    tri = sb("tri", [128, 64], F32)
    triu = sb("triu", [128, 128], F32)
    hsm = sb("hsm", [128, 2, 32], F32)
    hs2 = sb("hs2", [128, 8], F32)
    sbf = sb("sbf", [128, 128], BF16)
    ATt = sb("ATt", [128, 128], BF16)
    on32 = sb("on32", [128, 128], F32)
    onb = sb("onb", [128, 128], BF16)
    ktok = sb("ktok", [128, 128], BF16)
    hjunk = sb("hjunk", [128, 128], BF16)
    fence_s = sb("fence_s", [128, 2], F32)
    stabs = sb("stabs", [128, 2 * ST], F32)
    sbias = sb("sbias", [128, 2 * ST], F32)
    small = sb("small", [128, 64], F32)
    o0buf = sb("o0buf", [128, NBLK, 256], F32)
    qtbuf = sb("qtbuf", [128, 2, 2, TT], BF16)
    qtB = [Buf("qt0"), Buf("qt1")]
    ps = E(nc.psum_tensor("ps", [128, 4096], F32))

    def bank(i):
        return ps[:, i * 512:(i + 1) * 512]

    pb = [Buf("pb%d" % i, excl=True) for i in range(8)]

    xB = [Buf("x%d" % i) for i in range(NBLK)]
    xD = [P.dsem("xd%d" % i) for i in range(NBLK)]
    hTB = Buf("hT")
    oTB = Buf("oT")
    hbB = Buf("hb")
    smallB = Buf("small")
    constB = Buf("const")
    constD = P.dsem("constd")
    wslB = [Buf("ws%d" % i) for i in range(NWS)]
    wslD = [P.dsem("wsd%d" % i) for i in range(NWS)]
    ktB = [Buf("kt0"), Buf("kt1")]
    ktD = [P.dsem("ktd0"), P.dsem("ktd1")]
    vbB = Buf("vb")
    vbD = P.dsem("vbd")
    uTB = [Buf("uT%d" % g) for g in range(11)]
    slotB = [Buf("sl%d" % i) for i in range(NSL)]
    slotD = [P.dsem("sld%d" % i) for i in range(NSL)]
    slotDP = [P.dsem("slp%d" % i) for i in range(NSL)]
    poolD = {id(slotB[i]): slotDP[i] for i in range(NSL)}
    SstB = Buf("Sst")
    SstD = P.dsem("Sstd")
    biasB = Buf("biasH")
    o0B = Buf("o0")
    cvB = Buf("convst")
    cvD = P.dsem("cvd")
    wB = {n: Buf("w_" + n) for n in ["qkv", "o", "qf", "o2", "up0", "up1", "dn0", "dn1"]}
    wD = {n: P.dsem("wd_" + n) for n in wB}
    ktSB = Buf("ktS")
    vSB = Buf("vS")
    outD = P.dsem("outd")

    state = {"slot": 0, "ws": 0}

    def get_slot():
        i = state["slot"]
        state["slot"] = (i + 1) % NSL
        return slots[:, i, :], slotB[i], slotD[i]

    def ws_load(src_ap, ncols, kc, wbuf):
        i = state["ws"]
        state["ws"] = (i + 1) % NWS
        view = wsl[:, i, 0:kc * ncols].rearrange("p (c n) -> p c n", c=kc)
        P.dma(SPQ, view, src_ap, wslD[i], reads=[wbuf], writes=[wslB[i]])
        return view, wslB[i]

    KTb = [arena[:, i * 2 * KTW:(i + 1) * 2 * KTW].rearrange("p (c n) -> p c n", c=2) for i in range(2)]
    Vb = arena[:, 4 * KTW:4 * KTW + 33 * VW].rearrange("p (b e) -> p b e", e=VW)
    uT = arena[:, 0:FC * 512].rearrange("p (c n) -> p c n", c=FC)

    def cload(dst, src):
        P.dma(SPQ, dst, src, constD, writes=[constB])

    cload(identf[:], ident_d)
    cload(tabs[:], tabs_d)
    cload(cpos[:], cpos_d)
    cload(gcols[:], gcols_d)
    cload(sublnb[:], subln_d.partition_broadcast(128))
    cload(lamv[:], lamv_d.partition_broadcast(128))
    cload(rlb[:], rlb_d)
    cload(onormb[:], onorm_d.partition_broadcast(128))
    cload(convp[:], convp_d)
    cload(tri[:], tri_d)
    cload(triu[:], triu_d)
    if do_sample:
        cload(stabs[:, 0:ST], stabc_d)
        cload(stabs[:, ST:2 * ST], stabn_d)
    cb = [constB]
    P.op(DVE, lambda: V.tensor_copy(out=ident[:], in_=identf[:]), reads=cb, writes=cb)
    P.op(DVE, lambda: V.memset(convst[:], 0.0), writes=[cvB])
    P.op(DVE, lambda: V.memset(Sst[:], 0.0), writes=[SstB])
    sl, sB_, _ = get_slot()
    P.op(DVE, lambda: V.tensor_tensor(out=sl[:, 0:128], in0=lamv[:, 0:128], in1=lamv[:, 128:256], op=ALU.mult),
         reads=cb, writes=[sB_])
    P.op(DVE, lambda: V.reduce_sum(out=lamt[:, 0:1], in_=sl[:, 0:128], axis=AX.X), reads=[sB_], writes=cb)
    P.op(DVE, lambda: V.tensor_tensor(out=sl[:, 0:128], in0=lamv[:, 256:384], in1=lamv[:, 384:512], op=ALU.mult),
         reads=cb, writes=[sB_])
    P.op(DVE, lambda: V.reduce_sum(out=lamt[:, 1:2], in_=sl[:, 0:128], axis=AX.X), reads=[sB_], writes=cb)
    P.op(ACT, lambda: A.activation(out=lamt[:, 2:4], in_=lamt[:, 0:2], func=AF.Exp), reads=cb, writes=cb)
    P.op(DVE, lambda: V.tensor_tensor(out=lamt[:, 4:5], in0=lamt[:, 2:3], in1=lamt[:, 3:4], op=ALU.subtract),
         reads=cb, writes=cb)
    P.op(DVE, lambda: V.tensor_scalar(out=lamt[:, 4:5], in0=lamt[:, 4:5], scalar1=LAM_INIT0, scalar2=None, op0=ALU.add),
         reads=cb, writes=cb)
    P.op(DVE, lambda: V.tensor_scalar(out=lamt[:, 5:6], in0=lamt[:, 4:5], scalar1=-1.0, scalar2=None, op0=ALU.mult),
         reads=cb, writes=cb)
    P.op(DVE, lambda: V.tensor_scalar(out=sublnb[:], in0=sublnb[:], scalar1=1.0 - LAM_INIT0, scalar2=None, op0=ALU.mult),
         reads=cb, writes=cb)
    P.op(ACT, lambda: A.activation(out=lbt[:, 0:32], in_=rlb[:, 0:32], func=AF.Exp), reads=cb, writes=cb)
    P.op(DVE, lambda: V.tensor_tensor(out=lbt[:, 32:48], in0=lbt[:, 0:16], in1=lbt[:, 16:32], op=ALU.add),
         reads=cb, writes=cb)
    P.op(DVE, lambda: V.reciprocal(out=lbt[:, 32:48], in_=lbt[:, 32:48]), reads=cb, writes=cb)
    P.op(DVE, lambda: V.tensor_tensor(out=lbt[:, 32:48], in0=lbt[:, 32:48], in1=lbt[:, 16:32], op=ALU.mult),
         reads=cb, writes=cb)
    P.op(DVE, lambda: V.tensor_scalar(out=lbt[:, 48:64], in0=lbt[:, 32:48], scalar1=-1.0, scalar2=1.0,
                                      op0=ALU.mult, op1=ALU.add), reads=cb, writes=cb)

    def cast(dst, src, name):
        P.dma(POOL, dst, src, wD[name], writes=[wB[name]])

    def rk(ap):
        return ap.rearrange("(c p) n -> p c n", p=128)

    for h in range(NH):
        cast(wqk_b[h, :, :, 0:256], rk(w_qkv[:, h * 256:(h + 1) * 256]), "qkv")
        cast(wqk_b[h, :, :, 256:512], rk(w_qkv[:, D + h * 256:D + (h + 1) * 256]), "qkv")
        cast(wv_b[h], rk(w_qkv[:, 2 * D + h * 256:2 * D + (h + 1) * 256]), "qkv")
    for cp in range(4):
        cast(wo_b[cp], rk(w_o[:, cp * 512:(cp + 1) * 512]), "o")
    for l in range(2 if do_l1 else 1):
        for g in range(11):
            cast(wup_b[l, 2 * g], rk(w_up[l, :, g * 512:(g + 1) * 512]), "up%d" % l)
            cast(wup_b[l, 2 * g + 1], rk(w_up[l, :, DFF + g * 512:DFF + (g + 1) * 512]), "up%d" % l)
        for cp in range(4):
            for rg in range(4):
                cast(wdn_b[l, cp, rg], rk(w_dn[l, rg * 1408:(rg + 1) * 1408, cp * 512:(cp + 1) * 512]), "dn%d" % l)
        if l == 0 and do_l1:
            for h in range(RH):
                for part in range(4):
                    cast(wqf_b[h, :, :, part * 128:(part + 1) * 128],
                         rk(w_qfig[:, part * D + h * 128:part * D + (h + 1) * 128]), "qf")
            for cp in range(4):
                cast(wo2_b[cp], rk(w_o2[:, cp * 512:(cp + 1) * 512]), "o2")

    def mm_group(out_ap, pairs, out_bufs, read_bufs):
        n = len(pairs)
        for i, (l, r) in enumerate(pairs):
            P.op(PE, lambda: TE.matmul(out_ap, lhsT=l, rhs=r, start=(i == 0), stop=(i == n - 1)),
                 reads=read_bufs, writes=out_bufs, sig=(i == n - 1))

    pT = ps[:, 4 * 512:6 * 512].bitcast(BF16).rearrange("p (c n) -> p c n", c=KC)

    def norm_to_hT(gi, npart=128, nblk=NBLK, ncol=128):
        g_b = gcols[:, gi * KC:(gi + 1) * KC].unsqueeze(2).to_broadcast([128, KC, ncol])
        for blk in range(nblk):
            xs_ = xres[0:npart, blk, :]
            P.op(ACT, lambda: A.activation(out=hb[0:npart, :], in_=xs_, func=AF.Square,
                                           accum_out=small[0:npart, blk:blk + 1]),
                 reads=[xB[blk]], writes=[hbB, smallB])
            P.op(ACT, lambda: A.activation(out=small[0:npart, 8 + blk:9 + blk], in_=small[0:npart, blk:blk + 1],
                                           func=AF.Ln, scale=1.0 / D, bias=epsc[0:npart, :]),
                 reads=[smallB, constB], writes=[smallB])
            P.op(ACT, lambda: A.activation(out=small[0:npart, 16 + blk:17 + blk], in_=small[0:npart, 8 + blk:9 + blk],
                                           func=AF.Exp, scale=-0.5), reads=[smallB], writes=[smallB])
            P.op(DVE, lambda: V.tensor_scalar(out=hb[0:npart, :], in0=xs_, scalar1=small[0:npart, 16 + blk:17 + blk],
                                              scalar2=None, op0=ALU.mult),
                 reads=[xB[blk], smallB], writes=[hbB])
            for c in range(KC):
                P.op(PE, lambda: TE.transpose(out=pT[:, c, 0:npart], in_=hb[0:npart, c * 128:(c + 1) * 128],
                                              identity=ident[0:npart, 0:npart]),
                     reads=[hbB, constB], writes=[pb[4], pb[5]], sig=(c == KC - 1))
            P.op(DVE, lambda: V.tensor_tensor(out=hT[:, :, blk * ncol:(blk + 1) * ncol], in0=pT[:, :, 0:ncol],
                                              in1=g_b, op=ALU.mult),
                 reads=[pb[4], pb[5], constB], writes=[hTB])

    epsc = sb("epsc", [128, 1], F32)
    P.op(DVE, lambda: V.memset(epsc[:], EPS), writes=[constB])

    def load_x(j):
        for blk in range(NBLK):
            r0 = j * TT + blk * 128
            P.dma(SPQ, xres[:, blk, :], xp[r0:r0 + 128, :], xD[blk], writes=[xB[blk]])

    def attn_proj(j, h, wqk, wqkB, wv, wvB):
        QT = qtbuf[:, h % 2, :, :]
        qB = qtB[h % 2]
        ks_ap, kB, kD = get_slot()
        KT = ks_ap.bitcast(BF16)[:, 0:1024].rearrange("p (c n) -> p c n", c=2)
        vs_ap, vB, vD = get_slot()
        VH = vs_ap.bitcast(BF16)[:, 0:4 * VW].rearrange("p (b e) -> p b e", e=VW)
        hr = [hTB]
        for c in range(2):
            mm_group(bank(6), [(wqk[:, kc, c * 128:(c + 1) * 128], hT[:, kc, :]) for kc in range(KC)],
                     [pb[6]], hr + [wqkB])
            P.op(ACT, lambda: A.activation(out=QT[:, c, :], in_=bank(6), func=AF.Copy, scale=SCALE),
                 reads=[pb[6]], writes=[qB])
            mm_group(bank(7), [(wqk[:, kc, 256 + c * 128:256 + (c + 1) * 128], hT[:, kc, :]) for kc in range(KC)],
                     [pb[7]], hr + [wqkB])
            P.op(ACT, lambda: A.activation(out=KT[:, c, :], in_=bank(7), func=AF.Copy),
                 reads=[pb[7]], writes=[kB])
        if stage < 3.12:
            return QT, qB
        P.dma(SPQ, ktS[2 * h:2 * h + 2, :, j * TT:(j + 1) * TT].rearrange("c p n -> p c n"), KT, kD,
              reads=[kB], writes=[ktSB])
        if stage < 3.13:
            return QT, qB
        P.op(DVE, lambda: V.memset(VH[:, :, 256:VW], 1.0), writes=[vB])
        for half in range(2):
            so_ap, soB, soD = get_slot()
            for bb in range(2):
                blk = half * 2 + bb
                mm_group(bank(half)[:, bb * 256:(bb + 1) * 256],
                         [(hT[:, kc, blk * 128:(blk + 1) * 128], wqk[:, kc, 256:512]) for kc in range(KC)],
                         [pb[half]], hr + [wqkB])
            P.op(ACT, lambda: A.activation(out=so_ap[:, 0:512], in_=bank(half), func=AF.Copy),
                 reads=[pb[half]], writes=[soB])
            r0 = j * TT + half * 256
            P.dma(POOL, k_o[r0:r0 + 256, h * 256:(h + 1) * 256].rearrange("(b p) e -> p b e", p=128),
                  so_ap[:, 0:512].rearrange("p (b e) -> p b e", b=2), poolD[id(soB)], reads=[soB])
        if stage < 3.14:
            return QT, qB
        for half in range(2):
            so_ap, soB, soD = get_slot()
            for bb in range(2):
                blk = half * 2 + bb
                mm_group(bank(2 + half)[:, bb * 256:(bb + 1) * 256],
                         [(hT[:, kc, blk * 128:(blk + 1) * 128], wv[:, kc, :]) for kc in range(KC)],
                         [pb[2 + half]], hr + [wvB])
            P.op(ACT, lambda: A.activation(out=so_ap[:, 0:512], in_=bank(2 + half), func=AF.Copy),
                 reads=[pb[2 + half]], writes=[soB])
            P.op(DVE, lambda: V.tensor_copy(out=VH[:, 2 * half:2 * half + 2, 0:256],
                                            in_=so_ap[:, 0:512].rearrange("p (b e) -> p b e", b=2)),
                 reads=[soB], writes=[vB])
            r0 = j * TT + half * 256
            P.dma(POOL, v_o[r0:r0 + 256, h * 256:(h + 1) * 256].rearrange("(b p) e -> p b e", p=128),
                  so_ap[:, 0:512].rearrange("p (b e) -> p b e", b=2), poolD[id(soB)], reads=[soB])
        if stage < 3.15:
            return QT, qB
        P.dma(SPQ, vS[h, j * TT:(j + 1) * TT, :].rearrange("(b p) e -> p b e", p=128), VH, vD,
              reads=[vB], writes=[vSB])
        return QT, qB

    def attn_core(h, QT, qB, nkb, npast, par, nq, qsubs, diag_fn, slope, on_done, past_bias=None, kn_last=128):
        KTv = KTb[par]
        for c in range(2):
            pend = []
            LOOK = 2

            def qk(kb):
                wb_i = 4 + (kb % 3)
                kn = kn_last if kb == nkb - 1 else 128
                P.op(PE, lambda: TE.matmul(bank(wb_i)[0:kn, 0:nq], lhsT=KTv[:, c, kb * 128:kb * 128 + kn],
                                           rhs=QT[:, c, 0:nq], start=True, stop=True),
                     reads=[ktB[par], qB], writes=[pb[wb_i]])
                t_ap, tB, _ = get_slot()
                if kb < npast:
                    bias_ap = past_bias if past_bias is not None else biasH[:, 512:512 + nq]
                    ccol = cH[:, (npast - kb - 1):(npast - kb)]
                else:
                    bias_ap, ccol = diag_fn(kb - npast)
                P.op(DVE, lambda: V.tensor_tensor(out=t_ap[0:kn, 0:nq], in0=bank(wb_i)[0:kn, 0:nq], in1=bias_ap[0:kn, :],
                                                  op=ALU.add),
                     reads=[pb[wb_i], biasB], writes=[tB])
                p_ap, pB_, _ = get_slot()
                PT = p_ap.bitcast(BF16)
                P.op(ACT, lambda: A.activation(out=PT[0:kn, 0:nq], in_=t_ap[0:kn, 0:nq], func=AF.Exp, bias=ccol[0:kn, :]),
                     reads=[tB, biasB], writes=[pB_])
                return PT, pB_

            def pv(kb, PT, pB_):
                kn = kn_last if kb == nkb - 1 else 128
                for qi, (q0, qn) in enumerate(qsubs):
                    P.op(PE, lambda: TE.matmul(bank(qi)[0:qn, 0:257], lhsT=PT[0:kn, q0:q0 + qn], rhs=Vb[0:kn, kb, 0:257],
                                               start=(kb == 0), stop=(kb == nkb - 1)),
                         reads=[pB_, vbB], writes=[pb[qi]], sig=(qi == len(qsubs) - 1))

            for kb in range(nkb + LOOK):
                if kb < nkb:
                    pend.append(qk(kb))
                if kb >= LOOK:
                    PT, pB_ = pend[kb - LOOK]
                    pv(kb - LOOK, PT, pB_)
            nqs = len(qsubs)
            for qi, (q0, qn) in enumerate(qsubs):
                P.op(DVE, lambda: V.reciprocal(out=small[0:qn, 24 + qi:25 + qi], in_=bank(qi)[0:qn, 256:257]),
                     reads=[pb[qi]], writes=[smallB])
                if c == 0:
                    P.op(DVE, lambda: V.tensor_scalar(out=o0buf[0:qn, qi, :], in0=bank(qi)[0:qn, 0:256],
                                                      scalar1=small[0:qn, 24 + qi:25 + qi], scalar2=None, op0=ALU.mult),
                         reads=[pb[qi], smallB], writes=[o0B])
                else:
                    P.op(DVE, lambda: V.tensor_scalar(out=small[0:qn, 28 + qi:29 + qi], in0=small[0:qn, 24 + qi:25 + qi],
                                                      scalar1=lamt[0:qn, 5:6], scalar2=None, op0=ALU.mult),
                         reads=[smallB, constB], writes=[smallB])
                    P.op(DVE, lambda: V.scalar_tensor_tensor(out=o0buf[0:qn, qi, :], in0=bank(qi)[0:qn, 0:256],
                                                             scalar=small[0:qn, 28 + qi:29 + qi],
                                                             in1=o0buf[0:qn, qi, :], op0=ALU.mult, op1=ALU.add),
                         reads=[pb[qi], smallB], writes=[o0B])
        on_done()

    def attn_finish(h, qsubs, ncol):
        j_ap, jB, _ = get_slot()
        junk = j_ap.bitcast(BF16)
        on_ap, onB, _ = get_slot()
        ON = on_ap.bitcast(BF16)[:, 0:1024].rearrange("p (b e) -> p b e", b=4)
        for qi, (q0, qn) in enumerate(qsubs):
            P.op(ACT, lambda: A.activation(out=junk[0:qn, 0:256], in_=o0buf[0:qn, qi, :], func=AF.Square,
                                           accum_out=small[0:qn, 32 + qi:33 + qi]),
                 reads=[o0B], writes=[jB, smallB])
        nqs = len(qsubs)
        qn = qsubs[0][1]
        P.op(ACT, lambda: A.activation(out=small[0:qn, 36:36 + nqs], in_=small[0:qn, 32:32 + nqs], func=AF.Ln,
                                       scale=1.0 / 256, bias=epsc[0:qn, :]), reads=[smallB, constB], writes=[smallB])
        P.op(ACT, lambda: A.activation(out=small[0:qn, 40:40 + nqs], in_=small[0:qn, 36:36 + nqs], func=AF.Exp,
                                       scale=-0.5), reads=[smallB], writes=[smallB])
        for qi, (q0, qn) in enumerate(qsubs):
            P.op(DVE, lambda: V.scalar_tensor_tensor(out=ON[0:qn, qi, :], in0=o0buf[0:qn, qi, :],
                                                     scalar=small[0:qn, 40 + qi:41 + qi], in1=sublnb[0:qn, :],
                                                     op0=ALU.mult, op1=ALU.mult),
                 reads=[o0B, smallB, constB], writes=[onB])
        pO = bank(7).bitcast(BF16).rearrange("p (e n) -> p e n", e=2)
        for eh in range(2):
            for qi, (q0, qn) in enumerate(qsubs):
                last = (eh == 1 and qi == nqs - 1)
                P.op(PE, lambda: TE.transpose(out=pO[:, eh, qi * ncol:qi * ncol + qn],
                                              in_=ON[0:qn, qi, eh * 128:(eh + 1) * 128], identity=ident[0:qn, 0:qn]),
                     reads=[onB, constB], writes=[pb[7]], sig=last)
        ntok = nqs * ncol
        P.op(ACT, lambda: A.activation(out=oT[:, 2 * h:2 * h + 2, 0:ntok], in_=pO[:, :, 0:ntok], func=AF.Copy),
             reads=[pb[7]], writes=[oTB])

    def attention_layer(j):
        nkb = 4 * (j + 1)
        nk = nkb * 128
        qsubs = [(i * 128, 128) for i in range(4)]
        wq = [None] * NH
        wq[0] = (ws_load(wqk_b[0], 512, KC, wB["qkv"]), ws_load(wv_b[0], 256, KC, wB["qkv"]))
        for h in range(NH if stage >= 3.5 else 1):
            (wqk, wqkB), (wv, wvB) = wq[h]
            QT, qB = attn_proj(j, h, wqk, wqkB, wv, wvB)
            if stage < 3.2:
                continue
            par = h % 2
            first_arena = (h == 0)
            extra = uTB if first_arena else []
            P.dma(SPQ, KTb[par][:, :, 0:nk], ktS[2 * h:2 * h + 2, :, 0:nk].rearrange("c p n -> p c n"), ktD[par],
                  reads=[ktSB], writes=[ktB[par]] + (extra if h < 2 else []))
            P.dma(SPQ, Vb[:, 0:nkb, :], vS[h, 0:nk, :].rearrange("(b p) e -> p b e", p=128), vbD,
                  reads=[vSB], writes=[vbB] + extra)
            if h + 1 < NH:
                wq[h + 1] = (ws_load(wqk_b[h + 1], 512, KC, wB["qkv"]), ws_load(wv_b[h + 1], 256, KC, wB["qkv"]))
            if stage < 3.3:
                continue
            slope = 2.0 ** (-(h + 1))
            P.op(DVE, lambda: V.tensor_scalar(out=biasH[:], in0=tabs[:], scalar1=-slope, scalar2=None, op0=ALU.mult),
                 reads=[constB], writes=[biasB])
            P.op(DVE, lambda: V.tensor_scalar(out=cH[:], in0=cpos[:], scalar1=-slope, scalar2=None, op0=ALU.mult),
                 reads=[constB], writes=[biasB])

            def diag_fn(m):
                return biasH[:, 384 - 128 * m:384 - 128 * m + 512], cH[:, 0:1]

            attn_core(h, QT, qB, nkb, nkb - 4, par, TT, qsubs, diag_fn, slope, lambda: None)
            if stage < 3.4:
                continue
            attn_finish(h, qsubs, 128)

    def out_proj(w_scr, wname):
        nxt = ws_load(w_scr[0], 512, KC, wB[wname])
        for cp in range(4):
            wv_, wvB_ = nxt
            if cp + 1 < 4:
                nxt = ws_load(w_scr[cp + 1], 512, KC, wB[wname])
            CB = cfg["CB"]
            for blk in range(cfg["nblk"]):
                mm_group(bank(blk)[0:CB, :], [(oT[:, kc, blk * CB:(blk + 1) * CB], wv_[:, kc, :]) for kc in range(KC)],
                         [pb[blk]], [oTB, wvB_])
                xs_ = xres[0:CB, blk, cp * 512:(cp + 1) * 512]
                P.op(DVE, lambda: V.tensor_tensor(out=xs_, in0=bank(blk)[0:CB, :], in1=xs_, op=ALU.add),
                     reads=[pb[blk]], writes=[xB[blk]])

    def ffn(layer):
        TW, CB = cfg["TW"], cfg["CB"]
        wn_up = "up%d" % layer
        wn_dn = "dn%d" % layer
        cpv = convp[:, layer * FC * 4:(layer + 1) * FC * 4].rearrange("p (f k) -> p f k", k=4)
        seq = [wup_b[layer, i] for i in range(22)]
        loaded = [ws_load(seq[0], 512, KC, wB[wn_up]), ws_load(seq[1], 512, KC, wB[wn_up])]
        for g in range(11):
            (wg, wgB), (wvv, wvvB) = loaded[2 * g], loaded[2 * g + 1]
            for ch in range(4):
                fc = g * 4 + ch
                mm_group(bank(4 + (fc % 2))[:, 0:TW], [(wg[:, kc, ch * 128:(ch + 1) * 128], hT[:, kc, 0:TW]) for kc in range(KC)],
                         [pb[4 + (fc % 2)]], [hTB, wgB])
                g_ap, gB, _ = get_slot()
                t_ap, tB, _ = get_slot()
                first_w = [gB] + (([ktB[0], ktB[1], vbB]) if (g == 0 and ch == 0) else [])
                P.op(ACT, lambda: A.activation(out=g_ap[:, 2:2 + TW], in_=bank(4 + (fc % 2))[:, 0:TW], func=AF.Copy),
                     reads=[pb[4 + (fc % 2)]], writes=[gB])
                P.op(ACT, lambda: A.activation(out=g_ap[:, 0:2], in_=convst[:, layer, fc, :], func=AF.Copy),
                     reads=[cvB], writes=[gB])
                P.op(ACT, lambda: A.activation(out=convst[:, layer, fc, :], in_=g_ap[:, TW:TW + 2], func=AF.Copy),
                     reads=[gB], writes=[cvB])
                P.op(DVE, lambda: V.tensor_scalar(out=t_ap[:, 0:TW], in0=g_ap[:, 0:TW], scalar1=cpv[:, fc, 0:1],
                                                  scalar2=cpv[:, fc, 3:4], op0=ALU.mult, op1=ALU.add),
                     reads=[gB, constB], writes=[tB])
                P.op(DVE, lambda: V.scalar_tensor_tensor(out=t_ap[:, 0:TW], in0=g_ap[:, 1:1 + TW], scalar=cpv[:, fc, 1:2],
                                                         in1=t_ap[:, 0:TW], op0=ALU.mult, op1=ALU.add),
                     reads=[gB, constB], writes=[tB])
                P.op(DVE, lambda: V.scalar_tensor_tensor(out=t_ap[:, 0:TW], in0=g_ap[:, 2:2 + TW], scalar=cpv[:, fc, 2:3],
                                                         in1=t_ap[:, 0:TW], op0=ALU.mult, op1=ALU.add),
                     reads=[gB, constB], writes=[tB])
                P.op(ACT, lambda: A.activation(out=t_ap[:, 0:TW], in_=t_ap[:, 0:TW], func=AF.Silu),
                     reads=[tB], writes=[tB])
                mm_group(bank(6 + (fc % 2))[:, 0:TW], [(wvv[:, kc, ch * 128:(ch + 1) * 128], hT[:, kc, 0:TW]) for kc in range(KC)],
                         [pb[6 + (fc % 2)]], [hTB, wvvB])
                uw = [uTB[g]] + (([ktB[0], ktB[1], vbB]) if (g == 0 and ch == 0) else [])
                P.op(DVE, lambda: V.tensor_tensor(out=uT[:, fc, 0:TW], in0=t_ap[:, 0:TW], in1=bank(6 + (fc % 2))[:, 0:TW],
                                                  op=ALU.mult),
                     reads=[tB, pb[6 + (fc % 2)]], writes=uw)
            if g + 1 < 11:
                loaded.append(ws_load(seq[2 * g + 2], 512, KC, wB[wn_up]))
                loaded.append(ws_load(seq[2 * g + 3], 512, KC, wB[wn_up]))
        dseq = [(cp, rg) for cp in range(4) for rg in range(4)]
        nxt = ws_load(wdn_b[layer, 0, 0], 512, 11, wB[wn_dn])
        for i, (cp, rg) in enumerate(dseq):
            wd, wdB = nxt
            if i + 1 < len(dseq):
                cp2, rg2 = dseq[i + 1]
                nxt = ws_load(wdn_b[layer, cp2, rg2], 512, 11, wB[wn_dn])
            for blk in range(cfg["nblk"]):
                for ch in range(11):
                    fc = rg * 11 + ch
                    first = (rg == 0 and ch == 0)
                    last = (rg == 3 and ch == 10)
                    P.op(PE, lambda: TE.matmul(bank(blk)[0:CB, :], lhsT=uT[:, fc, blk * CB:(blk + 1) * CB], rhs=wd[:, ch, :],
                                               start=first, stop=last),
                         reads=[uTB[fc // 4], wdB], writes=[pb[blk]], sig=(ch == 10))
            if rg == 3:
                for blk in range(cfg["nblk"]):
                    xs_ = xres[0:CB, blk, cp * 512:(cp + 1) * 512]
                    P.op(DVE, lambda: V.tensor_tensor(out=xs_, in0=bank(blk)[0:CB, :], in1=xs_, op=ALU.add),
                         reads=[pb[blk]], writes=[xB[blk]])

    cfg = {"TW": TT, "nblk": NBLK, "CB": 128, "mid": 63, "HN": 8, "HC": 64, "hmid": 31}
    arenaF = arena[:, 0:ARENA].bitcast(F32)
    HSET = 5632
    hb_ = []
    for S_ in range(2):
        o = S_ * HSET
        d_ = {}
        d_["A"] = arenaF[:, o:o + 1024]
        d_["B"] = arenaF[:, o + 1024:o + 2048]
        d_["kk"] = arenaF[:, o + 2048:o + 2560]
        d_["qs"] = arenaF[:, o + 2560:o + 3072]
        d_["eb"] = arenaF[:, o + 3072:o + 3584]
        d_["sg"] = arenaF[:, o + 3584:o + 4608].rearrange("p (b e) -> p b e", e=128)
        d_["qtT"] = arenaF[:, o + 4608:o + 4864].bitcast(BF16)
        d_["ktT"] = arenaF[:, o + 4864:o + 5120].bitcast(BF16)
        d_["ibf"] = arenaF[:, o + 5120:o + 5632].bitcast(BF16).rearrange("p (b e) -> p b e", e=128)
        for n_ in list(d_.keys()):
            d_[n_ + "B"] = Buf("hg_%s%d" % (n_, S_))
        hb_.append(d_)
    hsmB = [Buf("hsm0"), Buf("hsm1")]
    hs2B = Buf("hs2")
    sbfB = Buf("sbf")
    ATB = Buf("ATt")
    on32B = Buf("on32")
    onbB = Buf("onb")
    ktokB = Buf("ktok")
    hjB = Buf("hjunk")
    hg_all = [v_ for S_ in range(2) for v_ in hb_[S_].values() if isinstance(v_, Buf)]
    fnb = arenaF[:, 0:D]
    fnbB = Buf("fnb")
    fnbD = P.dsem("fnbd")
    arena_bufs = hg_all + uTB + ktB + [vbB, fnbB]

    def arena_fence():
        P.op(DVE, lambda: V.memset(fence_s[:], 0.0), writes=arena_bufs)

    def hgrn_head(h, wqf, wqfB):
        TW, nblk, CB, mid = cfg["TW"], cfg["HN"], cfg["HC"], cfg["hmid"]
        S_ = h % 2
        b_ = hb_[S_]
        hs = hsm[:, S_, :]
        hsB = hsmB[S_]
        NA = nblk * 2 * CB
        Av = b_["A"][:, 0:NA].rearrange("p (b c) -> p b c", c=2 * CB)
        Bv = b_["B"][:, 0:NA].rearrange("p (b c) -> p b c", c=2 * CB)
        AB, BB = b_["AB"], b_["BB"]

        def v3(ap):
            return ap[:, 0:TW].rearrange("p (b c) -> p b c", c=CB)
        kk, qs, eb, sg, qtT, ktT, ibf = b_["kk"], b_["qs"], b_["eb"], b_["sg"], b_["qtT"], b_["ktT"], b_["ibf"]
        P.op(DVE, lambda: V.memset(Av[:, :, 0:CB], 0.0), writes=[AB])
        P.op(DVE, lambda: V.memset(Bv[:, :, 0:CB], 0.0), writes=[BB])
        mm_group(bank(4)[:, 0:TW], [(wqf[:, kc, 0:128], hT[:, kc, 0:TW]) for kc in range(KC)], [pb[4]], [hTB, wqfB])
        P.op(ACT, lambda: A.activation(out=qs[:, 0:TW], in_=bank(4)[:, 0:TW], func=AF.Silu),
             reads=[pb[4]], writes=[b_["qsB"]])
        mm_group(bank(5)[:, 0:TW], [(wqf[:, kc, 128:256], hT[:, kc, 0:TW]) for kc in range(KC)], [pb[5]], [hTB, wqfB])
        P.op(ACT, lambda: A.activation(out=eb[:, 0:TW], in_=bank(5)[:, 0:TW], func=AF.Sigmoid),
             reads=[pb[5]], writes=[b_["ebB"]])
        P.op(DVE, lambda: V.tensor_scalar(out=Av[:, :, CB:2 * CB], in0=v3(eb), scalar1=lbt[:, 48 + h:49 + h],
                                          scalar2=lbt[:, 32 + h:33 + h], op0=ALU.mult, op1=ALU.add),
             reads=[b_["ebB"], constB], writes=[AB])
        P.op(DVE, lambda: V.tensor_scalar(out=v3(kk), in0=Av[:, :, CB:2 * CB], scalar1=-1.0, scalar2=1.0,
                                          op0=ALU.mult, op1=ALU.add), reads=[AB], writes=[b_["kkB"]])
        P.op(ACT, lambda: A.activation(out=Av[:, :, CB:2 * CB], in_=Av[:, :, CB:2 * CB], func=AF.Ln),
             reads=[AB], writes=[AB])
        src, srcB, dst, dstB = Av, AB, Bv, BB
        k = 1
        while k < CB:
            P.op(DVE, lambda: V.tensor_tensor(out=dst[:, :, CB:2 * CB], in0=src[:, :, CB:2 * CB],
                                              in1=src[:, :, CB - k:2 * CB - k], op=ALU.add),
                 reads=[srcB], writes=[dstB])
            src, srcB, dst, dstB = dst, dstB, src, srcB
            k *= 2
        bv, bB = src, srcB
        P.op(DVE, lambda: V.tensor_scalar(out=hs[:, 0:nblk].unsqueeze(2), in0=bv[:, :, CB + mid:CB + mid + 1],
                                          scalar1=-1.0, scalar2=None, op0=ALU.mult), reads=[bB], writes=[hsB])
        P.op(ACT, lambda: A.activation(out=hs[:, 8:8 + nblk], in_=hs[:, 0:nblk], func=AF.Exp, scale=-1.0),
             reads=[hsB], writes=[hsB])
        P.op(ACT, lambda: A.activation(out=hs[:, 16:16 + nblk].unsqueeze(2), in_=bv[:, :, 2 * CB - 1:2 * CB], func=AF.Exp),
             reads=[bB], writes=[hsB])
        for blk in range(nblk):
            P.op(ACT, lambda: A.activation(out=eb[:, blk * CB:(blk + 1) * CB], in_=bv[:, blk, CB:2 * CB], func=AF.Exp,
                                           bias=hs[:, blk:blk + 1]), reads=[bB, hsB], writes=[b_["ebB"]])
        P.op(DVE, lambda: V.tensor_copy(out=hs[:, 24:24 + nblk].unsqueeze(2), in_=v3(eb)[:, :, CB - 1:CB]),
             reads=[b_["ebB"]], writes=[hsB])
        P.op(DVE, lambda: V.tensor_tensor(out=qtT[:, 0:TW], in0=qs[:, 0:TW], in1=eb[:, 0:TW], op=ALU.mult),
             reads=[b_["qsB"], b_["ebB"]], writes=[b_["qtTB"]])
        P.op(DVE, lambda: V.reciprocal(out=eb[:, 0:TW], in_=eb[:, 0:TW]), reads=[b_["ebB"]], writes=[b_["ebB"]])
        P.op(DVE, lambda: V.tensor_tensor(out=ktT[:, 0:TW], in0=kk[:, 0:TW], in1=eb[:, 0:TW], op=ALU.mult),
             reads=[b_["kkB"], b_["ebB"]], writes=[b_["ktTB"]])
        pK = bank(2).bitcast(BF16)
        for blk in range(nblk):
            cs = slice(blk * CB, (blk + 1) * CB)
            bk = 6 + blk % 2
            mm_group(bank(bk)[0:CB, 0:256],
                     [(hT[:, kc, blk * CB:(blk + 1) * CB], wqf[:, kc, 256:512]) for kc in range(KC)],
                     [pb[bk]], [hTB, wqfB])
            P.op(DVE, lambda: V.tensor_copy(out=ibf[0:CB, blk, :], in_=bank(bk)[0:CB, 0:128]),
                 reads=[pb[bk]], writes=[b_["ibfB"]])
            P.op(ACT, lambda: A.activation(out=sg[0:CB, blk, :], in_=bank(bk)[0:CB, 128:256], func=AF.Silu),
                 reads=[pb[bk]], writes=[b_["sgB"]])
            P.op(PE, lambda: TE.matmul(bank(0)[0:CB, 0:CB], lhsT=ktT[:, cs], rhs=qtT[:, cs], start=True, stop=True),
                 reads=[b_["ktTB"], b_["qtTB"]], writes=[pb[0]])
            P.op(DVE, lambda: V.tensor_tensor(out=ATt[0:CB, 0:CB], in0=bank(0)[0:CB, 0:CB], in1=triu[0:CB, 0:CB],
                                              op=ALU.mult), reads=[pb[0], constB], writes=[ATB])
            P.op(DVE, lambda: V.tensor_scalar(out=sbf[:, :], in0=Sst[:, h, :], scalar1=hs[:, 8 + blk:9 + blk],
                                              scalar2=None, op0=ALU.mult), reads=[SstB, hsB], writes=[sbfB])
            P.op(PE, lambda: TE.matmul(bank(1)[0:CB, 0:128], lhsT=ATt[0:CB, 0:CB], rhs=ibf[0:CB, blk, :],
                                       start=True, stop=False), reads=[ATB, b_["ibfB"]], writes=[pb[1]], sig=False)
            P.op(PE, lambda: TE.matmul(bank(1)[0:CB, 0:128], lhsT=qtT[:, cs], rhs=sbf[:, :], start=False, stop=True),
                 reads=[b_["qtTB"], sbfB], writes=[pb[1]])
            P.op(PE, lambda: TE.transpose(out=pK[0:CB, 0:128], in_=ktT[:, cs], identity=ident[:, :]),
                 reads=[b_["ktTB"], constB], writes=[pb[2]])
            P.op(ACT, lambda: A.activation(out=ktok[0:CB, :], in_=pK[0:CB, 0:128], func=AF.Copy),
                 reads=[pb[2]], writes=[ktokB])
            P.op(PE, lambda: TE.matmul(bank(3)[:, 0:128], lhsT=ktok[0:CB, :], rhs=ibf[0:CB, blk, :], start=True, stop=True),
                 reads=[ktokB, b_["ibfB"]], writes=[pb[3]])
            P.op(DVE, lambda: V.tensor_scalar(out=Sst[:, h, :], in0=Sst[:, h, :], scalar1=hs[:, 16 + blk:17 + blk],
                                              scalar2=None, op0=ALU.mult), reads=[hsB], writes=[SstB])
            P.op(DVE, lambda: V.scalar_tensor_tensor(out=Sst[:, h, :], in0=bank(3)[:, 0:128],
                                                     scalar=hs[:, 24 + blk:25 + blk], in1=Sst[:, h, :],
                                                     op0=ALU.mult, op1=ALU.add), reads=[pb[3], hsB], writes=[SstB])
            P.op(ACT, lambda: A.activation(out=hjunk[0:CB, :], in_=bank(1)[0:CB, 0:128], func=AF.Square,
                                           accum_out=hs2[0:CB, 0:1]), reads=[pb[1]], writes=[hjB, hs2B])
            P.op(ACT, lambda: A.activation(out=hs2[0:CB, 1:2], in_=hs2[0:CB, 0:1], func=AF.Ln, scale=1.0 / 128,
                                           bias=epsc[0:CB, :]), reads=[hs2B, constB], writes=[hs2B])
            P.op(ACT, lambda: A.activation(out=hs2[0:CB, 2:3], in_=hs2[0:CB, 1:2], func=AF.Exp, scale=-0.5),
                 reads=[hs2B], writes=[hs2B])
            P.op(DVE, lambda: V.scalar_tensor_tensor(out=on32[0:CB, :], in0=bank(1)[0:CB, 0:128], scalar=hs2[0:CB, 2:3],
                                                     in1=onormb[0:CB, :], op0=ALU.mult, op1=ALU.mult),
                 reads=[pb[1], hs2B, constB], writes=[on32B])
            P.op(DVE, lambda: V.tensor_tensor(out=onb[0:CB, :], in0=on32[0:CB, :], in1=sg[0:CB, blk, :], op=ALU.mult),
                 reads=[on32B, b_["sgB"]], writes=[onbB])
            P.op(PE, lambda: TE.transpose(out=pK[:, 256:256 + CB], in_=onb[0:CB, :], identity=ident[0:CB, 0:CB]),
                 reads=[onbB, constB], writes=[pb[2]])
            P.op(ACT, lambda: A.activation(out=oT[:, h, cs], in_=pK[:, 256:256 + CB], func=AF.Copy),
                 reads=[pb[2]], writes=[oTB])

    def hgrn_layer():
        nxt = ws_load(wqf_b[0], 512, KC, wB["qf"])
        for h in range(RH):
            wqf, wqfB = nxt
            if h + 1 < RH:
                nxt = ws_load(wqf_b[h + 1], 512, KC, wB["qf"])
            hgrn_head(h, wqf, wqfB)

    def final_store(out_ap, row0):
        TW, nblk, CB = cfg["TW"], cfg["nblk"], cfg["CB"]
        P.dma(SPQ, fnb, fnorm_d.partition_broadcast(128), fnbD, writes=[fnbB])
        for blk in range(nblk):
            xs_ = xres[0:CB, blk, :]
            P.op(ACT, lambda: A.activation(out=hb[0:CB, :], in_=xs_, func=AF.Square,
                                           accum_out=small[0:CB, blk:blk + 1]),
                 reads=[xB[blk]], writes=[hbB, smallB])
            P.op(ACT, lambda: A.activation(out=small[0:CB, 8 + blk:9 + blk], in_=small[0:CB, blk:blk + 1],
                                           func=AF.Ln, scale=1.0 / D, bias=epsc[0:CB, :]),
                 reads=[smallB, constB], writes=[smallB])
            P.op(ACT, lambda: A.activation(out=small[0:CB, 16 + blk:17 + blk], in_=small[0:CB, 8 + blk:9 + blk],
                                           func=AF.Exp, scale=-0.5), reads=[smallB], writes=[smallB])
            P.op(DVE, lambda: V.scalar_tensor_tensor(out=xs_, in0=xs_, scalar=small[0:CB, 16 + blk:17 + blk],
                                                     in1=fnb[0:CB, :], op0=ALU.mult, op1=ALU.mult),
                 reads=[smallB, fnbB], writes=[xB[blk]])
            r0 = row0 + blk * 128
            P.dma(POOL, out_ap[r0:r0 + CB, :], xs_, outD, reads=[xB[blk]])
        arena_fence()

    def store_x_dbg(j):
        for blk in range(NBLK):
            r0 = j * TT + blk * 128
            P.dma(POOL, y_o[r0:r0 + 128, :], xres[:, blk, :], outD, reads=[xB[blk]])

    for j in range(NT):
        if stage < 2:
            break
        load_x(j)
        norm_to_hT(0)
        if stage >= 3:
            attention_layer(j)
        if stage >= 4:
            out_proj(wo_b, "o")
            norm_to_hT(1)
        if stage >= 5:
            ffn(0)
        if stage >= 6 and do_l1:
            arena_fence()
            norm_to_hT(2)
            hgrn_layer()
        if stage >= 7 and do_l1:
            out_proj(wo2_b, "o2")
            norm_to_hT(3)
        if stage >= 8 and do_l1:
            arena_fence()
            ffn(1)
            arena_fence()
        if stage >= 9 and do_l1:
            final_store(y_o, j * TT)
        else:
            store_x_dbg(j)

    if do_l1:
        P.dma(POOL, shg_o.rearrange("h d v -> d h v"), Sst[:], SstD, reads=[SstB])
    P.dma(POOL, scv_o.rearrange("l p (f k) -> p l f k", k=2), convst[:], cvD, reads=[cvB])

    ktSsB = Buf("ktSs")
    ckD = P.dsem("ckd")

    def kt_prepass(q):
        for kb in range(PAST // 128):
            P.dma(POOL, hb[:, :], ck_d[q, kb * 128:(kb + 1) * 128, :], ckD, writes=[hbB])
            for c in range(KC):
                P.op(PE, lambda: TE.transpose(out=pT[:, c, 0:128], in_=hb[:, c * 128:(c + 1) * 128], identity=ident[:, :]),
                     reads=[hbB, constB], writes=[pb[4], pb[5]], sig=(c == KC - 1))
            P.op(ACT, lambda: A.activation(out=hT[:, :, (kb % 4) * 128:(kb % 4 + 1) * 128], in_=pT[:, :, 0:128], func=AF.Copy),
                 reads=[pb[4], pb[5]], writes=[hTB])
            if kb % 4 == 3:
                t0 = (kb // 4) * 512
                P.dma(SPQ, ktSs[q, :, :, t0:t0 + 512].rearrange("c p n -> p c n"), hT[:, :, :], ckD,
                      reads=[hTB], writes=[ktSsB])

    def attn_proj_s(q, h, wqk, wqkB, wv, wvB, par):
        QT = qtbuf[:, h % 2, :, :]
        qB = qtB[h % 2]
        hr = [hTB]
        for c in range(2):
            mm_group(bank(6)[:, 0:ST], [(wqk[:, kc, c * 128:(c + 1) * 128], hT[:, kc, 0:ST]) for kc in range(KC)],
                     [pb[6]], hr + [wqkB])
            P.op(ACT, lambda: A.activation(out=QT[:, c, 0:ST], in_=bank(6)[:, 0:ST], func=AF.Copy, scale=SCALE),
                 reads=[pb[6]], writes=[qB])
            mm_group(bank(7)[:, 0:ST], [(wqk[:, kc, 256 + c * 128:256 + (c + 1) * 128], hT[:, kc, 0:ST]) for kc in range(KC)],
                     [pb[7]], hr + [wqkB])
            P.op(ACT, lambda: A.activation(out=KTb[par][:, c, PAST:PAST + ST], in_=bank(7)[:, 0:ST], func=AF.Copy),
                 reads=[pb[7]], writes=[ktB[par]])
        so_ap, soB, _ = get_slot()
        mm_group(bank(0)[0:ST, 0:256], [(hT[:, kc, 0:ST], wqk[:, kc, 256:512]) for kc in range(KC)], [pb[0]], hr + [wqkB])
        P.op(ACT, lambda: A.activation(out=so_ap[0:ST, 0:256], in_=bank(0)[0:ST, 0:256], func=AF.Copy),
             reads=[pb[0]], writes=[soB])
        P.dma(POOL, ks_o[q * ST:(q + 1) * ST, h * 256:(h + 1) * 256], so_ap[0:ST, 0:256], poolD[id(soB)], reads=[soB])
        so2, so2B, _ = get_slot()
        mm_group(bank(2)[0:ST, 0:256], [(hT[:, kc, 0:ST], wv[:, kc, :]) for kc in range(KC)], [pb[2]], hr + [wvB])
        P.op(ACT, lambda: A.activation(out=so2[0:ST, 0:256], in_=bank(2)[0:ST, 0:256], func=AF.Copy),
             reads=[pb[2]], writes=[so2B])
        P.dma(POOL, vs_o[q * ST:(q + 1) * ST, h * 256:(h + 1) * 256], so2[0:ST, 0:256], poolD[id(so2B)], reads=[so2B])
        P.op(DVE, lambda: V.tensor_copy(out=Vb[0:ST, 32, 0:256], in_=so2[0:ST, 0:256]), reads=[so2B], writes=[vbB])
        P.op(DVE, lambda: V.memset(Vb[:, :, 256:VW], 1.0), writes=[vbB])
        return QT, qB

    def attention_layer_s(q):
        nkb = PAST // 128 + 1
        qsubs = [(0, ST)]
        wq = [None] * NH
        wq[0] = (ws_load(wqk_b[0], 512, KC, wB["qkv"]), ws_load(wv_b[0], 256, KC, wB["qkv"]))
        for h in range(NH):
            (wqk, wqkB), (wv, wvB) = wq[h]
            par = h % 2
            P.dma(SPQ, KTb[par][:, :, 0:PAST], ktSs[q, 2 * h:2 * h + 2, :, :].rearrange("c p n -> p c n"), ktD[par],
                  reads=[ktSsB], writes=[ktB[par]])
            for i4 in range(4):
                P.dma(POOL, Vb[:, 8 * i4:8 * i4 + 8, 0:256],
                      cv_d[q, 1024 * i4:1024 * (i4 + 1), h * 256:(h + 1) * 256].rearrange("(b p) e -> p b e", p=128),
                      vbD, writes=[vbB])
            QT, qB = attn_proj_s(q, h, wqk, wqkB, wv, wvB, par)
            if h + 1 < NH:
                wq[h + 1] = (ws_load(wqk_b[h + 1], 512, KC, wB["qkv"]), ws_load(wv_b[h + 1], 256, KC, wB["qkv"]))
            slope = 2.0 ** (-(h + 1))
            P.op(DVE, lambda: V.tensor_scalar(out=sbias[:], in0=stabs[:], scalar1=-slope, scalar2=None, op0=ALU.mult),
                 reads=[constB], writes=[biasB])
            P.op(DVE, lambda: V.tensor_scalar(out=cH[:], in0=cpos[:], scalar1=-slope, scalar2=None, op0=ALU.mult),
                 reads=[constB], writes=[biasB])

            def diag_fn(m):
                return sbias[:, ST:2 * ST], cH[:, 0:1]

            attn_core(h, QT, qB, nkb, nkb - 1, par, ST, qsubs, diag_fn, slope, lambda: None,
                      past_bias=sbias[:, 0:ST], kn_last=ST)
            attn_finish(h, qsubs, ST)

    if do_sample:
        cfg.update({"TW": ST, "nblk": 1, "CB": ST, "HN": 1, "HC": ST, "hmid": ST // 2 - 1})
        for q in range(SB):
            arena_fence()
            kt_prepass(q)
            P.dma(SPQ, Sst[:], shg_in[q].rearrange("h d v -> d h v"), SstD, writes=[SstB])
            for l in range(2):
                P.dma(SPQ, convst[:, l, :, :].rearrange("p f k -> p (f k)"), scv_in[l, q], cvD, writes=[cvB])
            P.dma(SPQ, xres[0:ST, 0, :], xs_d[q * ST:(q + 1) * ST, :], xD[0], writes=[xB[0]])
            norm_to_hT(0, ST, 1, ST)
            attention_layer_s(q)
            out_proj(wo_b, "o")
            norm_to_hT(1, ST, 1, ST)
            ffn(0)
            arena_fence()
            norm_to_hT(2, ST, 1, ST)
            hgrn_layer()
            out_proj(wo2_b, "o2")
            norm_to_hT(3, ST, 1, ST)
            arena_fence()
            ffn(1)
            arena_fence()
            final_store(ys_o, q * ST)
            P.dma(POOL, shgs_o[q].rearrange("h d v -> d h v"), Sst[:], SstD, reads=[SstB])
            for l in range(2):
                P.dma(POOL, scvs_o[l, q], convst[:, l, :, :].rearrange("p f k -> p (f k)"), cvD, reads=[cvB])

    allq = [outD, cvD, SstD, ckD] + slotD + slotDP
    for ds in allq:
        if ds.total > 0:
            nc.gpsimd.wait_ge(ds.h, ds.total)
    es.close()
    return nc


_CACHE = {}


def _prep_shared(inp):
    sh = {}
    sh["w_qkv"] = np.ascontiguousarray(inp["attn_w_qkv"][0])
    sh["w_o"] = np.ascontiguousarray(inp["attn_w_o"][0])
    sh["w_qfig"] = np.ascontiguousarray(inp["rec_w_qfig"][0])
    sh["w_o2"] = np.ascontiguousarray(inp["rec_w_o"][0])
    sh["w_up"] = np.ascontiguousarray(inp["ffn_w_up"])
    sh["w_dn"] = np.ascontiguousarray(inp["ffn_w_down"])
    norms = np.stack([inp["mixer_norm"][0], inp["ffn_norm"][0], inp["mixer_norm"][1], inp["ffn_norm"][1]])
    sh["gcols"] = np.ascontiguousarray(norms.reshape(4, KC, 128).transpose(2, 0, 1).reshape(128, 4 * KC))
    sh["fnorm"] = np.ascontiguousarray(inp["final_norm"])
    sh["subln"] = np.ascontiguousarray(inp["attn_subln"][0])
    sh["lamv"] = np.ascontiguousarray(np.concatenate([inp["attn_lambda_q1"][0], inp["attn_lambda_k1"][0],
                                                      inp["attn_lambda_q2"][0], inp["attn_lambda_k2"][0]]))
    sh["rlb"] = np.ascontiguousarray(inp["rec_lower_bounds"].reshape(2, RH, 128).transpose(2, 0, 1).reshape(128, 2 * RH))
    sh["onorm"] = np.ascontiguousarray(inp["rec_out_norm"][0])
    cw = np.concatenate([inp["ffn_conv_w"], inp["ffn_conv_b"][:, None, :]], axis=1)
    sh["convp"] = np.ascontiguousarray(cw.reshape(2, 4, FC, 128).transpose(3, 0, 2, 1).reshape(128, 2 * FC * 4))
    sh.update({k: v for k, v in host_consts().items() if k in ("ident", "tabs", "cpos", "tri", "triu")})
    return sh


def _conv_state_from(scv):
    return np.ascontiguousarray(scv.reshape(2, 128, FC, 2).transpose(0, 3, 2, 1).reshape(2, 2, DFF))


def kernel(**inp):
    dbg = inp.pop("_dbg", None)
    inp = {k: np.asarray(v) for k, v in inp.items()}
    NT_, do_l1, do_sample, stage, ncores, core0 = 8, True, DO_SAMPLE, 9, NCORES, 0
    if dbg is not None:
        NT_, do_l1, do_sample, stage, ncores, core0 = dbg
    key = (NT_, do_l1, do_sample, stage)
    if key not in _CACHE:
        _CACHE[key] = build_program(NT=NT_, do_l1=do_l1, do_sample=do_sample, stage=stage)
    nc = _CACHE[key]
    sh = _prep_shared(inp)
    in_maps = []
    for c in range(ncores):
        m = dict(sh)
        m["xp"] = np.ascontiguousarray(inp["x_prompt"][c % 4])
        if do_sample:
            s0 = c * SB
            m["xs"] = np.ascontiguousarray(inp["x_sample"][s0:s0 + SB].reshape(SB * ST, D))
            m["ck"] = np.ascontiguousarray(inp["cache_k"][0, s0:s0 + SB].reshape(SB, PAST, D))
            m["cv"] = np.ascontiguousarray(inp["cache_v"][0, s0:s0 + SB].reshape(SB, PAST, D))
            m["shg_in"] = np.ascontiguousarray(inp["state_hgrn"][0, s0:s0 + SB])
            sc_ = inp["state_conv"][:, s0:s0 + SB].reshape(2, SB, 2, FC, 128)
            m["scv_in"] = np.ascontiguousarray(sc_.transpose(0, 1, 4, 3, 2).reshape(2, SB, 128, FC * 2))
            hc = host_consts()
            m["stab_c"] = hc["stab_c"]
            m["stab_n"] = hc["stab_n"]
        in_maps.append(m)
    res = run_bass_kernel_spmd(nc, in_maps, core_ids=list(range(core0, core0 + ncores)))
    if dbg is not None:
        return res
    R_ = res.results
    B = 4
    y_p = np.stack([R_[b]["y"] for b in range(B)]).astype(np.float32)
    k_p = np.stack([R_[b]["kout"] for b in range(B)]).reshape(1, B, T, NH, 2, 128).astype(np.float32)
    v_p = np.stack([R_[b]["vout"] for b in range(B)]).reshape(1, B, T, NH, 256).astype(np.float32)
    sh_p = np.stack([R_[b]["shg"] for b in range(B)]).reshape(1, B, RH, 128, 128).astype(np.float32)
    sc_p = np.stack([_conv_state_from(R_[b]["scv"]) for b in range(B)], axis=1).astype(np.float32)
    NS = NCORES * SB
    if do_sample:
        y_s = np.concatenate([R_[c]["ys"].reshape(SB, ST, D) for c in range(NCORES)]).astype(np.float32)
        k_s = np.concatenate([R_[c]["ksout"].reshape(SB, ST, D) for c in range(NCORES)]).reshape(1, NS, ST, NH, 2, 128)
        v_s = np.concatenate([R_[c]["vsout"].reshape(SB, ST, D) for c in range(NCORES)]).reshape(1, NS, ST, NH, 256)
        sh_s = np.concatenate([R_[c]["shgs"] for c in range(NCORES)]).reshape(1, NS, RH, 128, 128)
        sc_s = np.concatenate([np.stack([_conv_state_from(R_[c]["scvs"][:, q]) for q in range(SB)], axis=1)
                               for c in range(NCORES)], axis=1)
    else:
        y_s = np.zeros((NS, ST, D), np.float32)
        k_s = np.zeros((1, NS, ST, NH, 2, 128), np.float32)
        v_s = np.zeros((1, NS, ST, NH, 256), np.float32)
        sh_s = np.zeros((1, NS, RH, 128, 128), np.float32)
        sc_s = np.zeros((2, NS, 2, DFF), np.float32)
    return (y_p, y_s, k_p, v_p, np.ascontiguousarray(k_s, dtype=np.float32), np.ascontiguousarray(v_s, dtype=np.float32),
            sh_p, np.ascontiguousarray(sh_s, dtype=np.float32), sc_p, np.ascontiguousarray(sc_s, dtype=np.float32))
```

```python
import numpy as np
from contextlib import ExitStack
import concourse.bass as bass
import concourse.mybir as mybir
from concourse.bass_utils import run_bass_kernel_spmd

F32 = mybir.dt.float32
BF16 = mybir.dt.bfloat16
AF = mybir.ActivationFunctionType
ALU = mybir.AluOpType
AX = mybir.AxisListType

D = 2048
NH = 8
T = 4096
TT = 512
NBLK = 4
KC = 16
DFF = 5632
FC = 44
RH = 16
EPS = 1e-6
SCALE = 128 ** -0.5
LAM_INIT0 = 0.8 - 0.6 * float(np.exp(-0.3 * 0))
NCORES = 8
SB = 2
ST = 16
PAST = 4096
DO_SAMPLE = True
VW = 264


class Buf:
    __slots__ = ("name", "w", "r", "excl")

    def __init__(self, name, excl=False):
        self.name = name
        self.w = {}
        self.r = {}
        self.excl = excl


class Group:
    def __init__(self, parts):
        self.parts = parts


def _flat(bs):
    out = []
    for b in bs:
        if isinstance(b, Group):
            out.extend(b.parts)
        else:
            out.append(b)
    return out


class DSem:
    def __init__(self, h):
        self.h = h
        self.total = 0


class Eng:
    def __init__(self, e, sem, name):
        self.e = e
        self.sem = sem
        self.cnt = 0
        self.waited = {}
        self.name = name


class Prog:
    def __init__(self, nc, es):
        self.nc = nc
        self.es = es
        self.nsem = 0

    def sem(self, name):
        self.nsem += 1
        return self.es.enter_context(self.nc.semaphore(name))

    def dsem(self, name):
        return DSem(self.sem(name))

    def wait(self, eng, evs, skip=None):
        for ev in evs:
            if ev[0] == "e":
                src, c = ev[1], ev[2]
                if src is eng and eng.name == "pe":
                    continue
                key = id(src)
                if eng.waited.get(key, 0) >= c:
                    continue
                eng.e.wait_ge(src.sem, c)
                eng.waited[key] = c
            else:
                ds = ev[1]
                if ds is skip:
                    continue
                c = ds.total
                key = id(ds)
                if eng.waited.get(key, 0) >= c:
                    continue
                eng.e.wait_ge(ds.h, c)
                eng.waited[key] = c

    def _deps(self, reads, writes):
        evs = []
        for b in reads:
            evs.extend(b.w.values())
            if b.excl:
                evs.extend(b.r.values())
        for b in writes:
            evs.extend(b.w.values())
            evs.extend(b.r.values())
        return evs

    def op(self, eng, fn, reads=(), writes=(), sig=True):
        reads, writes = _flat(reads), _flat(writes)
        self.wait(eng, self._deps(reads, writes))
        inst = fn()
        if sig:
            eng.cnt += 1
            inst.then_inc(eng.sem, 1)
            ev = ("e", eng, eng.cnt)
        else:
            ev = ("e", eng, eng.cnt + 1)
        k = id(eng)
        for b in reads:
            b.r[k] = ev
        for b in writes:
            b.w = {k: ev}
            b.r = {}
        return inst

    def dma(self, q, out, in_, ds, reads=(), writes=(), **kw):
        reads, writes = _flat(reads), _flat(writes)
        self.wait(q, self._deps(reads, writes), skip=ds)
        inst = q.e.dma_start(out=out, in_=in_, **kw)
        ds.total += 16
        inst.then_inc(ds.h, 16)
        ev = ("d", ds)
        k = id(ds)
        for b in reads:
            b.r[k] = ev
        for b in writes:
            b.w = {k: ev}
            b.r = {}
        return inst


def host_consts():
    c = {}
    c["ident"] = np.eye(128, dtype=np.float32)
    p = np.arange(128)[:, None]
    xx = np.arange(1024)[None, :] - 384
    allowed = (xx // 64) >= (p // 64)
    tabs = np.where(allowed, np.abs(xx - p), 1.0e6).astype(np.float32)
    c["tabs"] = tabs
    c["cpos"] = np.tile((np.arange(32, dtype=np.float32) * 128.0)[None, :], (128, 1)).astype(np.float32)
    s = np.arange(64)
    tri = (s[:, None] <= s[None, :]).astype(np.float32)
    c["tri"] = np.concatenate([tri, tri], axis=0)
    s2 = np.arange(128)
    c["triu"] = (s2[:, None] <= s2[None, :]).astype(np.float32)
    i = np.arange(ST)[None, :]
    c["stab_c"] = (i - p + 128).astype(np.float32)
    c["stab_n"] = np.where(p < ST, np.abs(i - p), 0).astype(np.float32)
    return c


def build_program(NT=8, do_l1=True, do_sample=True, dbg=False, stage=9):
    nc = bass.Bass("TRN2", target_bir_lowering=False)
    es = ExitStack()
    P = Prog(nc, es)
    E = es.enter_context

    def din(name, shape, dt=F32):
        return nc.dram_tensor(name, list(shape), dt, kind="ExternalInput").ap()

    def dout(name, shape, dt=F32):
        return nc.dram_tensor(name, list(shape), dt, kind="ExternalOutput").ap()

    def dscr(name, shape, dt):
        return nc.dram_tensor(name, list(shape), dt).ap()

    xp = din("xp", [T, D])
    w_qkv = din("w_qkv", [D, 3 * D])
    w_o = din("w_o", [D, D])
    w_qfig = din("w_qfig", [D, 4 * D])
    w_o2 = din("w_o2", [D, D])
    w_up = din("w_up", [2, D, 2 * DFF])
    w_dn = din("w_dn", [2, DFF, D])
    gcols_d = din("gcols", [128, 4 * KC])
    fnorm_d = din("fnorm", [D])
    subln_d = din("subln", [256])
    lamv_d = din("lamv", [4 * 128])
    rlb_d = din("rlb", [128, 2 * RH])
    onorm_d = din("onorm", [128])
    convp_d = din("convp", [128, 2 * FC * 4])
    ident_d = din("ident", [128, 128])
    tabs_d = din("tabs", [128, 1024])
    cpos_d = din("cpos", [128, 32])
    tri_d = din("tri", [128, 64])
    triu_d = din("triu", [128, 128])
    if do_sample:
        xs_d = din("xs", [SB * ST, D])
        ck_d = din("ck", [SB, PAST, D])
        cv_d = din("cv", [SB, PAST, D])
        shg_in = din("shg_in", [SB, RH, 128, 128])
        scv_in = din("scv_in", [2, SB, 128, FC * 2])
        stabc_d = din("stab_c", [128, ST])
        stabn_d = din("stab_n", [128, ST])

    y_o = dout("y", [T, D])
    k_o = dout("kout", [T, D])
    v_o = dout("vout", [T, D])
    shg_o = dout("shg", [RH, 128, 128])
    scv_o = dout("scv", [2, 128, FC * 2])
    if do_sample:
        ys_o = dout("ys", [SB * ST, D])
        ks_o = dout("ksout", [SB * ST, D])
        vs_o = dout("vsout", [SB * ST, D])
        shgs_o = dout("shgs", [SB, RH, 128, 128])
        scvs_o = dout("scvs", [2, SB, 128, FC * 2])

    wqk_b = dscr("wqk_b", [NH, 128, KC, 512], BF16)
    wv_b = dscr("wv_b", [NH, 128, KC, 256], BF16)
    wo_b = dscr("wo_b", [4, 128, KC, 512], BF16)
    wqf_b = dscr("wqf_b", [RH, 2, 128, KC, 256], BF16)
    wo2_b = dscr("wo2_b", [4, 128, KC, 512], BF16)
    wup_b = dscr("wup_b", [2, 44, 128, KC, 256], BF16)
    wdn_b = dscr("wdn_b", [2, 4, 4, 128, 11, 512], BF16)
    ktS = dscr("ktS", [2 * NH, 128, T], BF16)
    ktSs = dscr("ktSs", [SB, 2 * NH, 128, PAST], BF16)
    vS = dscr("vS", [NH, T, VW], BF16)

    PE = Eng(nc.tensor, P.sem("pe"), "pe")
    ACT = Eng(nc.scalar, P.sem("act"), "act")
    DVE = Eng(nc.vector, P.sem("dve"), "dve")
    POOL = Eng(nc.gpsimd, P.sem("pool"), "pool")
    SPQ = Eng(nc.sync, None, "sp")
    V = nc.vector
    A = nc.scalar
    TE = nc.tensor

    def sb(name, shape, dt):
        return E(nc.sbuf_tensor("s_" + name, list(shape), dt))

    xres = sb("xres", [128, NBLK, D], F32)
    hT = sb("hT", [128, KC, TT], BF16)
    oT = sb("oT", [128, KC, TT], BF16)
    hb = sb("hb", [128, D], BF16)
    NWS = 4
    HSW = KC * 256
    wsl = sb("wsl", [128, NWS * HSW], BF16)
    KTW = 4224
    ARENA = 4 * KTW + 33 * VW
    arena = sb("arena", [128, ARENA], BF16)
    NSL = 10
    SLW = 528
    slots = sb("slots", [128, NSL, SLW], F32)
    Sst = sb("Sst", [128, RH, 128], F32)
    tabs = sb("tabs", [128, 1024], F32)
    biasH = sb("biasH", [128, 1024], F32)
    cpos = sb("cpos", [128, 32], F32)
    cH = sb("cH", [128, 32], F32)
    identf = sb("identf", [128, 128], F32)
    ident = sb("ident", [128, 128], BF16)
    gcols = sb("gcols", [128, 4 * KC], F32)
    sublnb = sb("sublnb", [128, 256], F32)
    lamv = sb("lamv", [128, 4 * 128], F32)
    lamt = sb("lamt", [128, 8], F32)
    rlb = sb("rlb", [128, 2 * RH], F32)
    lbt = sb("lbt", [128, 4 * RH], F32)
    onormb = sb("onormb", [128, 128], F32)
    convp = sb("convp", [128, 2 * FC * 4], F32)
    convst = sb("convst", [128, 2, FC, 2], F32)
    tri = sb("tri", [128, 64], F32)
    triu = sb("triu", [128, 128], F32)
    hsm = sb("hsm", [128, 2, 32], F32)
    hs22 = sb("hs2", [128, 2, 8], F32)
    sbf2 = sb("sbf", [128, 2, 128], BF16)
    ATt2 = sb("ATt", [128, 2, 128], BF16)
    on322 = sb("on32", [128, 2, 128], F32)
    onb2 = sb("onb", [128, 2, 128], BF16)
    ktok2 = sb("ktok", [128, 2, 128], BF16)
    hjunk2 = sb("hjunk", [128, 2, 128], BF16)
    fence_s = sb("fence_s", [128, 2], F32)
    stabs = sb("stabs", [128, 2 * ST], F32)
    sbias = sb("sbias", [128, 2 * ST], F32)
    small = sb("small", [128, 64], F32)
    o0buf = sb("o0buf", [128, NBLK, 256], F32)
    qtbuf = sb("qtbuf", [128, 2, 2, TT], BF16)
    qtB = [Buf("qt0"), Buf("qt1")]
    ps = E(nc.psum_tensor("ps", [128, 4096], F32))

    def bank(i):
        return ps[:, i * 512:(i + 1) * 512]

    pb = [Buf("pb%d" % i, excl=True) for i in range(8)]

    xB = [Buf("x%d" % i) for i in range(NBLK)]
    xD = [P.dsem("xd%d" % i) for i in range(NBLK)]
    hTB = Buf("hT")
    oTB = Buf("oT")
    hbB = Buf("hb")
    smallB = Buf("small")
    constB = Buf("const")
    constD = P.dsem("constd")
    wslB = [Buf("ws%d" % i) for i in range(NWS)]
    wslD = [P.dsem("wsd%d" % i) for i in range(NWS)]
    ktB = [Buf("kt0"), Buf("kt1")]
    ktD = [P.dsem("ktd0"), P.dsem("ktd1")]
    vbB = Buf("vb")
    vbD = P.dsem("vbd")
    uTB = [Buf("uT%d" % g) for g in range(11)]
    slotB = [Buf("sl%d" % i) for i in range(NSL)]
    slotD = [P.dsem("sld%d" % i) for i in range(NSL)]
    slotDP = [P.dsem("slp%d" % i) for i in range(NSL)]
    poolD = {id(slotB[i]): slotDP[i] for i in range(NSL)}
    SstHB = [Buf("Sst%d" % h_) for h_ in range(RH)]
    SstD = P.dsem("Sstd")
    biasB = Buf("biasH")
    o0B = Buf("o0")
    cvB = Buf("convst")
    cvD = P.dsem("cvd")
    wB = {n: Buf("w_" + n) for n in ["qkv", "o", "qf", "o2", "up0", "up1", "dn0", "dn1"]}
    wD = {n: P.dsem("wd_" + n) for n in wB}
    ktSB = Buf("ktS")
    vSB = Buf("vS")
    outD = P.dsem("outd")

    state = {"slot": 0, "ws": 0}

    def get_slot():
        i = state["slot"]
        state["slot"] = (i + 1) % NSL
        return slots[:, i, :], slotB[i], slotD[i]

    def ws_load(src_ap, ncols, kc, wbuf):
        n = kc * ncols
        p = state["ws"]
        if n > HSW:
            p = (p + (p % 2)) % NWS
            bufs = [wslB[p], wslB[p + 1]]
            state["ws"] = (p + 2) % NWS
        else:
            bufs = [wslB[p]]
            state["ws"] = (p + 1) % NWS
        view = wsl[:, p * HSW:p * HSW + n].rearrange("p (c n) -> p c n", c=kc)
        P.dma(SPQ, view, src_ap, wslD[p], reads=[wbuf], writes=bufs)
        return view, Group(bufs)

    KTb = [arena[:, i * 2 * KTW:(i + 1) * 2 * KTW].rearrange("p (c n) -> p c n", c=2) for i in range(2)]
    Vb = arena[:, 4 * KTW:4 * KTW + 33 * VW].rearrange("p (b e) -> p b e", e=VW)
    uT = arena[:, 0:FC * 512].rearrange("p (c n) -> p c n", c=FC)

    def cload(dst, src):
        P.dma(SPQ, dst, src, constD, writes=[constB])

    cload(identf[:], ident_d)
    cload(tabs[:], tabs_d)
    cload(cpos[:], cpos_d)
    cload(gcols[:], gcols_d)
    cload(sublnb[:], subln_d.partition_broadcast(128))
    cload(lamv[:], lamv_d.partition_broadcast(128))
    cload(rlb[:], rlb_d)
    cload(onormb[:], onorm_d.partition_broadcast(128))
    cload(convp[:], convp_d)
    cload(tri[:], tri_d)
    cload(triu[:], triu_d)
    if do_sample:
        cload(stabs[:, 0:ST], stabc_d)
        cload(stabs[:, ST:2 * ST], stabn_d)
    cb = [constB]
    P.op(DVE, lambda: V.tensor_copy(out=ident[:], in_=identf[:]), reads=cb, writes=cb)
    P.op(DVE, lambda: V.memset(convst[:], 0.0), writes=[cvB])
    P.op(DVE, lambda: V.memset(Sst[:], 0.0), writes=SstHB)
    sl, sB_, _ = get_slot()
    P.op(DVE, lambda: V.tensor_tensor(out=sl[:, 0:128], in0=lamv[:, 0:128], in1=lamv[:, 128:256], op=ALU.mult),
         reads=cb, writes=[sB_])
    P.op(DVE, lambda: V.reduce_sum(out=lamt[:, 0:1], in_=sl[:, 0:128], axis=AX.X), reads=[sB_], writes=cb)
    P.op(DVE, lambda: V.tensor_tensor(out=sl[:, 0:128], in0=lamv[:, 256:384], in1=lamv[:, 384:512], op=ALU.mult),
         reads=cb, writes=[sB_])
    P.op(DVE, lambda: V.reduce_sum(out=lamt[:, 1:2], in_=sl[:, 0:128], axis=AX.X), reads=[sB_], writes=cb)
    P.op(ACT, lambda: A.activation(out=lamt[:, 2:4], in_=lamt[:, 0:2], func=AF.Exp), reads=cb, writes=cb)
    P.op(DVE, lambda: V.tensor_tensor(out=lamt[:, 4:5], in0=lamt[:, 2:3], in1=lamt[:, 3:4], op=ALU.subtract),
         reads=cb, writes=cb)
    P.op(DVE, lambda: V.tensor_scalar(out=lamt[:, 4:5], in0=lamt[:, 4:5], scalar1=LAM_INIT0, scalar2=None, op0=ALU.add),
         reads=cb, writes=cb)
    P.op(DVE, lambda: V.tensor_scalar(out=lamt[:, 5:6], in0=lamt[:, 4:5], scalar1=-1.0, scalar2=None, op0=ALU.mult),
         reads=cb, writes=cb)
    P.op(DVE, lambda: V.tensor_scalar(out=sublnb[:], in0=sublnb[:], scalar1=1.0 - LAM_INIT0, scalar2=None, op0=ALU.mult),
         reads=cb, writes=cb)
    P.op(ACT, lambda: A.activation(out=lbt[:, 0:32], in_=rlb[:, 0:32], func=AF.Exp), reads=cb, writes=cb)
    P.op(DVE, lambda: V.tensor_tensor(out=lbt[:, 32:48], in0=lbt[:, 0:16], in1=lbt[:, 16:32], op=ALU.add),
         reads=cb, writes=cb)
    P.op(DVE, lambda: V.reciprocal(out=lbt[:, 32:48], in_=lbt[:, 32:48]), reads=cb, writes=cb)
    P.op(DVE, lambda: V.tensor_tensor(out=lbt[:, 32:48], in0=lbt[:, 32:48], in1=lbt[:, 16:32], op=ALU.mult),
         reads=cb, writes=cb)
    P.op(DVE, lambda: V.tensor_scalar(out=lbt[:, 48:64], in0=lbt[:, 32:48], scalar1=-1.0, scalar2=1.0,
                                      op0=ALU.mult, op1=ALU.add), reads=cb, writes=cb)

    def cast(dst, src, name):
        P.dma(POOL, dst, src, wD[name], writes=[wB[name]])

    def rk(ap):
        return ap.rearrange("(c p) n -> p c n", p=128)

    for h in range(NH):
        cast(wqk_b[h, :, :, 0:256], rk(w_qkv[:, h * 256:(h + 1) * 256]), "qkv")
        cast(wqk_b[h, :, :, 256:512], rk(w_qkv[:, D + h * 256:D + (h + 1) * 256]), "qkv")
        cast(wv_b[h], rk(w_qkv[:, 2 * D + h * 256:2 * D + (h + 1) * 256]), "qkv")
    for cp in range(4):
        cast(wo_b[cp], rk(w_o[:, cp * 512:(cp + 1) * 512]), "o")
    for l in range(2 if do_l1 else 1):
        for g in range(22):
            cast(wup_b[l, 2 * g], rk(w_up[l, :, g * 256:(g + 1) * 256]), "up%d" % l)
            cast(wup_b[l, 2 * g + 1], rk(w_up[l, :, DFF + g * 256:DFF + (g + 1) * 256]), "up%d" % l)
        for cp in range(4):
            for rg in range(4):
                cast(wdn_b[l, cp, rg], rk(w_dn[l, rg * 1408:(rg + 1) * 1408, cp * 512:(cp + 1) * 512]), "dn%d" % l)
        if l == 0 and do_l1:
            for h in range(RH):
                for part in range(4):
                    cast(wqf_b[h, part // 2, :, :, (part % 2) * 128:(part % 2 + 1) * 128],
                         rk(w_qfig[:, part * D + h * 128:part * D + (h + 1) * 128]), "qf")
            for cp in range(4):
                cast(wo2_b[cp], rk(w_o2[:, cp * 512:(cp + 1) * 512]), "o2")

    def mm_group(out_ap, pairs, out_bufs, read_bufs):
        n = len(pairs)
        for i, (l, r) in enumerate(pairs):
            P.op(PE, lambda: TE.matmul(out_ap, lhsT=l, rhs=r, start=(i == 0), stop=(i == n - 1)),
                 reads=read_bufs, writes=out_bufs, sig=(i == n - 1))

    pT = ps[:, 4 * 512:6 * 512].bitcast(BF16).rearrange("p (c n) -> p c n", c=KC)

    def norm_to_hT(gi, npart=128, nblk=NBLK, ncol=128):
        g_b = gcols[:, gi * KC:(gi + 1) * KC].unsqueeze(2).to_broadcast([128, KC, ncol])
        for blk in range(nblk):
            xs_ = xres[0:npart, blk, :]
            P.op(ACT, lambda: A.activation(out=hb[0:npart, :], in_=xs_, func=AF.Square,
                                           accum_out=small[0:npart, blk:blk + 1]),
                 reads=[xB[blk]], writes=[hbB, smallB])
            P.op(ACT, lambda: A.activation(out=small[0:npart, 8 + blk:9 + blk], in_=small[0:npart, blk:blk + 1],
                                           func=AF.Ln, scale=1.0 / D, bias=epsc[0:npart, :]),
                 reads=[smallB, constB], writes=[smallB])
            P.op(ACT, lambda: A.activation(out=small[0:npart, 16 + blk:17 + blk], in_=small[0:npart, 8 + blk:9 + blk],
                                           func=AF.Exp, scale=-0.5), reads=[smallB], writes=[smallB])
            P.op(DVE, lambda: V.tensor_scalar(out=hb[0:npart, :], in0=xs_, scalar1=small[0:npart, 16 + blk:17 + blk],
                                              scalar2=None, op0=ALU.mult),
                 reads=[xB[blk], smallB], writes=[hbB])
            for c in range(KC):
                P.op(PE, lambda: TE.transpose(out=pT[:, c, 0:npart], in_=hb[0:npart, c * 128:(c + 1) * 128],
                                              identity=ident[0:npart, 0:npart]),
                     reads=[hbB, constB], writes=[pb[4], pb[5]], sig=(c == KC - 1))
            P.op(DVE, lambda: V.tensor_tensor(out=hT[:, :, blk * ncol:(blk + 1) * ncol], in0=pT[:, :, 0:ncol],
                                              in1=g_b, op=ALU.mult),
                 reads=[pb[4], pb[5], constB], writes=[hTB])

    epsc = sb("epsc", [128, 1], F32)
    P.op(DVE, lambda: V.memset(epsc[:], EPS), writes=[constB])

    def load_x(j):
        for blk in range(NBLK):
            r0 = j * TT + blk * 128
            P.dma(SPQ, xres[:, blk, :], xp[r0:r0 + 128, :], xD[blk], writes=[xB[blk]])

    def attn_proj(j, h, wqk, wqkB, wv, wvB):
        QT = qtbuf[:, h % 2, :, :]
        qB = qtB[h % 2]
        ks_ap, kB, kD = get_slot()
        KT = ks_ap.bitcast(BF16)[:, 0:1024].rearrange("p (c n) -> p c n", c=2)
        vs_ap, vB, vD = get_slot()
        VH = vs_ap.bitcast(BF16)[:, 0:4 * VW].rearrange("p (b e) -> p b e", e=VW)
        hr = [hTB]
        for c in range(2):
            mm_group(bank(6), [(wqk[:, kc, c * 128:(c + 1) * 128], hT[:, kc, :]) for kc in range(KC)],
                     [pb[6]], hr + [wqkB])
            P.op(ACT, lambda: A.activation(out=QT[:, c, :], in_=bank(6), func=AF.Copy, scale=SCALE),
                 reads=[pb[6]], writes=[qB])
            mm_group(bank(7), [(wqk[:, kc, 256 + c * 128:256 + (c + 1) * 128], hT[:, kc, :]) for kc in range(KC)],
                     [pb[7]], hr + [wqkB])
            P.op(ACT, lambda: A.activation(out=KT[:, c, :], in_=bank(7), func=AF.Copy),
                 reads=[pb[7]], writes=[kB])
        if stage < 3.12:
            return QT, qB
        P.dma(SPQ, ktS[2 * h:2 * h + 2, :, j * TT:(j + 1) * TT].rearrange("c p n -> p c n"), KT, kD,
              reads=[kB], writes=[ktSB])
        if stage < 3.13:
            return QT, qB
        P.op(DVE, lambda: V.memset(VH[:, :, 256:VW], 1.0), writes=[vB])
        for half in range(2):
            so_ap, soB, soD = get_slot()
            for bb in range(2):
                blk = half * 2 + bb
                mm_group(bank(half)[:, bb * 256:(bb + 1) * 256],
                         [(hT[:, kc, blk * 128:(blk + 1) * 128], wqk[:, kc, 256:512]) for kc in range(KC)],
                         [pb[half]], hr + [wqkB])
            P.op(ACT, lambda: A.activation(out=so_ap[:, 0:512], in_=bank(half), func=AF.Copy),
                 reads=[pb[half]], writes=[soB])
            r0 = j * TT + half * 256
            P.dma(POOL, k_o[r0:r0 + 256, h * 256:(h + 1) * 256].rearrange("(b p) e -> p b e", p=128),
                  so_ap[:, 0:512].rearrange("p (b e) -> p b e", b=2), poolD[id(soB)], reads=[soB])
        if stage < 3.14:
            return QT, qB
        for half in range(2):
            so_ap, soB, soD = get_slot()
            for bb in range(2):
                blk = half * 2 + bb
                mm_group(bank(2 + half)[:, bb * 256:(bb + 1) * 256],
                         [(hT[:, kc, blk * 128:(blk + 1) * 128], wv[:, kc, :]) for kc in range(KC)],
                         [pb[2 + half]], hr + [wvB])
            P.op(ACT, lambda: A.activation(out=so_ap[:, 0:512], in_=bank(2 + half), func=AF.Copy),
                 reads=[pb[2 + half]], writes=[soB])
            P.op(DVE, lambda: V.tensor_copy(out=VH[:, 2 * half:2 * half + 2, 0:256],
                                            in_=so_ap[:, 0:512].rearrange("p (b e) -> p b e", b=2)),
                 reads=[soB], writes=[vB])
            r0 = j * TT + half * 256
            P.dma(POOL, v_o[r0:r0 + 256, h * 256:(h + 1) * 256].rearrange("(b p) e -> p b e", p=128),
                  so_ap[:, 0:512].rearrange("p (b e) -> p b e", b=2), poolD[id(soB)], reads=[soB])
        if stage < 3.15:
            return QT, qB
        P.dma(SPQ, vS[h, j * TT:(j + 1) * TT, :].rearrange("(b p) e -> p b e", p=128), VH, vD,
              reads=[vB], writes=[vSB])
        return QT, qB

    def attn_core(h, QT, qB, nkb, npast, par, nq, qsubs, diag_fn, slope, on_done, past_bias=None, kn_last=128):
        KTv = KTb[par]
        for c in range(2):
            pend = []
            LOOK = 2

            def qk(kb):
                wb_i = 4 + (kb % 3)
                kn = kn_last if kb == nkb - 1 else 128
                P.op(PE, lambda: TE.matmul(bank(wb_i)[0:kn, 0:nq], lhsT=KTv[:, c, kb * 128:kb * 128 + kn],
                                           rhs=QT[:, c, 0:nq], start=True, stop=True),
                     reads=[ktB[par], qB], writes=[pb[wb_i]])
                t_ap, tB, _ = get_slot()
                if kb < npast:
                    bias_ap = past_bias if past_bias is not None else biasH[:, 512:512 + nq]
                    ccol = cH[:, (npast - kb - 1):(npast - kb)]
                else:
                    bias_ap, ccol = diag_fn(kb - npast)
                P.op(DVE, lambda: V.tensor_tensor(out=t_ap[0:kn, 0:nq], in0=bank(wb_i)[0:kn, 0:nq], in1=bias_ap[0:kn, :],
                                                  op=ALU.add),
                     reads=[pb[wb_i], biasB], writes=[tB])
                p_ap, pB_, _ = get_slot()
                PT = p_ap.bitcast(BF16)
                P.op(ACT, lambda: A.activation(out=PT[0:kn, 0:nq], in_=t_ap[0:kn, 0:nq], func=AF.Exp, bias=ccol[0:kn, :]),
                     reads=[tB, biasB], writes=[pB_])
                return PT, pB_

            def pv(kb, PT, pB_):
                kn = kn_last if kb == nkb - 1 else 128
                for qi, (q0, qn) in enumerate(qsubs):
                    P.op(PE, lambda: TE.matmul(bank(qi)[0:qn, 0:257], lhsT=PT[0:kn, q0:q0 + qn], rhs=Vb[0:kn, kb, 0:257],
                                               start=(kb == 0), stop=(kb == nkb - 1)),
                         reads=[pB_, vbB], writes=[pb[qi]], sig=(qi == len(qsubs) - 1))

            for kb in range(nkb + LOOK):
                if kb < nkb:
                    pend.append(qk(kb))
                if kb >= LOOK:
                    PT, pB_ = pend[kb - LOOK]
                    pv(kb - LOOK, PT, pB_)
            nqs = len(qsubs)
            for qi, (q0, qn) in enumerate(qsubs):
                P.op(DVE, lambda: V.reciprocal(out=small[0:qn, 24 + qi:25 + qi], in_=bank(qi)[0:qn, 256:257]),
                     reads=[pb[qi]], writes=[smallB])
                if c == 0:
                    P.op(DVE, lambda: V.tensor_scalar(out=o0buf[0:qn, qi, :], in0=bank(qi)[0:qn, 0:256],
                                                      scalar1=small[0:qn, 24 + qi:25 + qi], scalar2=None, op0=ALU.mult),
                         reads=[pb[qi], smallB], writes=[o0B])
                else:
                    P.op(DVE, lambda: V.tensor_scalar(out=small[0:qn, 28 + qi:29 + qi], in0=small[0:qn, 24 + qi:25 + qi],
                                                      scalar1=lamt[0:qn, 5:6], scalar2=None, op0=ALU.mult),
                         reads=[smallB, constB], writes=[smallB])
                    P.op(DVE, lambda: V.scalar_tensor_tensor(out=o0buf[0:qn, qi, :], in0=bank(qi)[0:qn, 0:256],
                                                             scalar=small[0:qn, 28 + qi:29 + qi],
                                                             in1=o0buf[0:qn, qi, :], op0=ALU.mult, op1=ALU.add),
                         reads=[pb[qi], smallB], writes=[o0B])
        on_done()

    def attn_finish(h, qsubs, ncol):
        j_ap, jB, _ = get_slot()
        junk = j_ap.bitcast(BF16)
        on_ap, onB, _ = get_slot()
        ON = on_ap.bitcast(BF16)[:, 0:1024].rearrange("p (b e) -> p b e", b=4)
        for qi, (q0, qn) in enumerate(qsubs):
            P.op(ACT, lambda: A.activation(out=junk[0:qn, 0:256], in_=o0buf[0:qn, qi, :], func=AF.Square,
                                           accum_out=small[0:qn, 32 + qi:33 + qi]),
                 reads=[o0B], writes=[jB, smallB])
        nqs = len(qsubs)
        qn = qsubs[0][1]
        P.op(ACT, lambda: A.activation(out=small[0:qn, 36:36 + nqs], in_=small[0:qn, 32:32 + nqs], func=AF.Ln,
                                       scale=1.0 / 256, bias=epsc[0:qn, :]), reads=[smallB, constB], writes=[smallB])
        P.op(ACT, lambda: A.activation(out=small[0:qn, 40:40 + nqs], in_=small[0:qn, 36:36 + nqs], func=AF.Exp,
                                       scale=-0.5), reads=[smallB], writes=[smallB])
        for qi, (q0, qn) in enumerate(qsubs):
            P.op(DVE, lambda: V.scalar_tensor_tensor(out=ON[0:qn, qi, :], in0=o0buf[0:qn, qi, :],
                                                     scalar=small[0:qn, 40 + qi:41 + qi], in1=sublnb[0:qn, :],
                                                     op0=ALU.mult, op1=ALU.mult),
                 reads=[o0B, smallB, constB], writes=[onB])
        pO = bank(7).bitcast(BF16).rearrange("p (e n) -> p e n", e=2)
        for eh in range(2):
            for qi, (q0, qn) in enumerate(qsubs):
                last = (eh == 1 and qi == nqs - 1)
                P.op(PE, lambda: TE.transpose(out=pO[:, eh, qi * ncol:qi * ncol + qn],
                                              in_=ON[0:qn, qi, eh * 128:(eh + 1) * 128], identity=ident[0:qn, 0:qn]),
                     reads=[onB, constB], writes=[pb[7]], sig=last)
        ntok = nqs * ncol
        P.op(ACT, lambda: A.activation(out=oT[:, 2 * h:2 * h + 2, 0:ntok], in_=pO[:, :, 0:ntok], func=AF.Copy),
             reads=[pb[7]], writes=[oTB])

    def attention_layer(j):
        nkb = 4 * (j + 1)
        nk = nkb * 128
        qsubs = [(i * 128, 128) for i in range(4)]
        wq = [None] * NH
        wq[0] = (ws_load(wqk_b[0], 512, KC, wB["qkv"]), ws_load(wv_b[0], 256, KC, wB["qkv"]))
        for h in range(NH if stage >= 3.5 else 1):
            (wqk, wqkB), (wv, wvB) = wq[h]
            QT, qB = attn_proj(j, h, wqk, wqkB, wv, wvB)
            if stage < 3.2:
                continue
            par = h % 2
            first_arena = (h == 0)
            extra = uTB if first_arena else []
            P.dma(SPQ, KTb[par][:, :, 0:nk], ktS[2 * h:2 * h + 2, :, 0:nk].rearrange("c p n -> p c n"), ktD[par],
                  reads=[ktSB], writes=[ktB[par]] + (extra if h < 2 else []))
            P.dma(SPQ, Vb[:, 0:nkb, :], vS[h, 0:nk, :].rearrange("(b p) e -> p b e", p=128), vbD,
                  reads=[vSB], writes=[vbB] + extra)
            if h + 1 < NH:
                wq[h + 1] = (ws_load(wqk_b[h + 1], 512, KC, wB["qkv"]), ws_load(wv_b[h + 1], 256, KC, wB["qkv"]))
            if stage < 3.3:
                continue
            slope = 2.0 ** (-(h + 1))
            P.op(DVE, lambda: V.tensor_scalar(out=biasH[:], in0=tabs[:], scalar1=-slope, scalar2=None, op0=ALU.mult),
                 reads=[constB], writes=[biasB])
            P.op(DVE, lambda: V.tensor_scalar(out=cH[:], in0=cpos[:], scalar1=-slope, scalar2=None, op0=ALU.mult),
                 reads=[constB], writes=[biasB])

            def diag_fn(m):
                return biasH[:, 384 - 128 * m:384 - 128 * m + 512], cH[:, 0:1]

            attn_core(h, QT, qB, nkb, nkb - 4, par, TT, qsubs, diag_fn, slope, lambda: None)
            if stage < 3.4:
                continue
            attn_finish(h, qsubs, 128)

    def out_proj(w_scr, wname):
        nxt = ws_load(w_scr[0], 512, KC, wB[wname])
        for cp in range(4):
            wv_, wvB_ = nxt
            if cp + 1 < 4:
                nxt = ws_load(w_scr[cp + 1], 512, KC, wB[wname])
            CB = cfg["CB"]
            for blk in range(cfg["nblk"]):
                mm_group(bank(blk)[0:CB, :], [(oT[:, kc, blk * CB:(blk + 1) * CB], wv_[:, kc, :]) for kc in range(KC)],
                         [pb[blk]], [oTB, wvB_])
                xs_ = xres[0:CB, blk, cp * 512:(cp + 1) * 512]
                P.op(DVE, lambda: V.tensor_tensor(out=xs_, in0=bank(blk)[0:CB, :], in1=xs_, op=ALU.add),
                     reads=[pb[blk]], writes=[xB[blk]])

    def ffn(layer):
        TW, CB = cfg["TW"], cfg["CB"]
        wn_up = "up%d" % layer
        wn_dn = "dn%d" % layer
        cpv = convp[:, layer * FC * 4:(layer + 1) * FC * 4].rearrange("p (f k) -> p f k", k=4)
        seq = [wup_b[layer, i] for i in range(44)]
        loaded = [ws_load(seq[0], 256, KC, wB[wn_up]), ws_load(seq[1], 256, KC, wB[wn_up])]
        for g in range(22):
            (wg, wgB), (wvv, wvvB) = loaded[2 * g], loaded[2 * g + 1]
            if g + 1 < 22:
                loaded.append(ws_load(seq[2 * g + 2], 256, KC, wB[wn_up]))
                loaded.append(ws_load(seq[2 * g + 3], 256, KC, wB[wn_up]))
            for ch in range(2):
                fc = g * 2 + ch
                mm_group(bank(4 + (fc % 2))[:, 0:TW], [(wg[:, kc, ch * 128:(ch + 1) * 128], hT[:, kc, 0:TW]) for kc in range(KC)],
                         [pb[4 + (fc % 2)]], [hTB, wgB])
                g_ap, gB, _ = get_slot()
                t_ap, tB, _ = get_slot()
                first_w = [gB] + (([ktB[0], ktB[1], vbB]) if (g == 0 and ch == 0) else [])
                P.op(ACT, lambda: A.activation(out=g_ap[:, 2:2 + TW], in_=bank(4 + (fc % 2))[:, 0:TW], func=AF.Copy),
                     reads=[pb[4 + (fc % 2)]], writes=[gB])
                P.op(ACT, lambda: A.activation(out=g_ap[:, 0:2], in_=convst[:, layer, fc, :], func=AF.Copy),
                     reads=[cvB], writes=[gB])
                P.op(ACT, lambda: A.activation(out=convst[:, layer, fc, :], in_=g_ap[:, TW:TW + 2], func=AF.Copy),
                     reads=[gB], writes=[cvB])
                P.op(DVE, lambda: V.tensor_scalar(out=t_ap[:, 0:TW], in0=g_ap[:, 0:TW], scalar1=cpv[:, fc, 0:1],
                                                  scalar2=cpv[:, fc, 3:4], op0=ALU.mult, op1=ALU.add),
                     reads=[gB, constB], writes=[tB])
                P.op(DVE, lambda: V.scalar_tensor_tensor(out=t_ap[:, 0:TW], in0=g_ap[:, 1:1 + TW], scalar=cpv[:, fc, 1:2],
                                                         in1=t_ap[:, 0:TW], op0=ALU.mult, op1=ALU.add),
                     reads=[gB, constB], writes=[tB])
                P.op(DVE, lambda: V.scalar_tensor_tensor(out=t_ap[:, 0:TW], in0=g_ap[:, 2:2 + TW], scalar=cpv[:, fc, 2:3],
                                                         in1=t_ap[:, 0:TW], op0=ALU.mult, op1=ALU.add),
                     reads=[gB, constB], writes=[tB])
                P.op(ACT, lambda: A.activation(out=t_ap[:, 0:TW], in_=t_ap[:, 0:TW], func=AF.Silu),
                     reads=[tB], writes=[tB])
                mm_group(bank(6 + (fc % 2))[:, 0:TW], [(wvv[:, kc, ch * 128:(ch + 1) * 128], hT[:, kc, 0:TW]) for kc in range(KC)],
                         [pb[6 + (fc % 2)]], [hTB, wvvB])
                uw = [uTB[fc // 4]] + (([ktB[0], ktB[1], vbB]) if (g == 0 and ch == 0) else [])
                P.op(DVE, lambda: V.tensor_tensor(out=uT[:, fc, 0:TW], in0=t_ap[:, 0:TW], in1=bank(6 + (fc % 2))[:, 0:TW],
                                                  op=ALU.mult),
                     reads=[tB, pb[6 + (fc % 2)]], writes=uw)
        dseq = [(cp, rg) for cp in range(4) for rg in range(4)]
        nxt = ws_load(wdn_b[layer, 0, 0], 512, 11, wB[wn_dn])
        for i, (cp, rg) in enumerate(dseq):
            wd, wdB = nxt
            if i + 1 < len(dseq):
                cp2, rg2 = dseq[i + 1]
                nxt = ws_load(wdn_b[layer, cp2, rg2], 512, 11, wB[wn_dn])
            for blk in range(cfg["nblk"]):
                for ch in range(11):
                    fc = rg * 11 + ch
                    first = (rg == 0 and ch == 0)
                    last = (rg == 3 and ch == 10)
                    P.op(PE, lambda: TE.matmul(bank(blk)[0:CB, :], lhsT=uT[:, fc, blk * CB:(blk + 1) * CB], rhs=wd[:, ch, :],
                                               start=first, stop=last),
                         reads=[uTB[fc // 4], wdB], writes=[pb[blk]], sig=(ch == 10))
            if rg == 3:
                for blk in range(cfg["nblk"]):
                    xs_ = xres[0:CB, blk, cp * 512:(cp + 1) * 512]
                    P.op(DVE, lambda: V.tensor_tensor(out=xs_, in0=bank(blk)[0:CB, :], in1=xs_, op=ALU.add),
                         reads=[pb[blk]], writes=[xB[blk]])

    cfg = {"TW": TT, "nblk": NBLK, "CB": 128, "mid": 63, "HN": 8, "HC": 64, "hmid": 31}
    arenaF = arena[:, 0:ARENA].bitcast(F32)
    HSET = 5632
    hb_ = []
    for S_ in range(2):
        o = S_ * HSET
        d_ = {}
        d_["A"] = arenaF[:, o:o + 1024]
        d_["B"] = arenaF[:, o + 1024:o + 2048]
        d_["kk"] = arenaF[:, o + 2048:o + 2560]
        d_["qs"] = arenaF[:, o + 2560:o + 3072]
        d_["eb"] = arenaF[:, o + 3072:o + 3584]
        d_["sg"] = arenaF[:, o + 3584:o + 4608].rearrange("p (b e) -> p b e", e=128)
        d_["qtT"] = arenaF[:, o + 4608:o + 4864].bitcast(BF16)
        d_["ktT"] = arenaF[:, o + 4864:o + 5120].bitcast(BF16)
        d_["ibf"] = arenaF[:, o + 5120:o + 5632].bitcast(BF16).rearrange("p (b e) -> p b e", e=128)
        for n_ in list(d_.keys()):
            d_[n_ + "B"] = Buf("hg_%s%d" % (n_, S_))
        hb_.append(d_)
    hsmB = [Buf("hsm0"), Buf("hsm1")]
    sgl = [[Buf("%s%d" % (n_, S_)) for n_ in ("sbf", "ATt", "on32", "onb", "ktok", "hjunk", "hs2")] for S_ in range(2)]
    hg_all = [v_ for S_ in range(2) for v_ in hb_[S_].values() if isinstance(v_, Buf)]
    fnb = arenaF[:, 0:D]
    fnbB = Buf("fnb")
    fnbD = P.dsem("fnbd")
    arena_bufs = hg_all + uTB + ktB + [vbB, fnbB]

    def arena_fence():
        P.op(DVE, lambda: V.memset(fence_s[:], 0.0), writes=arena_bufs)

    def hgrn_head(h, qz_, ig_):
        (wqf, wqfB), (wig, wigB) = qz_, ig_
        TW, nblk, CB, mid = cfg["TW"], cfg["HN"], cfg["HC"], cfg["hmid"]
        S_ = h % 2
        b_ = hb_[S_]
        hs = hsm[:, S_, :]
        hsB = hsmB[S_]
        bA, bI, bC, bD = 4 * S_, 4 * S_ + 1, 4 * S_ + 2, 4 * S_ + 3
        sbf, ATt, on32, onb, ktok, hjunk = sbf2[:, S_, :], ATt2[:, S_, :], on322[:, S_, :], onb2[:, S_, :], ktok2[:, S_, :], hjunk2[:, S_, :]
        hs2 = hs22[:, S_, :]
        sbfB, ATB, on32B, onbB, ktokB, hjB, hs2B = sgl[S_]
        SstB = SstHB[h]
        NA = nblk * 2 * CB
        Av = b_["A"][:, 0:NA].rearrange("p (b c) -> p b c", c=2 * CB)
        Bv = b_["B"][:, 0:NA].rearrange("p (b c) -> p b c", c=2 * CB)
        AB, BB = b_["AB"], b_["BB"]

        def v3(ap):
            return ap[:, 0:TW].rearrange("p (b c) -> p b c", c=CB)
        kk, qs, eb, sg, qtT, ktT, ibf = b_["kk"], b_["qs"], b_["eb"], b_["sg"], b_["qtT"], b_["ktT"], b_["ibf"]
        P.op(DVE, lambda: V.memset(Av[:, :, 0:CB], 0.0), writes=[AB])
        yield
        P.op(DVE, lambda: V.memset(Bv[:, :, 0:CB], 0.0), writes=[BB])
        yield
        mm_group(bank(bA)[:, 0:TW], [(wqf[:, kc, 0:128], hT[:, kc, 0:TW]) for kc in range(KC)], [pb[bA]], [hTB, wqfB])
        yield
        P.op(ACT, lambda: A.activation(out=qs[:, 0:TW], in_=bank(bA)[:, 0:TW], func=AF.Silu),
             reads=[pb[bA]], writes=[b_["qsB"]])
        yield
        mm_group(bank(bA)[:, 0:TW], [(wqf[:, kc, 128:256], hT[:, kc, 0:TW]) for kc in range(KC)], [pb[bA]], [hTB, wqfB])
        yield "prep"
        P.op(ACT, lambda: A.activation(out=eb[:, 0:TW], in_=bank(bA)[:, 0:TW], func=AF.Sigmoid),
             reads=[pb[bA]], writes=[b_["ebB"]])
        yield
        P.op(DVE, lambda: V.tensor_scalar(out=Av[:, :, CB:2 * CB], in0=v3(eb), scalar1=lbt[:, 48 + h:49 + h],
                                          scalar2=lbt[:, 32 + h:33 + h], op0=ALU.mult, op1=ALU.add),
             reads=[b_["ebB"], constB], writes=[AB])
        yield
        P.op(DVE, lambda: V.tensor_scalar(out=v3(kk), in0=Av[:, :, CB:2 * CB], scalar1=-1.0, scalar2=1.0,
                                          op0=ALU.mult, op1=ALU.add), reads=[AB], writes=[b_["kkB"]])
        yield
        P.op(ACT, lambda: A.activation(out=Av[:, :, CB:2 * CB], in_=Av[:, :, CB:2 * CB], func=AF.Ln),
             reads=[AB], writes=[AB])
        yield
        src, srcB, dst, dstB = Av, AB, Bv, BB
        k = 1
        while k < CB:
            P.op(DVE, lambda: V.tensor_tensor(out=dst[:, :, CB:2 * CB], in0=src[:, :, CB:2 * CB],
                                              in1=src[:, :, CB - k:2 * CB - k], op=ALU.add),
                 reads=[srcB], writes=[dstB])
            yield
            src, srcB, dst, dstB = dst, dstB, src, srcB
            k *= 2
        bv, bB = src, srcB
        P.op(DVE, lambda: V.tensor_scalar(out=hs[:, 0:nblk].unsqueeze(2), in0=bv[:, :, CB + mid:CB + mid + 1],
                                          scalar1=-1.0, scalar2=None, op0=ALU.mult), reads=[bB], writes=[hsB])
        yield
        P.op(ACT, lambda: A.activation(out=hs[:, 8:8 + nblk], in_=hs[:, 0:nblk], func=AF.Exp, scale=-1.0),
             reads=[hsB], writes=[hsB])
        yield
        P.op(ACT, lambda: A.activation(out=hs[:, 16:16 + nblk].unsqueeze(2), in_=bv[:, :, 2 * CB - 1:2 * CB], func=AF.Exp),
             reads=[bB], writes=[hsB])
        yield
        for blk in range(nblk):
            P.op(ACT, lambda: A.activation(out=eb[:, blk * CB:(blk + 1) * CB], in_=bv[:, blk, CB:2 * CB], func=AF.Exp,
                                           bias=hs[:, blk:blk + 1]), reads=[bB, hsB], writes=[b_["ebB"]])
            yield
        P.op(DVE, lambda: V.tensor_copy(out=hs[:, 24:24 + nblk].unsqueeze(2), in_=v3(eb)[:, :, CB - 1:CB]),
             reads=[b_["ebB"]], writes=[hsB])
        yield
        P.op(DVE, lambda: V.tensor_tensor(out=qtT[:, 0:TW], in0=qs[:, 0:TW], in1=eb[:, 0:TW], op=ALU.mult),
             reads=[b_["qsB"], b_["ebB"]], writes=[b_["qtTB"]])
        yield
        P.op(DVE, lambda: V.reciprocal(out=eb[:, 0:TW], in_=eb[:, 0:TW]), reads=[b_["ebB"]], writes=[b_["ebB"]])
        yield
        P.op(DVE, lambda: V.tensor_tensor(out=ktT[:, 0:TW], in0=kk[:, 0:TW], in1=eb[:, 0:TW], op=ALU.mult),
             reads=[b_["kkB"], b_["ebB"]], writes=[b_["ktTB"]])
        yield
        pKC = bank(bC).bitcast(BF16)
        pKD = bank(bD).bitcast(BF16)
        for blk in range(nblk):
            cs = slice(blk * CB, (blk + 1) * CB)
            bk = bI
            mm_group(bank(bk)[0:CB, 0:256],
                     [(hT[:, kc, blk * CB:(blk + 1) * CB], wig[:, kc, 0:256]) for kc in range(KC)],
                     [pb[bk]], [hTB, wigB])
            yield
            P.op(DVE, lambda: V.tensor_copy(out=ibf[0:CB, blk, :], in_=bank(bk)[0:CB, 0:128]),
                 reads=[pb[bk]], writes=[b_["ibfB"]])
            yield
            P.op(ACT, lambda: A.activation(out=sg[0:CB, blk, :], in_=bank(bk)[0:CB, 128:256], func=AF.Silu),
                 reads=[pb[bk]], writes=[b_["sgB"]])
            yield
            P.op(PE, lambda: TE.matmul(bank(bC)[0:CB, 0:CB], lhsT=ktT[:, cs], rhs=qtT[:, cs], start=True, stop=True),
                 reads=[b_["ktTB"], b_["qtTB"]], writes=[pb[bC]])
            yield
            P.op(DVE, lambda: V.tensor_tensor(out=ATt[0:CB, 0:CB], in0=bank(bC)[0:CB, 0:CB], in1=triu[0:CB, 0:CB],
                                              op=ALU.mult), reads=[pb[bC], constB], writes=[ATB])
            yield
            P.op(DVE, lambda: V.tensor_scalar(out=sbf[:, :], in0=Sst[:, h, :], scalar1=hs[:, 8 + blk:9 + blk],
                                              scalar2=None, op0=ALU.mult), reads=[SstB, hsB], writes=[sbfB])
            yield
            P.op(PE, lambda: TE.matmul(bank(bD)[0:CB, 0:128], lhsT=ATt[0:CB, 0:CB], rhs=ibf[0:CB, blk, :],
                                       start=True, stop=False), reads=[ATB, b_["ibfB"]], writes=[pb[bD]], sig=False)
            yield
            P.op(PE, lambda: TE.matmul(bank(bD)[0:CB, 0:128], lhsT=qtT[:, cs], rhs=sbf[:, :], start=False, stop=True),
                 reads=[b_["qtTB"], sbfB], writes=[pb[bD]])
            yield
            P.op(PE, lambda: TE.transpose(out=pKC[0:CB, 256:384], in_=ktT[:, cs], identity=ident[:, :]),
                 reads=[b_["ktTB"], constB], writes=[pb[bC]])
            yield
            P.op(ACT, lambda: A.activation(out=ktok[0:CB, :], in_=pKC[0:CB, 256:384], func=AF.Copy),
                 reads=[pb[bC]], writes=[ktokB])
            yield
            P.op(PE, lambda: TE.matmul(bank(bC)[:, 256:384], lhsT=ktok[0:CB, :], rhs=ibf[0:CB, blk, :], start=True, stop=True),
                 reads=[ktokB, b_["ibfB"]], writes=[pb[bC]])
            yield
            P.op(DVE, lambda: V.tensor_scalar(out=Sst[:, h, :], in0=Sst[:, h, :], scalar1=hs[:, 16 + blk:17 + blk],
                                              scalar2=None, op0=ALU.mult), reads=[hsB], writes=[SstB])
            yield
            P.op(DVE, lambda: V.scalar_tensor_tensor(out=Sst[:, h, :], in0=bank(bC)[:, 256:384],
                                                     scalar=hs[:, 24 + blk:25 + blk], in1=Sst[:, h, :],
                                                     op0=ALU.mult, op1=ALU.add), reads=[pb[bC], hsB], writes=[SstB])
            yield
            P.op(ACT, lambda: A.activation(out=hjunk[0:CB, :], in_=bank(bD)[0:CB, 0:128], func=AF.Square,
                                           accum_out=hs2[0:CB, 0:1]), reads=[pb[bD]], writes=[hjB, hs2B])
            yield
            P.op(ACT, lambda: A.activation(out=hs2[0:CB, 1:2], in_=hs2[0:CB, 0:1], func=AF.Ln, scale=1.0 / 128,
                                           bias=epsc[0:CB, :]), reads=[hs2B, constB], writes=[hs2B])
            yield
            P.op(ACT, lambda: A.activation(out=hs2[0:CB, 2:3], in_=hs2[0:CB, 1:2], func=AF.Exp, scale=-0.5),
                 reads=[hs2B], writes=[hs2B])
            yield
            P.op(DVE, lambda: V.scalar_tensor_tensor(out=on32[0:CB, :], in0=bank(bD)[0:CB, 0:128], scalar=hs2[0:CB, 2:3],
                                                     in1=onormb[0:CB, :], op0=ALU.mult, op1=ALU.mult),
                 reads=[pb[bD], hs2B, constB], writes=[on32B])
            yield
            P.op(DVE, lambda: V.tensor_tensor(out=onb[0:CB, :], in0=on32[0:CB, :], in1=sg[0:CB, blk, :], op=ALU.mult),
                 reads=[on32B, b_["sgB"]], writes=[onbB])
            yield
            P.op(PE, lambda: TE.transpose(out=pKD[:, 512:512 + CB], in_=onb[0:CB, :], identity=ident[0:CB, 0:CB]),
                 reads=[onbB, constB], writes=[pb[bD]])
            yield
            P.op(ACT, lambda: A.activation(out=oT[:, h, cs], in_=pKD[:, 512:512 + CB], func=AF.Copy),
                 reads=[pb[bD]], writes=[oTB])
            yield

    def hgrn_layer():
        def load_qz(h):
            return ws_load(wqf_b[h, 0], 256, KC, wB["qf"])

        def load_ig(h):
            return ws_load(wqf_b[h, 1], 256, KC, wB["qf"])

        qz = {0: load_qz(0), 1: load_qz(1)}
        ig = {0: load_ig(0), 1: load_ig(1)}
        for h0 in range(0, RH, 2):
            alive = [hgrn_head(h0, qz[h0], ig[h0]), hgrn_head(h0 + 1, qz[h0 + 1], ig[h0 + 1])]
            prep_done, issued = 0, False
            while alive:
                for g_ in list(alive):
                    try:
                        if next(g_) == "prep":
                            prep_done += 1
                    except StopIteration:
                        alive.remove(g_)
                if prep_done == 2 and not issued and h0 + 2 < RH:
                    qz[h0 + 2] = load_qz(h0 + 2)
                    qz[h0 + 3] = load_qz(h0 + 3)
                    issued = True
            if h0 + 2 < RH:
                ig[h0 + 2] = load_ig(h0 + 2)
                ig[h0 + 3] = load_ig(h0 + 3)

    def final_store(out_ap, row0):
        TW, nblk, CB = cfg["TW"], cfg["nblk"], cfg["CB"]
        P.dma(SPQ, fnb, fnorm_d.partition_broadcast(128), fnbD, writes=[fnbB])
        for blk in range(nblk):
            xs_ = xres[0:CB, blk, :]
            P.op(ACT, lambda: A.activation(out=hb[0:CB, :], in_=xs_, func=AF.Square,
                                           accum_out=small[0:CB, blk:blk + 1]),
                 reads=[xB[blk]], writes=[hbB, smallB])
            P.op(ACT, lambda: A.activation(out=small[0:CB, 8 + blk:9 + blk], in_=small[0:CB, blk:blk + 1],
                                           func=AF.Ln, scale=1.0 / D, bias=epsc[0:CB, :]),
                 reads=[smallB, constB], writes=[smallB])
            P.op(ACT, lambda: A.activation(out=small[0:CB, 16 + blk:17 + blk], in_=small[0:CB, 8 + blk:9 + blk],
                                           func=AF.Exp, scale=-0.5), reads=[smallB], writes=[smallB])
            P.op(DVE, lambda: V.scalar_tensor_tensor(out=xs_, in0=xs_, scalar=small[0:CB, 16 + blk:17 + blk],
                                                     in1=fnb[0:CB, :], op0=ALU.mult, op1=ALU.mult),
                 reads=[smallB, fnbB], writes=[xB[blk]])
            r0 = row0 + blk * 128
            P.dma(POOL, out_ap[r0:r0 + CB, :], xs_, outD, reads=[xB[blk]])
        arena_fence()

    def store_x_dbg(j):
        for blk in range(NBLK):
            r0 = j * TT + blk * 128
            P.dma(POOL, y_o[r0:r0 + 128, :], xres[:, blk, :], outD, reads=[xB[blk]])

    for j in range(NT):
        if stage < 2:
            break
        load_x(j)
        norm_to_hT(0)
        if stage >= 3:
            attention_layer(j)
        if stage >= 4:
            out_proj(wo_b, "o")
            norm_to_hT(1)
        if stage >= 5:
            ffn(0)
        if stage >= 6 and do_l1:
            arena_fence()
            norm_to_hT(2)
            hgrn_layer()
        if stage >= 7 and do_l1:
            out_proj(wo2_b, "o2")
            norm_to_hT(3)
        if stage >= 8 and do_l1:
            arena_fence()
            ffn(1)
            arena_fence()
        if stage >= 9 and do_l1:
            final_store(y_o, j * TT)
        else:
            store_x_dbg(j)

    if do_l1:
        P.dma(POOL, shg_o.rearrange("h d v -> d h v"), Sst[:], SstD, reads=SstHB)
    P.dma(POOL, scv_o.rearrange("l p (f k) -> p l f k", k=2), convst[:], cvD, reads=[cvB])

    ktSsB = Buf("ktSs")
    ckD = P.dsem("ckd")

    def kt_prepass(q):
        for kb in range(PAST // 128):
            P.dma(POOL, hb[:, :], ck_d[q, kb * 128:(kb + 1) * 128, :], ckD, writes=[hbB])
            for c in range(KC):
                P.op(PE, lambda: TE.transpose(out=pT[:, c, 0:128], in_=hb[:, c * 128:(c + 1) * 128], identity=ident[:, :]),
                     reads=[hbB, constB], writes=[pb[4], pb[5]], sig=(c == KC - 1))
            P.op(ACT, lambda: A.activation(out=hT[:, :, (kb % 4) * 128:(kb % 4 + 1) * 128], in_=pT[:, :, 0:128], func=AF.Copy),
                 reads=[pb[4], pb[5]], writes=[hTB])
            if kb % 4 == 3:
                t0 = (kb // 4) * 512
                P.dma(SPQ, ktSs[q, :, :, t0:t0 + 512].rearrange("c p n -> p c n"), hT[:, :, :], ckD,
                      reads=[hTB], writes=[ktSsB])

    def attn_proj_s(q, h, wqk, wqkB, wv, wvB, par):
        QT = qtbuf[:, h % 2, :, :]
        qB = qtB[h % 2]
        hr = [hTB]
        for c in range(2):
            mm_group(bank(6)[:, 0:ST], [(wqk[:, kc, c * 128:(c + 1) * 128], hT[:, kc, 0:ST]) for kc in range(KC)],
                     [pb[6]], hr + [wqkB])
            P.op(ACT, lambda: A.activation(out=QT[:, c, 0:ST], in_=bank(6)[:, 0:ST], func=AF.Copy, scale=SCALE),
                 reads=[pb[6]], writes=[qB])
            mm_group(bank(7)[:, 0:ST], [(wqk[:, kc, 256 + c * 128:256 + (c + 1) * 128], hT[:, kc, 0:ST]) for kc in range(KC)],
                     [pb[7]], hr + [wqkB])
            P.op(ACT, lambda: A.activation(out=KTb[par][:, c, PAST:PAST + ST], in_=bank(7)[:, 0:ST], func=AF.Copy),
                 reads=[pb[7]], writes=[ktB[par]])
        so_ap, soB, _ = get_slot()
        mm_group(bank(0)[0:ST, 0:256], [(hT[:, kc, 0:ST], wqk[:, kc, 256:512]) for kc in range(KC)], [pb[0]], hr + [wqkB])
        P.op(ACT, lambda: A.activation(out=so_ap[0:ST, 0:256], in_=bank(0)[0:ST, 0:256], func=AF.Copy),
             reads=[pb[0]], writes=[soB])
        P.dma(POOL, ks_o[q * ST:(q + 1) * ST, h * 256:(h + 1) * 256], so_ap[0:ST, 0:256], poolD[id(soB)], reads=[soB])
        so2, so2B, _ = get_slot()
        mm_group(bank(2)[0:ST, 0:256], [(hT[:, kc, 0:ST], wv[:, kc, :]) for kc in range(KC)], [pb[2]], hr + [wvB])
        P.op(ACT, lambda: A.activation(out=so2[0:ST, 0:256], in_=bank(2)[0:ST, 0:256], func=AF.Copy),
             reads=[pb[2]], writes=[so2B])
        P.dma(POOL, vs_o[q * ST:(q + 1) * ST, h * 256:(h + 1) * 256], so2[0:ST, 0:256], poolD[id(so2B)], reads=[so2B])
        P.op(DVE, lambda: V.tensor_copy(out=Vb[0:ST, 32, 0:256], in_=so2[0:ST, 0:256]), reads=[so2B], writes=[vbB])
        P.op(DVE, lambda: V.memset(Vb[:, :, 256:VW], 1.0), writes=[vbB])
        return QT, qB

    def attention_layer_s(q):
        nkb = PAST // 128 + 1
        qsubs = [(0, ST)]
        wq = [None] * NH
        wq[0] = (ws_load(wqk_b[0], 512, KC, wB["qkv"]), ws_load(wv_b[0], 256, KC, wB["qkv"]))
        for h in range(NH):
            (wqk, wqkB), (wv, wvB) = wq[h]
            par = h % 2
            P.dma(SPQ, KTb[par][:, :, 0:PAST], ktSs[q, 2 * h:2 * h + 2, :, :].rearrange("c p n -> p c n"), ktD[par],
                  reads=[ktSsB], writes=[ktB[par]])
            for i4 in range(4):
                P.dma(POOL, Vb[:, 8 * i4:8 * i4 + 8, 0:256],
                      cv_d[q, 1024 * i4:1024 * (i4 + 1), h * 256:(h + 1) * 256].rearrange("(b p) e -> p b e", p=128),
                      vbD, writes=[vbB])
            QT, qB = attn_proj_s(q, h, wqk, wqkB, wv, wvB, par)
            if h + 1 < NH:
                wq[h + 1] = (ws_load(wqk_b[h + 1], 512, KC, wB["qkv"]), ws_load(wv_b[h + 1], 256, KC, wB["qkv"]))
            slope = 2.0 ** (-(h + 1))
            P.op(DVE, lambda: V.tensor_scalar(out=sbias[:], in0=stabs[:], scalar1=-slope, scalar2=None, op0=ALU.mult),
                 reads=[constB], writes=[biasB])
            P.op(DVE, lambda: V.tensor_scalar(out=cH[:], in0=cpos[:], scalar1=-slope, scalar2=None, op0=ALU.mult),
                 reads=[constB], writes=[biasB])

            def diag_fn(m):
                return sbias[:, ST:2 * ST], cH[:, 0:1]

            attn_core(h, QT, qB, nkb, nkb - 1, par, ST, qsubs, diag_fn, slope, lambda: None,
                      past_bias=sbias[:, 0:ST], kn_last=ST)
            attn_finish(h, qsubs, ST)

    if do_sample:
        cfg.update({"TW": ST, "nblk": 1, "CB": ST, "HN": 1, "HC": ST, "hmid": ST // 2 - 1})
        for q in range(SB):
            arena_fence()
            kt_prepass(q)
            P.dma(SPQ, Sst[:], shg_in[q].rearrange("h d v -> d h v"), SstD, writes=SstHB)
            for l in range(2):
                P.dma(SPQ, convst[:, l, :, :].rearrange("p f k -> p (f k)"), scv_in[l, q], cvD, writes=[cvB])
            P.dma(SPQ, xres[0:ST, 0, :], xs_d[q * ST:(q + 1) * ST, :], xD[0], writes=[xB[0]])
            norm_to_hT(0, ST, 1, ST)
            attention_layer_s(q)
            out_proj(wo_b, "o")
            norm_to_hT(1, ST, 1, ST)
            ffn(0)
            arena_fence()
            norm_to_hT(2, ST, 1, ST)
            hgrn_layer()
            out_proj(wo2_b, "o2")
            norm_to_hT(3, ST, 1, ST)
            arena_fence()
            ffn(1)
            arena_fence()
            final_store(ys_o, q * ST)
            P.dma(POOL, shgs_o[q].rearrange("h d v -> d h v"), Sst[:], SstD, reads=SstHB)
            for l in range(2):
                P.dma(POOL, scvs_o[l, q], convst[:, l, :, :].rearrange("p f k -> p (f k)"), cvD, reads=[cvB])

    allq = [outD, cvD, SstD, ckD] + slotD + slotDP
    for ds in allq:
        if ds.total > 0:
            nc.gpsimd.wait_ge(ds.h, ds.total)
    es.close()
    return nc


_CACHE = {}


def _prep_shared(inp):
    sh = {}
    sh["w_qkv"] = np.ascontiguousarray(inp["attn_w_qkv"][0])
    sh["w_o"] = np.ascontiguousarray(inp["attn_w_o"][0])
    sh["w_qfig"] = np.ascontiguousarray(inp["rec_w_qfig"][0])
    sh["w_o2"] = np.ascontiguousarray(inp["rec_w_o"][0])
    sh["w_up"] = np.ascontiguousarray(inp["ffn_w_up"])
    sh["w_dn"] = np.ascontiguousarray(inp["ffn_w_down"])
    norms = np.stack([inp["mixer_norm"][0], inp["ffn_norm"][0], inp["mixer_norm"][1], inp["ffn_norm"][1]])
    sh["gcols"] = np.ascontiguousarray(norms.reshape(4, KC, 128).transpose(2, 0, 1).reshape(128, 4 * KC))
    sh["fnorm"] = np.ascontiguousarray(inp["final_norm"])
    sh["subln"] = np.ascontiguousarray(inp["attn_subln"][0])
    sh["lamv"] = np.ascontiguousarray(np.concatenate([inp["attn_lambda_q1"][0], inp["attn_lambda_k1"][0],
                                                      inp["attn_lambda_q2"][0], inp["attn_lambda_k2"][0]]))
    sh["rlb"] = np.ascontiguousarray(inp["rec_lower_bounds"].reshape(2, RH, 128).transpose(2, 0, 1).reshape(128, 2 * RH))
    sh["onorm"] = np.ascontiguousarray(inp["rec_out_norm"][0])
    cw = np.concatenate([inp["ffn_conv_w"], inp["ffn_conv_b"][:, None, :]], axis=1)
    sh["convp"] = np.ascontiguousarray(cw.reshape(2, 4, FC, 128).transpose(3, 0, 2, 1).reshape(128, 2 * FC * 4))
    sh.update({k: v for k, v in host_consts().items() if k in ("ident", "tabs", "cpos", "tri", "triu")})
    return sh


def _conv_state_from(scv):
    return np.ascontiguousarray(scv.reshape(2, 128, FC, 2).transpose(0, 3, 2, 1).reshape(2, 2, DFF))


def kernel(**inp):
    dbg = inp.pop("_dbg", None)
    inp = {k: np.asarray(v) for k, v in inp.items()}
    NT_, do_l1, do_sample, stage, ncores, core0 = 8, True, DO_SAMPLE, 9, NCORES, 0
    if dbg is not None:
        NT_, do_l1, do_sample, stage, ncores, core0 = dbg
    key = (NT_, do_l1, do_sample, stage)
    if key not in _CACHE:
        _CACHE[key] = build_program(NT=NT_, do_l1=do_l1, do_sample=do_sample, stage=stage)
    nc = _CACHE[key]
    sh = _prep_shared(inp)
    in_maps = []
    for c in range(ncores):
        m = dict(sh)
        m["xp"] = np.ascontiguousarray(inp["x_prompt"][c % 4])
        if do_sample:
            s0 = c * SB
            m["xs"] = np.ascontiguousarray(inp["x_sample"][s0:s0 + SB].reshape(SB * ST, D))
            m["ck"] = np.ascontiguousarray(inp["cache_k"][0, s0:s0 + SB].reshape(SB, PAST, D))
            m["cv"] = np.ascontiguousarray(inp["cache_v"][0, s0:s0 + SB].reshape(SB, PAST, D))
            m["shg_in"] = np.ascontiguousarray(inp["state_hgrn"][0, s0:s0 + SB])
            sc_ = inp["state_conv"][:, s0:s0 + SB].reshape(2, SB, 2, FC, 128)
            m["scv_in"] = np.ascontiguousarray(sc_.transpose(0, 1, 4, 3, 2).reshape(2, SB, 128, FC * 2))
            hc = host_consts()
            m["stab_c"] = hc["stab_c"]
            m["stab_n"] = hc["stab_n"]
        in_maps.append(m)
    res = run_bass_kernel_spmd(nc, in_maps, core_ids=list(range(core0, core0 + ncores)))
    if dbg is not None:
        return res
    R_ = res.results
    B = 4
    y_p = np.stack([R_[b]["y"] for b in range(B)]).astype(np.float32)
    k_p = np.stack([R_[b]["kout"] for b in range(B)]).reshape(1, B, T, NH, 2, 128).astype(np.float32)
    v_p = np.stack([R_[b]["vout"] for b in range(B)]).reshape(1, B, T, NH, 256).astype(np.float32)
    sh_p = np.stack([R_[b]["shg"] for b in range(B)]).reshape(1, B, RH, 128, 128).astype(np.float32)
    sc_p = np.stack([_conv_state_from(R_[b]["scv"]) for b in range(B)], axis=1).astype(np.float32)
    NS = NCORES * SB
    if do_sample:
        y_s = np.concatenate([R_[c]["ys"].reshape(SB, ST, D) for c in range(NCORES)]).astype(np.float32)
        k_s = np.concatenate([R_[c]["ksout"].reshape(SB, ST, D) for c in range(NCORES)]).reshape(1, NS, ST, NH, 2, 128)
        v_s = np.concatenate([R_[c]["vsout"].reshape(SB, ST, D) for c in range(NCORES)]).reshape(1, NS, ST, NH, 256)
        sh_s = np.concatenate([R_[c]["shgs"] for c in range(NCORES)]).reshape(1, NS, RH, 128, 128)
        sc_s = np.concatenate([np.stack([_conv_state_from(R_[c]["scvs"][:, q]) for q in range(SB)], axis=1)
                               for c in range(NCORES)], axis=1)
    else:
        y_s = np.zeros((NS, ST, D), np.float32)
        k_s = np.zeros((1, NS, ST, NH, 2, 128), np.float32)
        v_s = np.zeros((1, NS, ST, NH, 256), np.float32)
        sh_s = np.zeros((1, NS, RH, 128, 128), np.float32)
        sc_s = np.zeros((2, NS, 2, DFF), np.float32)
    return (y_p, y_s, k_p, v_p, np.ascontiguousarray(k_s, dtype=np.float32), np.ascontiguousarray(v_s, dtype=np.float32),
            sh_p, np.ascontiguousarray(sh_s, dtype=np.float32), sc_p, np.ascontiguousarray(sc_s, dtype=np.float32))
```

```python
import numpy as np
from contextlib import ExitStack
import concourse.bass as bass
import concourse.mybir as mybir
from concourse.bass_utils import run_bass_kernel_spmd

F32 = mybir.dt.float32
BF16 = mybir.dt.bfloat16
AF = mybir.ActivationFunctionType
ALU = mybir.AluOpType
AX = mybir.AxisListType

D = 2048
NH = 8
T = 4096
TT = 512
NBLK = 4
KC = 16
DFF = 5632
FC = 44
RH = 16
EPS = 1e-6
SCALE = 128 ** -0.5
LAM_INIT0 = 0.8 - 0.6 * float(np.exp(-0.3 * 0))
NCORES = 8
SB = 2
ST = 16
PAST = 4096
DO_SAMPLE = True
VW = 264


class Buf:
    __slots__ = ("name", "w", "r", "excl")

    def __init__(self, name, excl=False):
        self.name = name
        self.w = {}
        self.r = {}
        self.excl = excl


class Group:
    def __init__(self, parts):
        self.parts = parts


def _flat(bs):
    out = []
    for b in bs:
        if isinstance(b, Group):
            out.extend(b.parts)
        else:
            out.append(b)
    return out


class DSem:
    def __init__(self, h):
        self.h = h
        self.total = 0


class Eng:
    def __init__(self, e, sem, name):
        self.e = e
        self.sem = sem
        self.cnt = 0
        self.waited = {}
        self.name = name


class Prog:
    def __init__(self, nc, es):
        self.nc = nc
        self.es = es
        self.nsem = 0

    def sem(self, name):
        self.nsem += 1
        return self.es.enter_context(self.nc.semaphore(name))

    def dsem(self, name):
        return DSem(self.sem(name))

    def wait(self, eng, evs, skip=None):
        for ev in evs:
            if ev[0] == "e":
                src, c = ev[1], ev[2]
                if src is eng and eng.name == "pe":
                    continue
                key = id(src)
                if eng.waited.get(key, 0) >= c:
                    continue
                eng.e.wait_ge(src.sem, c)
                eng.waited[key] = c
            else:
                ds = ev[1]
                if ds is skip:
                    continue
                c = ds.total
                key = id(ds)
                if eng.waited.get(key, 0) >= c:
                    continue
                eng.e.wait_ge(ds.h, c)
                eng.waited[key] = c

    def _deps(self, reads, writes):
        evs = []
        for b in reads:
            evs.extend(b.w.values())
            if b.excl:
                evs.extend(b.r.values())
        for b in writes:
            evs.extend(b.w.values())
            evs.extend(b.r.values())
        return evs

    def op(self, eng, fn, reads=(), writes=(), sig=True):
        reads, writes = _flat(reads), _flat(writes)
        self.wait(eng, self._deps(reads, writes))
        inst = fn()
        if sig:
            eng.cnt += 1
            inst.then_inc(eng.sem, 1)
            ev = ("e", eng, eng.cnt)
        else:
            ev = ("e", eng, eng.cnt + 1)
        k = id(eng)
        for b in reads:
            b.r[k] = ev
        for b in writes:
            b.w = {k: ev}
            b.r = {}
        return inst

    def dma(self, q, out, in_, ds, reads=(), writes=(), **kw):
        reads, writes = _flat(reads), _flat(writes)
        self.wait(q, self._deps(reads, writes), skip=ds)
        inst = q.e.dma_start(out=out, in_=in_, **kw)
        ds.total += 16
        inst.then_inc(ds.h, 16)
        ev = ("d", ds)
        k = id(ds)
        for b in reads:
            b.r[k] = ev
        for b in writes:
            b.w = {k: ev}
            b.r = {}
        return inst


def host_consts():
    c = {}
    c["ident"] = np.eye(128, dtype=np.float32)
    p = np.arange(128)[:, None]
    xx = np.arange(1024)[None, :] - 384
    allowed = (xx // 64) >= (p // 64)
    tabs = np.where(allowed, np.abs(xx - p), 1.0e6).astype(np.float32)
    c["tabs"] = tabs
    c["cpos"] = np.tile((np.arange(32, dtype=np.float32) * 128.0)[None, :], (128, 1)).astype(np.float32)
    s = np.arange(64)
    tri = (s[:, None] <= s[None, :]).astype(np.float32)
    c["tri"] = np.concatenate([tri, tri], axis=0)
    s2 = np.arange(128)
    c["triu"] = (s2[:, None] <= s2[None, :]).astype(np.float32)
    i = np.arange(ST)[None, :]
    c["stab_c"] = (i - p + 128).astype(np.float32)
    c["stab_n"] = np.where(p < ST, np.abs(i - p), 0).astype(np.float32)
    return c


def build_program(NT=8, do_l1=True, do_sample=True, dbg=False, stage=9):
    nc = bass.Bass("TRN2", target_bir_lowering=False)
    es = ExitStack()
    P = Prog(nc, es)
    E = es.enter_context

    def din(name, shape, dt=F32):
        return nc.dram_tensor(name, list(shape), dt, kind="ExternalInput").ap()

    def dout(name, shape, dt=F32):
        return nc.dram_tensor(name, list(shape), dt, kind="ExternalOutput").ap()

    def dscr(name, shape, dt):
        return nc.dram_tensor(name, list(shape), dt).ap()

    xp = din("xp", [T, D])
    w_qkv = din("w_qkv", [D, 3 * D])
    w_o = din("w_o", [D, D])
    w_qfig = din("w_qfig", [D, 4 * D])
    w_o2 = din("w_o2", [D, D])
    w_up = din("w_up", [2, D, 2 * DFF])
    w_dn = din("w_dn", [2, DFF, D])
    gcols_d = din("gcols", [128, 4 * KC])
    fnorm_d = din("fnorm", [D])
    subln_d = din("subln", [256])
    lamv_d = din("lamv", [4 * 128])
    rlb_d = din("rlb", [128, 2 * RH])
    onorm_d = din("onorm", [128])
    convp_d = din("convp", [128, 2 * FC * 4])
    ident_d = din("ident", [128, 128])
    tabs_d = din("tabs", [128, 1024])
    cpos_d = din("cpos", [128, 32])
    tri_d = din("tri", [128, 64])
    triu_d = din("triu", [128, 128])
    if do_sample:
        xs_d = din("xs", [SB * ST, D])
        ck_d = din("ck", [SB, PAST, D])
        cv_d = din("cv", [SB, PAST, D])
        shg_in = din("shg_in", [SB, RH, 128, 128])
        scv_in = din("scv_in", [2, SB, 128, FC * 2])
        stabc_d = din("stab_c", [128, ST])
        stabn_d = din("stab_n", [128, ST])

    y_o = dout("y", [T, D])
    k_o = dout("kout", [T, D])
    v_o = dout("vout", [T, D])
    shg_o = dout("shg", [RH, 128, 128])
    scv_o = dout("scv", [2, 128, FC * 2])
    if do_sample:
        ys_o = dout("ys", [SB * ST, D])
        ks_o = dout("ksout", [SB * ST, D])
        vs_o = dout("vsout", [SB * ST, D])
        shgs_o = dout("shgs", [SB, RH, 128, 128])
        scvs_o = dout("scvs", [2, SB, 128, FC * 2])

    wqk_b = dscr("wqk_b", [NH, 128, KC, 512], BF16)
    wv_b = dscr("wv_b", [NH, 128, KC, 256], BF16)
    wo_b = dscr("wo_b", [4, 128, KC, 512], BF16)
    wqf_b = dscr("wqf_b", [RH, 2, 128, KC, 256], BF16)
    wo2_b = dscr("wo2_b", [4, 128, KC, 512], BF16)
    wup_b = dscr("wup_b", [2, 44, 128, KC, 256], BF16)
    wdn_b = dscr("wdn_b", [2, 4, 4, 128, 11, 512], BF16)
    ktS = dscr("ktS", [2 * NH, 128, T], BF16)
    ktSs = dscr("ktSs", [SB, 2 * NH, 128, PAST], BF16)
    vS = dscr("vS", [NH, T, VW], BF16)

    PE = Eng(nc.tensor, P.sem("pe"), "pe")
    ACT = Eng(nc.scalar, P.sem("act"), "act")
    DVE = Eng(nc.vector, P.sem("dve"), "dve")
    POOL = Eng(nc.gpsimd, P.sem("pool"), "pool")
    SPQ = Eng(nc.sync, None, "sp")
    V = nc.vector
    A = nc.scalar
    TE = nc.tensor

    def sb(name, shape, dt):
        return E(nc.sbuf_tensor("s_" + name, list(shape), dt))

    xres = sb("xres", [128, NBLK, D], F32)
    hT = sb("hT", [128, KC, TT], BF16)
    oT = sb("oT", [128, KC, TT], BF16)
    hb = sb("hb", [128, D], BF16)
    NWS = 4
    HSW = KC * 256
    wsl = sb("wsl", [128, NWS * HSW], BF16)
    KTW = 4224
    ARENA = 4 * KTW + 33 * VW
    arena = sb("arena", [128, ARENA], BF16)
    NSL = 10
    SLW = 528
    slots = sb("slots", [128, NSL, SLW], F32)
    Sst = sb("Sst", [128, RH, 128], F32)
    tabs = sb("tabs", [128, 1024], F32)
    biasH = sb("biasH", [128, 1024], F32)
    cpos = sb("cpos", [128, 32], F32)
    cH = sb("cH", [128, 32], F32)
    identf = sb("identf", [128, 128], F32)
    ident = sb("ident", [128, 128], BF16)
    gcols = sb("gcols", [128, 4 * KC], F32)
    sublnb = sb("sublnb", [128, 256], F32)
    lamv = sb("lamv", [128, 4 * 128], F32)
    lamt = sb("lamt", [128, 8], F32)
    rlb = sb("rlb", [128, 2 * RH], F32)
    lbt = sb("lbt", [128, 4 * RH], F32)
    onormb = sb("onormb", [128, 128], F32)
    convp = sb("convp", [128, 2 * FC * 4], F32)
    convst = sb("convst", [128, 2, FC, 2], F32)
    tri = sb("tri", [128, 64], F32)
    triu = sb("triu", [128, 128], F32)
    hsm = sb("hsm", [128, 2, 32], F32)
    hs22 = sb("hs2", [128, 2, 8], F32)
    sbf2 = sb("sbf", [128, 2, 128], BF16)
    ATt2 = sb("ATt", [128, 2, 128], BF16)
    on322 = sb("on32", [128, 2, 128], F32)
    onb2 = sb("onb", [128, 2, 128], BF16)
    ktok2 = sb("ktok", [128, 2, 128], BF16)
    hjunk2 = sb("hjunk", [128, 2, 128], BF16)
    fence_s = sb("fence_s", [128, 2], F32)
    stabs = sb("stabs", [128, 2 * ST], F32)
    sbias = sb("sbias", [128, 2 * ST], F32)
    small = sb("small", [128, 64], F32)
    o0buf = sb("o0buf", [128, NBLK, 256], F32)
    qtbuf = sb("qtbuf", [128, 2, 2, TT], BF16)
    qtB = [Buf("qt0"), Buf("qt1")]
    ps = E(nc.psum_tensor("ps", [128, 4096], F32))

    def bank(i):
        return ps[:, i * 512:(i + 1) * 512]

    pb = [Buf("pb%d" % i, excl=True) for i in range(8)]

    xB = [Buf("x%d" % i) for i in range(NBLK)]
    xD = [P.dsem("xd%d" % i) for i in range(NBLK)]
    hTB = Buf("hT")
    oTB = Buf("oT")
    hbB = Buf("hb")
    smallB = Buf("small")
    constB = Buf("const")
    constD = P.dsem("constd")
    wslB = [Buf("ws%d" % i) for i in range(NWS)]
    wslD = [P.dsem("wsd%d" % i) for i in range(NWS)]
    ktB = [Buf("kt0"), Buf("kt1")]
    ktD = [P.dsem("ktd0"), P.dsem("ktd1")]
    vbB = Buf("vb")
    vbD = P.dsem("vbd")
    uTB = [Buf("uT%d" % g) for g in range(11)]
    slotB = [Buf("sl%d" % i) for i in range(NSL)]
    slotD = [P.dsem("sld%d" % i) for i in range(NSL)]
    slotDP = [P.dsem("slp%d" % i) for i in range(NSL)]
    poolD = {id(slotB[i]): slotDP[i] for i in range(NSL)}
    SstHB = [Buf("Sst%d" % h_) for h_ in range(RH)]
    SstD = P.dsem("Sstd")
    biasB = Buf("biasH")
    o0B = Buf("o0")
    cvB = Buf("convst")
    cvD = P.dsem("cvd")
    wB = {n: Buf("w_" + n) for n in ["qkv", "o", "qf", "o2", "up0", "up1", "dn0", "dn1"]}
    wD = {n: P.dsem("wd_" + n) for n in wB}
    ktSB = Buf("ktS")
    vSB = Buf("vS")
    outD = P.dsem("outd")

    state = {"slot": 0, "ws": 0}

    def get_slot():
        i = state["slot"]
        state["slot"] = (i + 1) % NSL
        return slots[:, i, :], slotB[i], slotD[i]

    def ws_load(src_ap, ncols, kc, wbuf):
        n = kc * ncols
        p = state["ws"]
        if n > HSW:
            p = (p + (p % 2)) % NWS
            bufs = [wslB[p], wslB[p + 1]]
            state["ws"] = (p + 2) % NWS
        else:
            bufs = [wslB[p]]
            state["ws"] = (p + 1) % NWS
        view = wsl[:, p * HSW:p * HSW + n].rearrange("p (c n) -> p c n", c=kc)
        P.dma(SPQ, view, src_ap, wslD[p], reads=[wbuf], writes=bufs)
        return view, Group(bufs)

    KTb = [arena[:, i * 2 * KTW:(i + 1) * 2 * KTW].rearrange("p (c n) -> p c n", c=2) for i in range(2)]
    Vb = arena[:, 4 * KTW:4 * KTW + 33 * VW].rearrange("p (b e) -> p b e", e=VW)
    uT = arena[:, 0:FC * 512].rearrange("p (c n) -> p c n", c=FC)

    def cload(dst, src):
        P.dma(SPQ, dst, src, constD, writes=[constB])

    cload(identf[:], ident_d)
    cload(tabs[:], tabs_d)
    cload(cpos[:], cpos_d)
    cload(gcols[:], gcols_d)
    cload(sublnb[:], subln_d.partition_broadcast(128))
    cload(lamv[:], lamv_d.partition_broadcast(128))
    cload(rlb[:], rlb_d)
    cload(onormb[:], onorm_d.partition_broadcast(128))
    cload(convp[:], convp_d)
    cload(tri[:], tri_d)
    cload(triu[:], triu_d)
    if do_sample:
        cload(stabs[:, 0:ST], stabc_d)
        cload(stabs[:, ST:2 * ST], stabn_d)
    cb = [constB]
    P.op(DVE, lambda: V.tensor_copy(out=ident[:], in_=identf[:]), reads=cb, writes=cb)
    P.op(DVE, lambda: V.memset(convst[:], 0.0), writes=[cvB])
    P.op(DVE, lambda: V.memset(Sst[:], 0.0), writes=SstHB)
    sl, sB_, _ = get_slot()
    P.op(DVE, lambda: V.tensor_tensor(out=sl[:, 0:128], in0=lamv[:, 0:128], in1=lamv[:, 128:256], op=ALU.mult),
         reads=cb, writes=[sB_])
    P.op(DVE, lambda: V.reduce_sum(out=lamt[:, 0:1], in_=sl[:, 0:128], axis=AX.X), reads=[sB_], writes=cb)
    P.op(DVE, lambda: V.tensor_tensor(out=sl[:, 0:128], in0=lamv[:, 256:384], in1=lamv[:, 384:512], op=ALU.mult),
         reads=cb, writes=[sB_])
    P.op(DVE, lambda: V.reduce_sum(out=lamt[:, 1:2], in_=sl[:, 0:128], axis=AX.X), reads=[sB_], writes=cb)
    P.op(ACT, lambda: A.activation(out=lamt[:, 2:4], in_=lamt[:, 0:2], func=AF.Exp), reads=cb, writes=cb)
    P.op(DVE, lambda: V.tensor_tensor(out=lamt[:, 4:5], in0=lamt[:, 2:3], in1=lamt[:, 3:4], op=ALU.subtract),
         reads=cb, writes=cb)
    P.op(DVE, lambda: V.tensor_scalar(out=lamt[:, 4:5], in0=lamt[:, 4:5], scalar1=LAM_INIT0, scalar2=None, op0=ALU.add),
         reads=cb, writes=cb)
    P.op(DVE, lambda: V.tensor_scalar(out=lamt[:, 5:6], in0=lamt[:, 4:5], scalar1=-1.0, scalar2=None, op0=ALU.mult),
         reads=cb, writes=cb)
    P.op(DVE, lambda: V.tensor_scalar(out=sublnb[:], in0=sublnb[:], scalar1=1.0 - LAM_INIT0, scalar2=None, op0=ALU.mult),
         reads=cb, writes=cb)
    P.op(ACT, lambda: A.activation(out=lbt[:, 0:32], in_=rlb[:, 0:32], func=AF.Exp), reads=cb, writes=cb)
    P.op(DVE, lambda: V.tensor_tensor(out=lbt[:, 32:48], in0=lbt[:, 0:16], in1=lbt[:, 16:32], op=ALU.add),
         reads=cb, writes=cb)
    P.op(DVE, lambda: V.reciprocal(out=lbt[:, 32:48], in_=lbt[:, 32:48]), reads=cb, writes=cb)
    P.op(DVE, lambda: V.tensor_tensor(out=lbt[:, 32:48], in0=lbt[:, 32:48], in1=lbt[:, 16:32], op=ALU.mult),
         reads=cb, writes=cb)
    P.op(DVE, lambda: V.tensor_scalar(out=lbt[:, 48:64], in0=lbt[:, 32:48], scalar1=-1.0, scalar2=1.0,
                                      op0=ALU.mult, op1=ALU.add), reads=cb, writes=cb)

    def cast(dst, src, name):
        P.dma(POOL, dst, src, wD[name], writes=[wB[name]])

    def rk(ap):
        return ap.rearrange("(c p) n -> p c n", p=128)

    for h in range(NH):
        cast(wqk_b[h, :, :, 0:256], rk(w_qkv[:, h * 256:(h + 1) * 256]), "qkv")
        cast(wqk_b[h, :, :, 256:512], rk(w_qkv[:, D + h * 256:D + (h + 1) * 256]), "qkv")
        cast(wv_b[h], rk(w_qkv[:, 2 * D + h * 256:2 * D + (h + 1) * 256]), "qkv")
    for cp in range(4):
        cast(wo_b[cp], rk(w_o[:, cp * 512:(cp + 1) * 512]), "o")
    for l in range(2 if do_l1 else 1):
        for g in range(22):
            cast(wup_b[l, 2 * g], rk(w_up[l, :, g * 256:(g + 1) * 256]), "up%d" % l)
            cast(wup_b[l, 2 * g + 1], rk(w_up[l, :, DFF + g * 256:DFF + (g + 1) * 256]), "up%d" % l)
        for cp in range(4):
            for rg in range(4):
                cast(wdn_b[l, cp, rg], rk(w_dn[l, rg * 1408:(rg + 1) * 1408, cp * 512:(cp + 1) * 512]), "dn%d" % l)
        if l == 0 and do_l1:
            for h in range(RH):
                for part in range(4):
                    cast(wqf_b[h, part // 2, :, :, (part % 2) * 128:(part % 2 + 1) * 128],
                         rk(w_qfig[:, part * D + h * 128:part * D + (h + 1) * 128]), "qf")
            for cp in range(4):
                cast(wo2_b[cp], rk(w_o2[:, cp * 512:(cp + 1) * 512]), "o2")

    def mm_group(out_ap, pairs, out_bufs, read_bufs):
        n = len(pairs)
        for i, (l, r) in enumerate(pairs):
            P.op(PE, lambda: TE.matmul(out_ap, lhsT=l, rhs=r, start=(i == 0), stop=(i == n - 1)),
                 reads=read_bufs, writes=out_bufs, sig=(i == n - 1))

    pT = ps[:, 4 * 512:6 * 512].bitcast(BF16).rearrange("p (c n) -> p c n", c=KC)

    def norm_to_hT(gi, npart=128, nblk=NBLK, ncol=128):
        g_b = gcols[:, gi * KC:(gi + 1) * KC].unsqueeze(2).to_broadcast([128, KC, ncol])
        for blk in range(nblk):
            xs_ = xres[0:npart, blk, :]
            P.op(ACT, lambda: A.activation(out=hb[0:npart, :], in_=xs_, func=AF.Square,
                                           accum_out=small[0:npart, blk:blk + 1]),
                 reads=[xB[blk]], writes=[hbB, smallB])
            P.op(ACT, lambda: A.activation(out=small[0:npart, 8 + blk:9 + blk], in_=small[0:npart, blk:blk + 1],
                                           func=AF.Ln, scale=1.0 / D, bias=epsc[0:npart, :]),
                 reads=[smallB, constB], writes=[smallB])
            P.op(ACT, lambda: A.activation(out=small[0:npart, 16 + blk:17 + blk], in_=small[0:npart, 8 + blk:9 + blk],
                                           func=AF.Exp, scale=-0.5), reads=[smallB], writes=[smallB])
            P.op(DVE, lambda: V.tensor_scalar(out=hb[0:npart, :], in0=xs_, scalar1=small[0:npart, 16 + blk:17 + blk],
                                              scalar2=None, op0=ALU.mult),
                 reads=[xB[blk], smallB], writes=[hbB])
            for c in range(KC):
                P.op(PE, lambda: TE.transpose(out=pT[:, c, 0:npart], in_=hb[0:npart, c * 128:(c + 1) * 128],
                                              identity=ident[0:npart, 0:npart]),
                     reads=[hbB, constB], writes=[pb[4], pb[5]], sig=(c == KC - 1))
            P.op(DVE, lambda: V.tensor_tensor(out=hT[:, :, blk * ncol:(blk + 1) * ncol], in0=pT[:, :, 0:ncol],
                                              in1=g_b, op=ALU.mult),
                 reads=[pb[4], pb[5], constB], writes=[hTB])

    epsc = sb("epsc", [128, 1], F32)
    P.op(DVE, lambda: V.memset(epsc[:], EPS), writes=[constB])

    def load_x(j):
        for blk in range(NBLK):
            r0 = j * TT + blk * 128
            P.dma(SPQ, xres[:, blk, :], xp[r0:r0 + 128, :], xD[blk], writes=[xB[blk]])

    def attn_proj(j, h, wqk, wqkB, wv, wvB):
        QT = qtbuf[:, h % 2, :, :]
        qB = qtB[h % 2]
        ks_ap, kB, kD = get_slot()
        KT = ks_ap.bitcast(BF16)[:, 0:1024].rearrange("p (c n) -> p c n", c=2)
        vs_ap, vB, vD = get_slot()
        VH = vs_ap.bitcast(BF16)[:, 0:4 * VW].rearrange("p (b e) -> p b e", e=VW)
        hr = [hTB]
        for c in range(2):
            mm_group(bank(6), [(wqk[:, kc, c * 128:(c + 1) * 128], hT[:, kc, :]) for kc in range(KC)],
                     [pb[6]], hr + [wqkB])
            P.op(ACT, lambda: A.activation(out=QT[:, c, :], in_=bank(6), func=AF.Copy, scale=SCALE),
                 reads=[pb[6]], writes=[qB])
            mm_group(bank(7), [(wqk[:, kc, 256 + c * 128:256 + (c + 1) * 128], hT[:, kc, :]) for kc in range(KC)],
                     [pb[7]], hr + [wqkB])
            P.op(ACT, lambda: A.activation(out=KT[:, c, :], in_=bank(7), func=AF.Copy),
                 reads=[pb[7]], writes=[kB])
        if stage < 3.12:
            return QT, qB
        P.dma(SPQ, ktS[2 * h:2 * h + 2, :, j * TT:(j + 1) * TT].rearrange("c p n -> p c n"), KT, kD,
              reads=[kB], writes=[ktSB])
        if stage < 3.13:
            return QT, qB
        P.op(DVE, lambda: V.memset(VH[:, :, 256:VW], 1.0), writes=[vB])
        for half in range(2):
            so_ap, soB, soD = get_slot()
            for bb in range(2):
                blk = half * 2 + bb
                mm_group(bank(half)[:, bb * 256:(bb + 1) * 256],
                         [(hT[:, kc, blk * 128:(blk + 1) * 128], wqk[:, kc, 256:512]) for kc in range(KC)],
                         [pb[half]], hr + [wqkB])
            P.op(ACT, lambda: A.activation(out=so_ap[:, 0:512], in_=bank(half), func=AF.Copy),
                 reads=[pb[half]], writes=[soB])
            r0 = j * TT + half * 256
            P.dma(POOL, k_o[r0:r0 + 256, h * 256:(h + 1) * 256].rearrange("(b p) e -> p b e", p=128),
                  so_ap[:, 0:512].rearrange("p (b e) -> p b e", b=2), poolD[id(soB)], reads=[soB])
        if stage < 3.14:
            return QT, qB
        for half in range(2):
            so_ap, soB, soD = get_slot()
            for bb in range(2):
                blk = half * 2 + bb
                mm_group(bank(2 + half)[:, bb * 256:(bb + 1) * 256],
                         [(hT[:, kc, blk * 128:(blk + 1) * 128], wv[:, kc, :]) for kc in range(KC)],
                         [pb[2 + half]], hr + [wvB])
            P.op(ACT, lambda: A.activation(out=so_ap[:, 0:512], in_=bank(2 + half), func=AF.Copy),
                 reads=[pb[2 + half]], writes=[soB])
            P.op(DVE, lambda: V.tensor_copy(out=VH[:, 2 * half:2 * half + 2, 0:256],
                                            in_=so_ap[:, 0:512].rearrange("p (b e) -> p b e", b=2)),
                 reads=[soB], writes=[vB])
            r0 = j * TT + half * 256
            P.dma(POOL, v_o[r0:r0 + 256, h * 256:(h + 1) * 256].rearrange("(b p) e -> p b e", p=128),
                  so_ap[:, 0:512].rearrange("p (b e) -> p b e", b=2), poolD[id(soB)], reads=[soB])
        if stage < 3.15:
            return QT, qB
        P.dma(SPQ, vS[h, j * TT:(j + 1) * TT, :].rearrange("(b p) e -> p b e", p=128), VH, vD,
              reads=[vB], writes=[vSB])
        return QT, qB

    def attn_core(h, QT, qB, nkb, npast, par, nq, qsubs, diag_fn, slope, on_done, past_bias=None, kn_last=128):
        KTv = KTb[par]
        for c in range(2):
            pend = []
            LOOK = 2

            def qk(kb):
                wb_i = 4 + (kb % 3)
                kn = kn_last if kb == nkb - 1 else 128
                P.op(PE, lambda: TE.matmul(bank(wb_i)[0:kn, 0:nq], lhsT=KTv[:, c, kb * 128:kb * 128 + kn],
                                           rhs=QT[:, c, 0:nq], start=True, stop=True),
                     reads=[ktB[par], qB], writes=[pb[wb_i]])
                t_ap, tB, _ = get_slot()
                if kb < npast:
                    bias_ap = past_bias if past_bias is not None else biasH[:, 512:512 + nq]
                    ccol = cH[:, (npast - kb - 1):(npast - kb)]
                else:
                    bias_ap, ccol = diag_fn(kb - npast)
                P.op(DVE, lambda: V.tensor_tensor(out=t_ap[0:kn, 0:nq], in0=bank(wb_i)[0:kn, 0:nq], in1=bias_ap[0:kn, :],
                                                  op=ALU.add),
                     reads=[pb[wb_i], biasB], writes=[tB])
                p_ap, pB_, _ = get_slot()
                PT = p_ap.bitcast(BF16)
                P.op(ACT, lambda: A.activation(out=PT[0:kn, 0:nq], in_=t_ap[0:kn, 0:nq], func=AF.Exp, bias=ccol[0:kn, :]),
                     reads=[tB, biasB], writes=[pB_])
                return PT, pB_

            def pv(kb, PT, pB_):
                kn = kn_last if kb == nkb - 1 else 128
                for qi, (q0, qn) in enumerate(qsubs):
                    P.op(PE, lambda: TE.matmul(bank(qi)[0:qn, 0:257], lhsT=PT[0:kn, q0:q0 + qn], rhs=Vb[0:kn, kb, 0:257],
                                               start=(kb == 0), stop=(kb == nkb - 1)),
                         reads=[pB_, vbB], writes=[pb[qi]], sig=(qi == len(qsubs) - 1))

            for kb in range(nkb + LOOK):
                if kb < nkb:
                    pend.append(qk(kb))
                if kb >= LOOK:
                    PT, pB_ = pend[kb - LOOK]
                    pv(kb - LOOK, PT, pB_)
            nqs = len(qsubs)
            for qi, (q0, qn) in enumerate(qsubs):
                P.op(DVE, lambda: V.reciprocal(out=small[0:qn, 24 + qi:25 + qi], in_=bank(qi)[0:qn, 256:257]),
                     reads=[pb[qi]], writes=[smallB])
                if c == 0:
                    P.op(DVE, lambda: V.tensor_scalar(out=o0buf[0:qn, qi, :], in0=bank(qi)[0:qn, 0:256],
                                                      scalar1=small[0:qn, 24 + qi:25 + qi], scalar2=None, op0=ALU.mult),
                         reads=[pb[qi], smallB], writes=[o0B])
                else:
                    P.op(DVE, lambda: V.tensor_scalar(out=small[0:qn, 28 + qi:29 + qi], in0=small[0:qn, 24 + qi:25 + qi],
                                                      scalar1=lamt[0:qn, 5:6], scalar2=None, op0=ALU.mult),
                         reads=[smallB, constB], writes=[smallB])
                    P.op(DVE, lambda: V.scalar_tensor_tensor(out=o0buf[0:qn, qi, :], in0=bank(qi)[0:qn, 0:256],
                                                             scalar=small[0:qn, 28 + qi:29 + qi],
                                                             in1=o0buf[0:qn, qi, :], op0=ALU.mult, op1=ALU.add),
                         reads=[pb[qi], smallB], writes=[o0B])
        on_done()

    def attn_finish(h, qsubs, ncol):
        j_ap, jB, _ = get_slot()
        junk = j_ap.bitcast(BF16)
        on_ap, onB, _ = get_slot()
        ON = on_ap.bitcast(BF16)[:, 0:1024].rearrange("p (b e) -> p b e", b=4)
        for qi, (q0, qn) in enumerate(qsubs):
            P.op(ACT, lambda: A.activation(out=junk[0:qn, 0:256], in_=o0buf[0:qn, qi, :], func=AF.Square,
                                           accum_out=small[0:qn, 32 + qi:33 + qi]),
                 reads=[o0B], writes=[jB, smallB])
        nqs = len(qsubs)
        qn = qsubs[0][1]
        P.op(ACT, lambda: A.activation(out=small[0:qn, 36:36 + nqs], in_=small[0:qn, 32:32 + nqs], func=AF.Ln,
                                       scale=1.0 / 256, bias=epsc[0:qn, :]), reads=[smallB, constB], writes=[smallB])
        P.op(ACT, lambda: A.activation(out=small[0:qn, 40:40 + nqs], in_=small[0:qn, 36:36 + nqs], func=AF.Exp,
                                       scale=-0.5), reads=[smallB], writes=[smallB])
        for qi, (q0, qn) in enumerate(qsubs):
            P.op(DVE, lambda: V.scalar_tensor_tensor(out=ON[0:qn, qi, :], in0=o0buf[0:qn, qi, :],
                                                     scalar=small[0:qn, 40 + qi:41 + qi], in1=sublnb[0:qn, :],
                                                     op0=ALU.mult, op1=ALU.mult),
                 reads=[o0B, smallB, constB], writes=[onB])
        pO = bank(7).bitcast(BF16).rearrange("p (e n) -> p e n", e=2)
        for eh in range(2):
            for qi, (q0, qn) in enumerate(qsubs):
                last = (eh == 1 and qi == nqs - 1)
                P.op(PE, lambda: TE.transpose(out=pO[:, eh, qi * ncol:qi * ncol + qn],
                                              in_=ON[0:qn, qi, eh * 128:(eh + 1) * 128], identity=ident[0:qn, 0:qn]),
                     reads=[onB, constB], writes=[pb[7]], sig=last)
        ntok = nqs * ncol
        P.op(ACT, lambda: A.activation(out=oT[:, 2 * h:2 * h + 2, 0:ntok], in_=pO[:, :, 0:ntok], func=AF.Copy),
             reads=[pb[7]], writes=[oTB])

    def attention_layer(j):
        nkb = 4 * (j + 1)
        nk = nkb * 128
        qsubs = [(i * 128, 128) for i in range(4)]
        wq = [None] * NH
        wq[0] = (ws_load(wqk_b[0], 512, KC, wB["qkv"]), ws_load(wv_b[0], 256, KC, wB["qkv"]))
        for h in range(NH if stage >= 3.5 else 1):
            (wqk, wqkB), (wv, wvB) = wq[h]
            QT, qB = attn_proj(j, h, wqk, wqkB, wv, wvB)
            if stage < 3.2:
                continue
            par = h % 2
            first_arena = (h == 0)
            extra = uTB if first_arena else []
            P.dma(SPQ, KTb[par][:, :, 0:nk], ktS[2 * h:2 * h + 2, :, 0:nk].rearrange("c p n -> p c n"), ktD[par],
                  reads=[ktSB], writes=[ktB[par]] + (extra if h < 2 else []))
            P.dma(SPQ, Vb[:, 0:nkb, :], vS[h, 0:nk, :].rearrange("(b p) e -> p b e", p=128), vbD,
                  reads=[vSB], writes=[vbB] + extra)
            if h + 1 < NH:
                wq[h + 1] = (ws_load(wqk_b[h + 1], 512, KC, wB["qkv"]), ws_load(wv_b[h + 1], 256, KC, wB["qkv"]))
            if stage < 3.3:
                continue
            slope = 2.0 ** (-(h + 1))
            P.op(DVE, lambda: V.tensor_scalar(out=biasH[:], in0=tabs[:], scalar1=-slope, scalar2=None, op0=ALU.mult),
                 reads=[constB], writes=[biasB])
            P.op(DVE, lambda: V.tensor_scalar(out=cH[:], in0=cpos[:], scalar1=-slope, scalar2=None, op0=ALU.mult),
                 reads=[constB], writes=[biasB])

            def diag_fn(m):
                return biasH[:, 384 - 128 * m:384 - 128 * m + 512], cH[:, 0:1]

            attn_core(h, QT, qB, nkb, nkb - 4, par, TT, qsubs, diag_fn, slope, lambda: None)
            if stage < 3.4:
                continue
            attn_finish(h, qsubs, 128)

    def out_proj(w_scr, wname):
        nxt = ws_load(w_scr[0], 512, KC, wB[wname])
        for cp in range(4):
            wv_, wvB_ = nxt
            if cp + 1 < 4:
                nxt = ws_load(w_scr[cp + 1], 512, KC, wB[wname])
            CB = cfg["CB"]
            for blk in range(cfg["nblk"]):
                mm_group(bank(blk)[0:CB, :], [(oT[:, kc, blk * CB:(blk + 1) * CB], wv_[:, kc, :]) for kc in range(KC)],
                         [pb[blk]], [oTB, wvB_])
                xs_ = xres[0:CB, blk, cp * 512:(cp + 1) * 512]
                P.op(DVE, lambda: V.tensor_tensor(out=xs_, in0=bank(blk)[0:CB, :], in1=xs_, op=ALU.add),
                     reads=[pb[blk]], writes=[xB[blk]])

    def ffn(layer):
        TW, CB = cfg["TW"], cfg["CB"]
        wn_up = "up%d" % layer
        wn_dn = "dn%d" % layer
        cpv = convp[:, layer * FC * 4:(layer + 1) * FC * 4].rearrange("p (f k) -> p f k", k=4)
        seq = [wup_b[layer, i] for i in range(44)]
        loaded = [ws_load(seq[0], 256, KC, wB[wn_up]), ws_load(seq[1], 256, KC, wB[wn_up])]
        for g in range(22):
            (wg, wgB), (wvv, wvvB) = loaded[2 * g], loaded[2 * g + 1]
            if g + 1 < 22:
                loaded.append(ws_load(seq[2 * g + 2], 256, KC, wB[wn_up]))
                loaded.append(ws_load(seq[2 * g + 3], 256, KC, wB[wn_up]))
            for ch in range(2):
                fc = g * 2 + ch
                mm_group(bank(4 + (fc % 2))[:, 0:TW], [(wg[:, kc, ch * 128:(ch + 1) * 128], hT[:, kc, 0:TW]) for kc in range(KC)],
                         [pb[4 + (fc % 2)]], [hTB, wgB])
                g_ap, gB, _ = get_slot()
                t_ap, tB, _ = get_slot()
                first_w = [gB] + (([ktB[0], ktB[1], vbB]) if (g == 0 and ch == 0) else [])
                P.op(ACT, lambda: A.activation(out=g_ap[:, 2:2 + TW], in_=bank(4 + (fc % 2))[:, 0:TW], func=AF.Copy),
                     reads=[pb[4 + (fc % 2)]], writes=[gB])
                P.op(ACT, lambda: A.activation(out=g_ap[:, 0:2], in_=convst[:, layer, fc, :], func=AF.Copy),
                     reads=[cvB], writes=[gB])
                P.op(ACT, lambda: A.activation(out=convst[:, layer, fc, :], in_=g_ap[:, TW:TW + 2], func=AF.Copy),
                     reads=[gB], writes=[cvB])
                P.op(DVE, lambda: V.tensor_scalar(out=t_ap[:, 0:TW], in0=g_ap[:, 0:TW], scalar1=cpv[:, fc, 0:1],
                                                  scalar2=cpv[:, fc, 3:4], op0=ALU.mult, op1=ALU.add),
                     reads=[gB, constB], writes=[tB])
                P.op(DVE, lambda: V.scalar_tensor_tensor(out=t_ap[:, 0:TW], in0=g_ap[:, 1:1 + TW], scalar=cpv[:, fc, 1:2],
                                                         in1=t_ap[:, 0:TW], op0=ALU.mult, op1=ALU.add),
                     reads=[gB, constB], writes=[tB])
                P.op(DVE, lambda: V.scalar_tensor_tensor(out=t_ap[:, 0:TW], in0=g_ap[:, 2:2 + TW], scalar=cpv[:, fc, 2:3],
                                                         in1=t_ap[:, 0:TW], op0=ALU.mult, op1=ALU.add),
                     reads=[gB, constB], writes=[tB])
                P.op(ACT, lambda: A.activation(out=t_ap[:, 0:TW], in_=t_ap[:, 0:TW], func=AF.Silu),
                     reads=[tB], writes=[tB])
                mm_group(bank(6 + (fc % 2))[:, 0:TW], [(wvv[:, kc, ch * 128:(ch + 1) * 128], hT[:, kc, 0:TW]) for kc in range(KC)],
                         [pb[6 + (fc % 2)]], [hTB, wvvB])
                uw = [uTB[fc // 4]] + (([ktB[0], ktB[1], vbB]) if (g == 0 and ch == 0) else [])
                P.op(DVE, lambda: V.tensor_tensor(out=uT[:, fc, 0:TW], in0=t_ap[:, 0:TW], in1=bank(6 + (fc % 2))[:, 0:TW],
                                                  op=ALU.mult),
                     reads=[tB, pb[6 + (fc % 2)]], writes=uw)
        dseq = [(cp, rg) for cp in range(4) for rg in range(4)]
        nxt = ws_load(wdn_b[layer, 0, 0], 512, 11, wB[wn_dn])
        for i, (cp, rg) in enumerate(dseq):
            wd, wdB = nxt
            if i + 1 < len(dseq):
                cp2, rg2 = dseq[i + 1]
                nxt = ws_load(wdn_b[layer, cp2, rg2], 512, 11, wB[wn_dn])
            for blk in range(cfg["nblk"]):
                for ch in range(11):
                    fc = rg * 11 + ch
                    first = (rg == 0 and ch == 0)
                    last = (rg == 3 and ch == 10)
                    P.op(PE, lambda: TE.matmul(bank(blk)[0:CB, :], lhsT=uT[:, fc, blk * CB:(blk + 1) * CB], rhs=wd[:, ch, :],
                                               start=first, stop=last),
                         reads=[uTB[fc // 4], wdB], writes=[pb[blk]], sig=(ch == 10))
            if rg == 3:
                for blk in range(cfg["nblk"]):
                    xs_ = xres[0:CB, blk, cp * 512:(cp + 1) * 512]
                    P.op(DVE, lambda: V.tensor_tensor(out=xs_, in0=bank(blk)[0:CB, :], in1=xs_, op=ALU.add),
                         reads=[pb[blk]], writes=[xB[blk]])

    cfg = {"TW": TT, "nblk": NBLK, "CB": 128, "mid": 63, "HN": 8, "HC": 64, "hmid": 31}
    arenaF = arena[:, 0:ARENA].bitcast(F32)
    HSET = 5632
    hb_ = []
    for S_ in range(2):
        o = S_ * HSET
        d_ = {}
        d_["A"] = arenaF[:, o:o + 1024]
        d_["B"] = arenaF[:, o + 1024:o + 2048]
        d_["kk"] = arenaF[:, o + 2048:o + 2560]
        d_["qs"] = arenaF[:, o + 2560:o + 3072]
        d_["eb"] = arenaF[:, o + 3072:o + 3584]
        d_["sg"] = arenaF[:, o + 3584:o + 4608].rearrange("p (b e) -> p b e", e=128)
        d_["qtT"] = arenaF[:, o + 4608:o + 4864].bitcast(BF16)
        d_["ktT"] = arenaF[:, o + 4864:o + 5120].bitcast(BF16)
        d_["ibf"] = arenaF[:, o + 5120:o + 5632].bitcast(BF16).rearrange("p (b e) -> p b e", e=128)
        for n_ in list(d_.keys()):
            d_[n_ + "B"] = Buf("hg_%s%d" % (n_, S_))
        hb_.append(d_)
    hsmB = [Buf("hsm0"), Buf("hsm1")]
    sgl = [[Buf("%s%d" % (n_, S_)) for n_ in ("sbf", "ATt", "on32", "onb", "ktok", "hjunk", "hs2")] for S_ in range(2)]
    hg_all = [v_ for S_ in range(2) for v_ in hb_[S_].values() if isinstance(v_, Buf)]
    fnb = arenaF[:, 0:D]
    fnbB = Buf("fnb")
    fnbD = P.dsem("fnbd")
    arena_bufs = hg_all + uTB + ktB + [vbB, fnbB]

    def arena_fence():
        P.op(DVE, lambda: V.memset(fence_s[:], 0.0), writes=arena_bufs)

    def hgrn_head(h, qz_, ig_):
        (wqf, wqfB), (wig, wigB) = qz_, ig_
        TW, nblk, CB, mid = cfg["TW"], cfg["HN"], cfg["HC"], cfg["hmid"]
        S_ = h % 2
        b_ = hb_[S_]
        hs = hsm[:, S_, :]
        hsB = hsmB[S_]
        bA, bI, bC, bD = 4 * S_, 4 * S_ + 1, 4 * S_ + 2, 4 * S_ + 3
        sbf, ATt, on32, onb, ktok, hjunk = sbf2[:, S_, :], ATt2[:, S_, :], on322[:, S_, :], onb2[:, S_, :], ktok2[:, S_, :], hjunk2[:, S_, :]
        hs2 = hs22[:, S_, :]
        sbfB, ATB, on32B, onbB, ktokB, hjB, hs2B = sgl[S_]
        SstB = SstHB[h]
        NA = nblk * 2 * CB
        Av = b_["A"][:, 0:NA].rearrange("p (b c) -> p b c", c=2 * CB)
        Bv = b_["B"][:, 0:NA].rearrange("p (b c) -> p b c", c=2 * CB)
        AB, BB = b_["AB"], b_["BB"]

        def v3(ap):
            return ap[:, 0:TW].rearrange("p (b c) -> p b c", c=CB)
        kk, qs, eb, sg, qtT, ktT, ibf = b_["kk"], b_["qs"], b_["eb"], b_["sg"], b_["qtT"], b_["ktT"], b_["ibf"]
        P.op(DVE, lambda: V.memset(Av[:, :, 0:CB], 0.0), writes=[AB])
        yield
        P.op(DVE, lambda: V.memset(Bv[:, :, 0:CB], 0.0), writes=[BB])
        yield
        mm_group(bank(bA)[:, 0:TW], [(wqf[:, kc, 0:128], hT[:, kc, 0:TW]) for kc in range(KC)], [pb[bA]], [hTB, wqfB])
        yield
        P.op(ACT, lambda: A.activation(out=qs[:, 0:TW], in_=bank(bA)[:, 0:TW], func=AF.Silu),
             reads=[pb[bA]], writes=[b_["qsB"]])
        yield
        for blk in range(nblk):
            bk = (bI, bC, bD)[blk % 3]
            mm_group(bank(bk)[0:CB, 0:256],
                     [(hT[:, kc, blk * CB:(blk + 1) * CB], wig[:, kc, 0:256]) for kc in range(KC)],
                     [pb[bk]], [hTB, wigB])
            yield
            P.op(DVE, lambda: V.tensor_copy(out=ibf[0:CB, blk, :], in_=bank(bk)[0:CB, 0:128]),
                 reads=[pb[bk]], writes=[b_["ibfB"]])
            yield
            P.op(ACT, lambda: A.activation(out=sg[0:CB, blk, :], in_=bank(bk)[0:CB, 128:256], func=AF.Silu),
                 reads=[pb[bk]], writes=[b_["sgB"]])
            yield
        mm_group(bank(bA)[:, 0:TW], [(wqf[:, kc, 128:256], hT[:, kc, 0:TW]) for kc in range(KC)], [pb[bA]], [hTB, wqfB])
        yield "prep"
        P.op(ACT, lambda: A.activation(out=eb[:, 0:TW], in_=bank(bA)[:, 0:TW], func=AF.Sigmoid),
             reads=[pb[bA]], writes=[b_["ebB"]])
        yield
        P.op(DVE, lambda: V.tensor_scalar(out=Av[:, :, CB:2 * CB], in0=v3(eb), scalar1=lbt[:, 48 + h:49 + h],
                                          scalar2=lbt[:, 32 + h:33 + h], op0=ALU.mult, op1=ALU.add),
             reads=[b_["ebB"], constB], writes=[AB])
        yield
        P.op(DVE, lambda: V.tensor_scalar(out=v3(kk), in0=Av[:, :, CB:2 * CB], scalar1=-1.0, scalar2=1.0,
                                          op0=ALU.mult, op1=ALU.add), reads=[AB], writes=[b_["kkB"]])
        yield
        P.op(ACT, lambda: A.activation(out=Av[:, :, CB:2 * CB], in_=Av[:, :, CB:2 * CB], func=AF.Ln),
             reads=[AB], writes=[AB])
        yield
        src, srcB, dst, dstB = Av, AB, Bv, BB
        k = 1
        while k < CB:
            P.op(DVE, lambda: V.tensor_tensor(out=dst[:, :, CB:2 * CB], in0=src[:, :, CB:2 * CB],
                                              in1=src[:, :, CB - k:2 * CB - k], op=ALU.add),
                 reads=[srcB], writes=[dstB])
            yield
            src, srcB, dst, dstB = dst, dstB, src, srcB
            k *= 2
        bv, bB = src, srcB
        P.op(DVE, lambda: V.tensor_scalar(out=hs[:, 0:nblk].unsqueeze(2), in0=bv[:, :, CB + mid:CB + mid + 1],
                                          scalar1=-1.0, scalar2=None, op0=ALU.mult), reads=[bB], writes=[hsB])
        yield
        P.op(ACT, lambda: A.activation(out=hs[:, 8:8 + nblk], in_=hs[:, 0:nblk], func=AF.Exp, scale=-1.0),
             reads=[hsB], writes=[hsB])
        yield
        P.op(ACT, lambda: A.activation(out=hs[:, 16:16 + nblk].unsqueeze(2), in_=bv[:, :, 2 * CB - 1:2 * CB], func=AF.Exp),
             reads=[bB], writes=[hsB])
        yield
        for blk in range(nblk):
            P.op(ACT, lambda: A.activation(out=eb[:, blk * CB:(blk + 1) * CB], in_=bv[:, blk, CB:2 * CB], func=AF.Exp,
                                           bias=hs[:, blk:blk + 1]), reads=[bB, hsB], writes=[b_["ebB"]])
            yield
        P.op(DVE, lambda: V.tensor_copy(out=hs[:, 24:24 + nblk].unsqueeze(2), in_=v3(eb)[:, :, CB - 1:CB]),
             reads=[b_["ebB"]], writes=[hsB])
        yield
        P.op(DVE, lambda: V.tensor_tensor(out=qtT[:, 0:TW], in0=qs[:, 0:TW], in1=eb[:, 0:TW], op=ALU.mult),
             reads=[b_["qsB"], b_["ebB"]], writes=[b_["qtTB"]])
        yield
        P.op(DVE, lambda: V.reciprocal(out=eb[:, 0:TW], in_=eb[:, 0:TW]), reads=[b_["ebB"]], writes=[b_["ebB"]])
        yield
        P.op(DVE, lambda: V.tensor_tensor(out=ktT[:, 0:TW], in0=kk[:, 0:TW], in1=eb[:, 0:TW], op=ALU.mult),
             reads=[b_["kkB"], b_["ebB"]], writes=[b_["ktTB"]])
        yield
        pKC = bank(bC).bitcast(BF16)
        pKD = bank(bD).bitcast(BF16)
        for blk in range(nblk):
            cs = slice(blk * CB, (blk + 1) * CB)
            P.op(PE, lambda: TE.matmul(bank(bC)[0:CB, 0:CB], lhsT=ktT[:, cs], rhs=qtT[:, cs], start=True, stop=True),
                 reads=[b_["ktTB"], b_["qtTB"]], writes=[pb[bC]])
            yield
            P.op(DVE, lambda: V.tensor_tensor(out=ATt[0:CB, 0:CB], in0=bank(bC)[0:CB, 0:CB], in1=triu[0:CB, 0:CB],
                                              op=ALU.mult), reads=[pb[bC], constB], writes=[ATB])
            yield
            P.op(DVE, lambda: V.tensor_scalar(out=sbf[:, :], in0=Sst[:, h, :], scalar1=hs[:, 8 + blk:9 + blk],
                                              scalar2=None, op0=ALU.mult), reads=[SstB, hsB], writes=[sbfB])
            yield
            P.op(PE, lambda: TE.matmul(bank(bD)[0:CB, 0:128], lhsT=ATt[0:CB, 0:CB], rhs=ibf[0:CB, blk, :],
                                       start=True, stop=False), reads=[ATB, b_["ibfB"]], writes=[pb[bD]], sig=False)
            yield
            P.op(PE, lambda: TE.matmul(bank(bD)[0:CB, 0:128], lhsT=qtT[:, cs], rhs=sbf[:, :], start=False, stop=True),
                 reads=[b_["qtTB"], sbfB], writes=[pb[bD]])
            yield
            P.op(PE, lambda: TE.transpose(out=pKC[0:CB, 256:384], in_=ktT[:, cs], identity=ident[:, :]),
                 reads=[b_["ktTB"], constB], writes=[pb[bC]])
            yield
            P.op(ACT, lambda: A.activation(out=ktok[0:CB, :], in_=pKC[0:CB, 256:384], func=AF.Copy),
                 reads=[pb[bC]], writes=[ktokB])
            yield
            P.op(PE, lambda: TE.matmul(bank(bC)[:, 256:384], lhsT=ktok[0:CB, :], rhs=ibf[0:CB, blk, :], start=True, stop=True),
                 reads=[ktokB, b_["ibfB"]], writes=[pb[bC]])
            yield
            P.op(DVE, lambda: V.tensor_scalar(out=Sst[:, h, :], in0=Sst[:, h, :], scalar1=hs[:, 16 + blk:17 + blk],
                                              scalar2=None, op0=ALU.mult), reads=[hsB], writes=[SstB])
            yield
            P.op(DVE, lambda: V.scalar_tensor_tensor(out=Sst[:, h, :], in0=bank(bC)[:, 256:384],
                                                     scalar=hs[:, 24 + blk:25 + blk], in1=Sst[:, h, :],
                                                     op0=ALU.mult, op1=ALU.add), reads=[pb[bC], hsB], writes=[SstB])
            yield
            P.op(ACT, lambda: A.activation(out=hjunk[0:CB, :], in_=bank(bD)[0:CB, 0:128], func=AF.Square,
                                           accum_out=hs2[0:CB, 0:1]), reads=[pb[bD]], writes=[hjB, hs2B])
            yield
            P.op(ACT, lambda: A.activation(out=hs2[0:CB, 1:2], in_=hs2[0:CB, 0:1], func=AF.Ln, scale=1.0 / 128,
                                           bias=epsc[0:CB, :]), reads=[hs2B, constB], writes=[hs2B])
            yield
            P.op(ACT, lambda: A.activation(out=hs2[0:CB, 2:3], in_=hs2[0:CB, 1:2], func=AF.Exp, scale=-0.5),
                 reads=[hs2B], writes=[hs2B])
            yield
            P.op(DVE, lambda: V.scalar_tensor_tensor(out=on32[0:CB, :], in0=bank(bD)[0:CB, 0:128], scalar=hs2[0:CB, 2:3],
                                                     in1=onormb[0:CB, :], op0=ALU.mult, op1=ALU.mult),
                 reads=[pb[bD], hs2B, constB], writes=[on32B])
            yield
            P.op(DVE, lambda: V.tensor_tensor(out=onb[0:CB, :], in0=on32[0:CB, :], in1=sg[0:CB, blk, :], op=ALU.mult),
                 reads=[on32B, b_["sgB"]], writes=[onbB])
            yield
            P.op(PE, lambda: TE.transpose(out=pKD[:, 512:512 + CB], in_=onb[0:CB, :], identity=ident[0:CB, 0:CB]),
                 reads=[onbB, constB], writes=[pb[bD]])
            yield
            P.op(ACT, lambda: A.activation(out=oT[:, h, cs], in_=pKD[:, 512:512 + CB], func=AF.Copy),
                 reads=[pb[bD]], writes=[oTB])
            yield

    def hgrn_layer():
        def load_qz(h):
            return ws_load(wqf_b[h, 0], 256, KC, wB["qf"])

        def load_ig(h):
            return ws_load(wqf_b[h, 1], 256, KC, wB["qf"])

        qz = {0: load_qz(0), 1: load_qz(1)}
        ig = {0: load_ig(0), 1: load_ig(1)}
        for h0 in range(0, RH, 2):
            alive = [hgrn_head(h0, qz[h0], ig[h0]), hgrn_head(h0 + 1, qz[h0 + 1], ig[h0 + 1])]
            prep_done, issued = 0, False
            while alive:
                for g_ in list(alive):
                    try:
                        if next(g_) == "prep":
                            prep_done += 1
                    except StopIteration:
                        alive.remove(g_)
                if prep_done == 2 and not issued and h0 + 2 < RH:
                    qz[h0 + 2] = load_qz(h0 + 2)
                    qz[h0 + 3] = load_qz(h0 + 3)
                    ig[h0 + 2] = load_ig(h0 + 2)
                    ig[h0 + 3] = load_ig(h0 + 3)
                    issued = True

    def final_store(out_ap, row0):
        TW, nblk, CB = cfg["TW"], cfg["nblk"], cfg["CB"]
        P.dma(SPQ, fnb, fnorm_d.partition_broadcast(128), fnbD, writes=[fnbB])
        for blk in range(nblk):
            xs_ = xres[0:CB, blk, :]
            P.op(ACT, lambda: A.activation(out=hb[0:CB, :], in_=xs_, func=AF.Square,
                                           accum_out=small[0:CB, blk:blk + 1]),
                 reads=[xB[blk]], writes=[hbB, smallB])
            P.op(ACT, lambda: A.activation(out=small[0:CB, 8 + blk:9 + blk], in_=small[0:CB, blk:blk + 1],
                                           func=AF.Ln, scale=1.0 / D, bias=epsc[0:CB, :]),
                 reads=[smallB, constB], writes=[smallB])
            P.op(ACT, lambda: A.activation(out=small[0:CB, 16 + blk:17 + blk], in_=small[0:CB, 8 + blk:9 + blk],
                                           func=AF.Exp, scale=-0.5), reads=[smallB], writes=[smallB])
            P.op(DVE, lambda: V.scalar_tensor_tensor(out=xs_, in0=xs_, scalar=small[0:CB, 16 + blk:17 + blk],
                                                     in1=fnb[0:CB, :], op0=ALU.mult, op1=ALU.mult),
                 reads=[smallB, fnbB], writes=[xB[blk]])
            r0 = row0 + blk * 128
            P.dma(POOL, out_ap[r0:r0 + CB, :], xs_, outD, reads=[xB[blk]])
        arena_fence()

    def store_x_dbg(j):
        for blk in range(NBLK):
            r0 = j * TT + blk * 128
            P.dma(POOL, y_o[r0:r0 + 128, :], xres[:, blk, :], outD, reads=[xB[blk]])

    for j in range(NT):
        if stage < 2:
            break
        load_x(j)
        norm_to_hT(0)
        if stage >= 3:
            attention_layer(j)
        if stage >= 4:
            out_proj(wo_b, "o")
            norm_to_hT(1)
        if stage >= 5:
            ffn(0)
        if stage >= 6 and do_l1:
            arena_fence()
            norm_to_hT(2)
            hgrn_layer()
        if stage >= 7 and do_l1:
            out_proj(wo2_b, "o2")
            norm_to_hT(3)
        if stage >= 8 and do_l1:
            arena_fence()
            ffn(1)
            arena_fence()
        if stage >= 9 and do_l1:
            final_store(y_o, j * TT)
        else:
            store_x_dbg(j)

    if do_l1:
        P.dma(POOL, shg_o.rearrange("h d v -> d h v"), Sst[:], SstD, reads=SstHB)
    P.dma(POOL, scv_o.rearrange("l p (f k) -> p l f k", k=2), convst[:], cvD, reads=[cvB])

    ktSsB = Buf("ktSs")
    ckD = P.dsem("ckd")

    def kt_prepass(q):
        for kb in range(PAST // 128):
            P.dma(POOL, hb[:, :], ck_d[q, kb * 128:(kb + 1) * 128, :], ckD, writes=[hbB])
            for c in range(KC):
                P.op(PE, lambda: TE.transpose(out=pT[:, c, 0:128], in_=hb[:, c * 128:(c + 1) * 128], identity=ident[:, :]),
                     reads=[hbB, constB], writes=[pb[4], pb[5]], sig=(c == KC - 1))
            P.op(ACT, lambda: A.activation(out=hT[:, :, (kb % 4) * 128:(kb % 4 + 1) * 128], in_=pT[:, :, 0:128], func=AF.Copy),
                 reads=[pb[4], pb[5]], writes=[hTB])
            if kb % 4 == 3:
                t0 = (kb // 4) * 512
                P.dma(SPQ, ktSs[q, :, :, t0:t0 + 512].rearrange("c p n -> p c n"), hT[:, :, :], ckD,
                      reads=[hTB], writes=[ktSsB])

    def attn_proj_s(q, h, wqk, wqkB, wv, wvB, par):
        QT = qtbuf[:, h % 2, :, :]
        qB = qtB[h % 2]
        hr = [hTB]
        for c in range(2):
            mm_group(bank(6)[:, 0:ST], [(wqk[:, kc, c * 128:(c + 1) * 128], hT[:, kc, 0:ST]) for kc in range(KC)],
                     [pb[6]], hr + [wqkB])
            P.op(ACT, lambda: A.activation(out=QT[:, c, 0:ST], in_=bank(6)[:, 0:ST], func=AF.Copy, scale=SCALE),
                 reads=[pb[6]], writes=[qB])
            mm_group(bank(7)[:, 0:ST], [(wqk[:, kc, 256 + c * 128:256 + (c + 1) * 128], hT[:, kc, 0:ST]) for kc in range(KC)],
                     [pb[7]], hr + [wqkB])
            P.op(ACT, lambda: A.activation(out=KTb[par][:, c, PAST:PAST + ST], in_=bank(7)[:, 0:ST], func=AF.Copy),
                 reads=[pb[7]], writes=[ktB[par]])
        so_ap, soB, _ = get_slot()
        mm_group(bank(0)[0:ST, 0:256], [(hT[:, kc, 0:ST], wqk[:, kc, 256:512]) for kc in range(KC)], [pb[0]], hr + [wqkB])
        P.op(ACT, lambda: A.activation(out=so_ap[0:ST, 0:256], in_=bank(0)[0:ST, 0:256], func=AF.Copy),
             reads=[pb[0]], writes=[soB])
        P.dma(POOL, ks_o[q * ST:(q + 1) * ST, h * 256:(h + 1) * 256], so_ap[0:ST, 0:256], poolD[id(soB)], reads=[soB])
        so2, so2B, _ = get_slot()
        mm_group(bank(2)[0:ST, 0:256], [(hT[:, kc, 0:ST], wv[:, kc, :]) for kc in range(KC)], [pb[2]], hr + [wvB])
        P.op(ACT, lambda: A.activation(out=so2[0:ST, 0:256], in_=bank(2)[0:ST, 0:256], func=AF.Copy),
             reads=[pb[2]], writes=[so2B])
        P.dma(POOL, vs_o[q * ST:(q + 1) * ST, h * 256:(h + 1) * 256], so2[0:ST, 0:256], poolD[id(so2B)], reads=[so2B])
        P.op(DVE, lambda: V.tensor_copy(out=Vb[0:ST, 32, 0:256], in_=so2[0:ST, 0:256]), reads=[so2B], writes=[vbB])
        P.op(DVE, lambda: V.memset(Vb[:, :, 256:VW], 1.0), writes=[vbB])
        return QT, qB

    def attention_layer_s(q):
        nkb = PAST // 128 + 1
        qsubs = [(0, ST)]
        wq = [None] * NH
        wq[0] = (ws_load(wqk_b[0], 512, KC, wB["qkv"]), ws_load(wv_b[0], 256, KC, wB["qkv"]))
        for h in range(NH):
            (wqk, wqkB), (wv, wvB) = wq[h]
            par = h % 2
            P.dma(SPQ, KTb[par][:, :, 0:PAST], ktSs[q, 2 * h:2 * h + 2, :, :].rearrange("c p n -> p c n"), ktD[par],
                  reads=[ktSsB], writes=[ktB[par]])
            for i4 in range(4):
                P.dma(POOL, Vb[:, 8 * i4:8 * i4 + 8, 0:256],
                      cv_d[q, 1024 * i4:1024 * (i4 + 1), h * 256:(h + 1) * 256].rearrange("(b p) e -> p b e", p=128),
                      vbD, writes=[vbB])
            QT, qB = attn_proj_s(q, h, wqk, wqkB, wv, wvB, par)
            if h + 1 < NH:
                wq[h + 1] = (ws_load(wqk_b[h + 1], 512, KC, wB["qkv"]), ws_load(wv_b[h + 1], 256, KC, wB["qkv"]))
            slope = 2.0 ** (-(h + 1))
            P.op(DVE, lambda: V.tensor_scalar(out=sbias[:], in0=stabs[:], scalar1=-slope, scalar2=None, op0=ALU.mult),
                 reads=[constB], writes=[biasB])
            P.op(DVE, lambda: V.tensor_scalar(out=cH[:], in0=cpos[:], scalar1=-slope, scalar2=None, op0=ALU.mult),
                 reads=[constB], writes=[biasB])

            def diag_fn(m):
                return sbias[:, ST:2 * ST], cH[:, 0:1]

            attn_core(h, QT, qB, nkb, nkb - 1, par, ST, qsubs, diag_fn, slope, lambda: None,
                      past_bias=sbias[:, 0:ST], kn_last=ST)
            attn_finish(h, qsubs, ST)

    if do_sample:
        cfg.update({"TW": ST, "nblk": 1, "CB": ST, "HN": 1, "HC": ST, "hmid": ST // 2 - 1})
        for q in range(SB):
            arena_fence()
            kt_prepass(q)
            P.dma(SPQ, Sst[:], shg_in[q].rearrange("h d v -> d h v"), SstD, writes=SstHB)
            for l in range(2):
                P.dma(SPQ, convst[:, l, :, :].rearrange("p f k -> p (f k)"), scv_in[l, q], cvD, writes=[cvB])
            P.dma(SPQ, xres[0:ST, 0, :], xs_d[q * ST:(q + 1) * ST, :], xD[0], writes=[xB[0]])
            norm_to_hT(0, ST, 1, ST)
            attention_layer_s(q)
            out_proj(wo_b, "o")
            norm_to_hT(1, ST, 1, ST)
            ffn(0)
            arena_fence()
            norm_to_hT(2, ST, 1, ST)
            hgrn_layer()
            out_proj(wo2_b, "o2")
            norm_to_hT(3, ST, 1, ST)
            arena_fence()
            ffn(1)
            arena_fence()
            final_store(ys_o, q * ST)
            P.dma(POOL, shgs_o[q].rearrange("h d v -> d h v"), Sst[:], SstD, reads=SstHB)
            for l in range(2):
                P.dma(POOL, scvs_o[l, q], convst[:, l, :, :].rearrange("p f k -> p (f k)"), cvD, reads=[cvB])

    allq = [outD, cvD, SstD, ckD] + slotD + slotDP
    for ds in allq:
        if ds.total > 0:
            nc.gpsimd.wait_ge(ds.h, ds.total)
    es.close()
    return nc


_CACHE = {}


def _prep_shared(inp):
    sh = {}
    sh["w_qkv"] = np.ascontiguousarray(inp["attn_w_qkv"][0])
    sh["w_o"] = np.ascontiguousarray(inp["attn_w_o"][0])
    sh["w_qfig"] = np.ascontiguousarray(inp["rec_w_qfig"][0])
    sh["w_o2"] = np.ascontiguousarray(inp["rec_w_o"][0])
    sh["w_up"] = np.ascontiguousarray(inp["ffn_w_up"])
    sh["w_dn"] = np.ascontiguousarray(inp["ffn_w_down"])
    norms = np.stack([inp["mixer_norm"][0], inp["ffn_norm"][0], inp["mixer_norm"][1], inp["ffn_norm"][1]])
    sh["gcols"] = np.ascontiguousarray(norms.reshape(4, KC, 128).transpose(2, 0, 1).reshape(128, 4 * KC))
    sh["fnorm"] = np.ascontiguousarray(inp["final_norm"])
    sh["subln"] = np.ascontiguousarray(inp["attn_subln"][0])
    sh["lamv"] = np.ascontiguousarray(np.concatenate([inp["attn_lambda_q1"][0], inp["attn_lambda_k1"][0],
                                                      inp["attn_lambda_q2"][0], inp["attn_lambda_k2"][0]]))
    sh["rlb"] = np.ascontiguousarray(inp["rec_lower_bounds"].reshape(2, RH, 128).transpose(2, 0, 1).reshape(128, 2 * RH))
    sh["onorm"] = np.ascontiguousarray(inp["rec_out_norm"][0])
    cw = np.concatenate([inp["ffn_conv_w"], inp["ffn_conv_b"][:, None, :]], axis=1)
    sh["convp"] = np.ascontiguousarray(cw.reshape(2, 4, FC, 128).transpose(3, 0, 2, 1).reshape(128, 2 * FC * 4))
    sh.update({k: v for k, v in host_consts().items() if k in ("ident", "tabs", "cpos", "tri", "triu")})
    return sh


def _conv_state_from(scv):
    return np.ascontiguousarray(scv.reshape(2, 128, FC, 2).transpose(0, 3, 2, 1).reshape(2, 2, DFF))


def kernel(**inp):
    dbg = inp.pop("_dbg", None)
    inp = {k: np.asarray(v) for k, v in inp.items()}
    NT_, do_l1, do_sample, stage, ncores, core0 = 8, True, DO_SAMPLE, 9, NCORES, 0
    if dbg is not None:
        NT_, do_l1, do_sample, stage, ncores, core0 = dbg
    key = (NT_, do_l1, do_sample, stage)
    if key not in _CACHE:
        _CACHE[key] = build_program(NT=NT_, do_l1=do_l1, do_sample=do_sample, stage=stage)
    nc = _CACHE[key]
    sh = _prep_shared(inp)
    in_maps = []
    for c in range(ncores):
        m = dict(sh)
        m["xp"] = np.ascontiguousarray(inp["x_prompt"][c % 4])
        if do_sample:
            s0 = c * SB
            m["xs"] = np.ascontiguousarray(inp["x_sample"][s0:s0 + SB].reshape(SB * ST, D))
            m["ck"] = np.ascontiguousarray(inp["cache_k"][0, s0:s0 + SB].reshape(SB, PAST, D))
            m["cv"] = np.ascontiguousarray(inp["cache_v"][0, s0:s0 + SB].reshape(SB, PAST, D))
            m["shg_in"] = np.ascontiguousarray(inp["state_hgrn"][0, s0:s0 + SB])
            sc_ = inp["state_conv"][:, s0:s0 + SB].reshape(2, SB, 2, FC, 128)
            m["scv_in"] = np.ascontiguousarray(sc_.transpose(0, 1, 4, 3, 2).reshape(2, SB, 128, FC * 2))
            hc = host_consts()
            m["stab_c"] = hc["stab_c"]
            m["stab_n"] = hc["stab_n"]
        in_maps.append(m)
    res = run_bass_kernel_spmd(nc, in_maps, core_ids=list(range(core0, core0 + ncores)))
    if dbg is not None:
        return res
    R_ = res.results
    B = 4
    y_p = np.stack([R_[b]["y"] for b in range(B)]).astype(np.float32)
    k_p = np.stack([R_[b]["kout"] for b in range(B)]).reshape(1, B, T, NH, 2, 128).astype(np.float32)
    v_p = np.stack([R_[b]["vout"] for b in range(B)]).reshape(1, B, T, NH, 256).astype(np.float32)
    sh_p = np.stack([R_[b]["shg"] for b in range(B)]).reshape(1, B, RH, 128, 128).astype(np.float32)
    sc_p = np.stack([_conv_state_from(R_[b]["scv"]) for b in range(B)], axis=1).astype(np.float32)
    NS = NCORES * SB
    if do_sample:
        y_s = np.concatenate([R_[c]["ys"].reshape(SB, ST, D) for c in range(NCORES)]).astype(np.float32)
        k_s = np.concatenate([R_[c]["ksout"].reshape(SB, ST, D) for c in range(NCORES)]).reshape(1, NS, ST, NH, 2, 128)
        v_s = np.concatenate([R_[c]["vsout"].reshape(SB, ST, D) for c in range(NCORES)]).reshape(1, NS, ST, NH, 256)
        sh_s = np.concatenate([R_[c]["shgs"] for c in range(NCORES)]).reshape(1, NS, RH, 128, 128)
        sc_s = np.concatenate([np.stack([_conv_state_from(R_[c]["scvs"][:, q]) for q in range(SB)], axis=1)
                               for c in range(NCORES)], axis=1)
    else:
        y_s = np.zeros((NS, ST, D), np.float32)
        k_s = np.zeros((1, NS, ST, NH, 2, 128), np.float32)
        v_s = np.zeros((1, NS, ST, NH, 256), np.float32)
        sh_s = np.zeros((1, NS, RH, 128, 128), np.float32)
        sc_s = np.zeros((2, NS, 2, DFF), np.float32)
    return (y_p, y_s, k_p, v_p, np.ascontiguousarray(k_s, dtype=np.float32), np.ascontiguousarray(v_s, dtype=np.float32),
            sh_p, np.ascontiguousarray(sh_s, dtype=np.float32), sc_p, np.ascontiguousarray(sc_s, dtype=np.float32))
```
